# Optimizing a Trainium2 kernel written in Bass

```python
import math
import jax, jax.numpy as jnp
from jax import lax
import numpy as np

D_MODEL = 1024
BATCH = 4
SEQ = 8192
DEPTH = 2

CONV_WIDTH = 31
HEAD_DIM = 64
N_Q_HEADS = D_MODEL // HEAD_DIM
N_KV_HEADS = 4
GROUP = N_Q_HEADS // N_KV_HEADS
WINDOW = 128
BLOCK = 128
NUM_BUCKETS = 32
MAX_DISTANCE = 128
D_FF = ((8 * D_MODEL // 3 + 255) // 256) * 256
EPS = 1e-6

kernel_name = "hybrid_conv_swa_sink_t5_swiglu"


def rms_norm(x, g):
    xf = x.astype(jnp.float32)
    y = xf * lax.rsqrt(jnp.mean(xf * xf, axis=-1, keepdims=True) + EPS)
    return (y * g.astype(jnp.float32)).astype(x.dtype)


def layer_norm(x, g, b):
    xf = x.astype(jnp.float32)
    mu = jnp.mean(xf, axis=-1, keepdims=True)
    xc = xf - mu
    var = jnp.mean(xc * xc, axis=-1, keepdims=True)
    y = xc * lax.rsqrt(var + EPS) * g.astype(jnp.float32) + b.astype(jnp.float32)
    return y.astype(x.dtype)


def conformer_conv(x, w_in, b_in, dw_w, dw_b, ln_g, ln_b, w_out, b_out):
    u = x @ w_in + b_in
    val, gate = jnp.split(u, 2, axis=-1)
    u = val * jax.nn.sigmoid(gate)
    u = lax.conv_general_dilated(
        u, dw_w[:, None, :].astype(u.dtype), window_strides=(1,),
        padding=[(CONV_WIDTH - 1, 0)],
        dimension_numbers=("NWC", "WIO", "NWC"),
        feature_group_count=D_MODEL) + dw_b
    u = jax.nn.silu(layer_norm(u, ln_g, ln_b))
    return u @ w_out + b_out


def t5_causal_bucket(dist):
    dist = jnp.maximum(dist, 0)
    max_exact = NUM_BUCKETS // 2
    large = max_exact + (
        jnp.log(jnp.maximum(dist, 1).astype(jnp.float32) / max_exact)
        / math.log(MAX_DISTANCE / max_exact) * (NUM_BUCKETS - max_exact)
    ).astype(jnp.int32)
    large = jnp.minimum(large, NUM_BUCKETS - 1)
    return jnp.where(dist < max_exact, dist, large)


def swa_sink_attention(x, w_qkv, b_qkv, w_o, b_o, sinks, rel_bias):
    B, S, _ = x.shape
    nb = S // BLOCK
    qkv = x @ w_qkv + b_qkv
    q, k, v = jnp.split(qkv, [N_Q_HEADS * HEAD_DIM, (N_Q_HEADS + N_KV_HEADS) * HEAD_DIM], axis=-1)
    q = q.reshape(B, nb, BLOCK, N_KV_HEADS, GROUP, HEAD_DIM)
    k = k.reshape(B, nb, BLOCK, N_KV_HEADS, HEAD_DIM)
    v = v.reshape(B, nb, BLOCK, N_KV_HEADS, HEAD_DIM)
    k_prev = jnp.concatenate([jnp.zeros_like(k[:, :1]), k[:, :-1]], axis=1)
    v_prev = jnp.concatenate([jnp.zeros_like(v[:, :1]), v[:, :-1]], axis=1)
    kb = jnp.concatenate([k_prev, k], axis=2)
    vb = jnp.concatenate([v_prev, v], axis=2)

    scale = HEAD_DIM ** -0.5
    logits = jnp.einsum("bnqkgd,bnskd->bnkgqs", q, kb,
                        preferred_element_type=jnp.float32) * scale

    q_loc = jnp.arange(BLOCK, dtype=jnp.int32)[:, None] + BLOCK
    s_loc = jnp.arange(2 * BLOCK, dtype=jnp.int32)[None, :]
    dist = q_loc - s_loc
    band = (dist >= 0) & (dist < WINDOW)
    bias = rel_bias.astype(jnp.float32)[t5_causal_bucket(dist)]
    bias = jnp.transpose(bias, (2, 0, 1)).reshape(N_KV_HEADS, GROUP, BLOCK, 2 * BLOCK)
    blk = jnp.arange(nb, dtype=jnp.int32)[:, None, None]
    valid = band[None] & ((blk * BLOCK - BLOCK + s_loc[None]) >= 0)

    logits = jnp.where(valid[None, :, None, None], logits + bias, -jnp.inf)
    sink = sinks.astype(jnp.float32).reshape(N_KV_HEADS, GROUP)[None, None, :, :, None, None]
    m = jnp.maximum(jnp.max(logits, axis=-1, keepdims=True), sink)
    p = jnp.exp(logits - m)
    denom = jnp.sum(p, axis=-1, keepdims=True) + jnp.exp(sink - m)
    p = (p / denom).astype(vb.dtype)
    out = jnp.einsum("bnkgqs,bnskd->bnqkgd", p, vb).reshape(B, S, N_Q_HEADS * HEAD_DIM)
    return out @ w_o + b_o


def swiglu_ffn(x, w_gate_up, w_down):
    g, u = jnp.split(x @ w_gate_up, 2, axis=-1)
    return (jax.nn.silu(g) * u) @ w_down


def setup_inputs(seed: int = 0) -> dict:
    key = jax.random.key(seed)
    keys = iter(jax.random.split(key, 64))
    n_conv = (DEPTH + 1) // 2
    n_attn = DEPTH // 2
    f32 = jnp.float32

    def w(shape, fan_in):
        return jax.random.normal(next(keys), shape, f32) * fan_in ** -0.5

    def gain(shape):
        return 1.0 + 0.05 * jax.random.normal(next(keys), shape, f32)

    def small(shape, s=0.02):
        return s * jax.random.normal(next(keys), shape, f32)

    qkv_w = (N_Q_HEADS + 2 * N_KV_HEADS) * HEAD_DIM
    return {
        "x": jax.random.normal(next(keys), (BATCH, SEQ, D_MODEL), f32),
        "mix_pre_g": gain((DEPTH, D_MODEL)),
        "mix_post_g": gain((DEPTH, D_MODEL)),
        "ffn_pre_g": gain((DEPTH, D_MODEL)),
        "ffn_post_g": gain((DEPTH, D_MODEL)),
        "conv_w_in": w((n_conv, D_MODEL, 2 * D_MODEL), D_MODEL),
        "conv_b_in": small((n_conv, 2 * D_MODEL)),
        "conv_dw_w": w((n_conv, CONV_WIDTH, D_MODEL), CONV_WIDTH),
        "conv_dw_b": small((n_conv, D_MODEL)),
        "conv_ln_g": gain((n_conv, D_MODEL)),
        "conv_ln_b": small((n_conv, D_MODEL)),
        "conv_w_out": w((n_conv, D_MODEL, D_MODEL), D_MODEL),
        "conv_b_out": small((n_conv, D_MODEL)),
        "attn_w_qkv": w((n_attn, D_MODEL, qkv_w), D_MODEL),
        "attn_b_qkv": small((n_attn, qkv_w)),
        "attn_w_o": w((n_attn, N_Q_HEADS * HEAD_DIM, D_MODEL), N_Q_HEADS * HEAD_DIM),
        "attn_b_o": small((n_attn, D_MODEL)),
        "attn_sinks": 0.5 * jax.random.normal(next(keys), (n_attn, N_Q_HEADS), f32),
        "rel_bias": 0.1 * jax.random.normal(next(keys), (NUM_BUCKETS, N_Q_HEADS), f32),
        "ffn_w_gate_up": w((DEPTH, D_MODEL, 2 * D_FF), D_MODEL),
        "ffn_w_down": w((DEPTH, D_FF, D_MODEL), D_FF),
    }


def reference(x, mix_pre_g, mix_post_g, ffn_pre_g, ffn_post_g,
              conv_w_in, conv_b_in, conv_dw_w, conv_dw_b, conv_ln_g, conv_ln_b,
              conv_w_out, conv_b_out,
              attn_w_qkv, attn_b_qkv, attn_w_o, attn_b_o, attn_sinks, rel_bias,
              ffn_w_gate_up, ffn_w_down):
    h = x
    for i in range(DEPTH):
        j = i // 2
        u = rms_norm(h, mix_pre_g[i])
        if i % 2 == 0:
            u = conformer_conv(u, conv_w_in[j], conv_b_in[j], conv_dw_w[j], conv_dw_b[j],
                               conv_ln_g[j], conv_ln_b[j], conv_w_out[j], conv_b_out[j])
        else:
            u = swa_sink_attention(u, attn_w_qkv[j], attn_b_qkv[j], attn_w_o[j], attn_b_o[j],
                                   attn_sinks[j], rel_bias)
        h = h + rms_norm(u, mix_post_g[i])
        f = swiglu_ffn(rms_norm(h, ffn_pre_g[i]), ffn_w_gate_up[i], ffn_w_down[i])
        h = h + rms_norm(f, ffn_post_g[i])
    return h
```

```python
import numpy as np
from contextlib import ExitStack
import concourse.bass as bass
import concourse.mybir as mybir
from concourse.bass_utils import run_bass_kernel_spmd

F32 = mybir.dt.float32
BF16 = mybir.dt.bfloat16
AF = mybir.ActivationFunctionType
ALU = mybir.AluOpType

P = 128
D = 1024
NCH = 8
DFF = 2816
NF = 22
SEQ = 8192
BATCH = 4
NCORES = 8
TOK = 4096
HALO = 256
TLOC = TOK + HALO
TT = 1024
SW = 512
CONVW = 31
UH = 32
DT = 11
NQH = 16
EPS = 1e-6
NEG = -30000.0
NSLOT = 2
SLOT_E = 8192

GU_UNITS = [(0, 4), (4, 8), (8, 12), (12, 16), (16, 19), (19, 22)]

_CL = {}
_off = 0
for _n, _w in [("g_mix_pre0", 8), ("g_mix_post0", 8), ("g_ffn_pre0", 8), ("g_ffn_post0", 8),
               ("g_mix_pre1", 8), ("g_mix_post1", 8), ("g_ffn_pre1", 8), ("g_ffn_post1", 8),
               ("b_in_v", 8), ("b_in_g", 8), ("dw_b", 8), ("ln_g", 8), ("ln_b", 8), ("b_out", 8),
               ("dw_w", 8 * CONVW), ("b_q", 8), ("b_k", 2), ("b_o", 8), ("flag", 1), ("sinks", 8),
               ("eps", 1), ("zero", 1), ("b_v", 256)]:
    _CL[_n] = _off
    _off += _w
NCONST = _off


class Buf:
    __slots__ = ("name", "w", "r")

    def __init__(self, name):
        self.name = name
        self.w = None
        self.r = {}


class _Eng:
    def __init__(self, name):
        self.name = name
        self.cnt = 0
        self.ops = []
        self.waited = {}


class Sched:
    def __init__(self):
        self.engs = {k: _Eng(k) for k in ("pe", "act", "dve", "pool", "sp")}
        self.dcnt = {}

    def _waits(self, e, reads, writes):
        deps = {}
        for b in reads:
            if b.w is not None and deps.get(b.w[0], 0) < b.w[1]:
                deps[b.w[0]] = b.w[1]
        for b in writes:
            if b.w is not None and deps.get(b.w[0], 0) < b.w[1]:
                deps[b.w[0]] = b.w[1]
            for s, v in b.r.items():
                if deps.get(s, 0) < v:
                    deps[s] = v
        waits = []
        for s, v in deps.items():
            if e.name == "pe" and s == "pe":
                continue
            if e.waited.get(s, 0) >= v:
                continue
            if s == e.name:
                assert v <= e.cnt, (e.name, v, e.cnt)
            e.waited[s] = v
            waits.append((s, v))
        return waits

    def op(self, eng, fn, reads=(), writes=(), inc=True):
        e = self.engs[eng]
        assert inc or eng == "pe"
        waits = self._waits(e, reads, writes)
        ev = (eng, e.cnt + 1)
        e.ops.append((waits, fn, eng if inc else None, 1))
        if inc:
            e.cnt += 1
        for b in reads:
            if b.r.get(eng, 0) < ev[1]:
                b.r[eng] = ev[1]
        for b in writes:
            b.w = ev
            b.r = {}

    def dma(self, eng, fn, dsem, reads=(), writes=()):
        e = self.engs[eng]
        waits = self._waits(e, reads, writes)
        self.dcnt[dsem] = self.dcnt.get(dsem, 0) + 16
        ev = (dsem, self.dcnt[dsem])
        e.ops.append((waits, fn, dsem, 16))
        for b in reads:
            if b.r.get(dsem, 0) < ev[1]:
                b.r[dsem] = ev[1]
        for b in writes:
            b.w = ev
            b.r = {}


class Prog:
    def __init__(self):
        self.nc = bass.Bass("TRN2", target_bir_lowering=False)
        self.S = Sched()
        self.bank_rr = 0
        self.tmp_rr = 0
        self.wnext = 0
        self.wissued = 0

    @staticmethod
    def tiles():
        t = [dict(idx=0, t0=0, W=HALO, subs=[(0, HALO)], halo=True)]
        for i in range(TOK // TT):
            t.append(dict(idx=i + 1, t0=HALO + i * TT, W=TT,
                          subs=[(s * SW, SW) for s in range(TT // SW)], halo=False))
        return t

    def weight_plan(self):
        plan = []
        for tl in self.tiles():
            plan += [("in", 0), ("in", 1), ("out", 0)]
            plan += [("gu0", u) for u in range(len(GU_UNITS))]
            plan += [("dn0", u) for u in range(4)]
            if tl["halo"]:
                plan.append(("kv", 0))
                continue
            plan += [("q", 0), ("kv", 0), ("o", 0)]
            plan += [("gu1", u) for u in range(len(GU_UNITS))]
            plan += [("dn1", u) for u in range(4)]
        return plan

    def unit_E(self, kind, idx):
        if kind.startswith("gu"):
            a, b = GU_UNITS[idx]
            return (b - a) * 2048
        return {"in": 8192, "out": 8192, "dn0": 5632, "dn1": 5632, "q": 8192, "kv": 4096, "o": 8192}[kind]

    def build(self):
        nc = self.nc
        S = self.S
        es = ExitStack()
        with es:
            def dram(name, shape, dt=F32, kind="ExternalInput"):
                return nc.dram_tensor(name, shape, dt, kind=kind).ap()

            self.xT = dram("xT", [P, NCH * TLOC])
            self.outT = dram("outT", [P, NCH * TOK], kind="ExternalOutput")
            self.consts_d = dram("consts", [P, NCONST])
            self.biasT_d = dram("biasT", [P, NQH * 2 * P])
            self.maskT_d = dram("maskT", [P, 2 * P])
            self.ident_d = dram("ident", [P, P])
            self.wd = {
                "in": dram("w_in", [2 * P, 8192]), "out": dram("w_out", [P, 8192]),
                "gu0": dram("w_gu0", [6 * P, 8192]), "gu1": dram("w_gu1", [6 * P, 8192]),
                "dn0": dram("w_dn0", [4 * P, 5632]), "dn1": dram("w_dn1", [4 * P, 5632]),
                "q": dram("w_q", [P, 8192]), "kv": dram("w_kv", [P, 4096]), "o": dram("w_o", [P, 8192]),
            }

            def sb(name, shape, dt):
                return es.enter_context(nc.sbuf_tensor(name, shape, dt))

            self.hT = sb("hT", [P, NCH * TT], F32)
            self.xn = sb("xn", [P, NCH * TT], BF16)
            self.sq = sb("sq", [P, NCH * TT], BF16)
            self.a32 = sb("a32", [P, NCH * TT], F32)
            self.r44 = sb("r44", [P, NF * TT], BF16)
            self.uhalo = sb("uhalo", [P, NCH * UH], BF16)
            self.kprev = sb("kprev", [P, 2 * P], BF16)
            self.vprev = sb("vprev", [P, 256], BF16)
            self.wring = sb("wring", [P, NSLOT * SLOT_E], BF16)
            self.diag = sb("diag", [P, 2 * DT * P], BF16)
            self.ident = sb("identb", [P, P], BF16)
            self.onesm = sb("onesm", [P, P], BF16)
            self.ones1 = sb("ones1", [P, P], BF16)
            self.consts = sb("constsb", [P, NCONST], F32)
            self.exps = sb("exps", [P, 8], F32)
            self.bhi = sb("bhi", [P, NQH * 2 * P], BF16)
            self.st = sb("st", [P, 4 * SW], F32)
            self.dummy = sb("dummyt", [P, 16], F32)
            self.dummy_b = Buf("dummy")
            self.ps = [es.enter_context(nc.psum_tensor(f"ps{i}", [P, SW], F32)) for i in range(8)]

            nsub = TT // SW
            self.hT_b = [[Buf(f"hT{c}_{s}") for s in range(nsub)] for c in range(NCH)]
            self.xn_b = [[Buf(f"xn{c}_{s}") for s in range(nsub)] for c in range(NCH)]
            self.sq_b = [[Buf(f"sq{c}_{s}") for s in range(nsub)] for c in range(NCH)]
            self.a32_b = [[Buf(f"a32{c}_{s}") for s in range(nsub)] for c in range(NCH)]
            self.r44_b = Buf("r44")
            self.aT_b = [[Buf(f"aT{i}_{s}") for s in range(nsub)] for i in range(NF)]
            self.uT_b = [Buf(f"uT{c}") for c in range(NCH)]
            self.qT_b = [[Buf(f"qT{c}_{s}") for s in range(nsub)] for c in range(NCH)]
            self.at_b = [[Buf(f"at{c}_{s}") for s in range(nsub)] for c in range(NCH)]
            self.kT_b = [[Buf(f"kT{j}_{b}") for b in range(TT // P + 1)] for j in range(2)]
            self.V_b = [Buf(f"V{b}") for b in range(TT // P + 1)]
            self.uhalo_b = Buf("uhalo")
            self.kprev_b = Buf("kprev")
            self.vprev_b = Buf("vprev")
            self.w_b = [Buf(f"w{i}") for i in range(NSLOT)]
            self.diag_b = [Buf("diag0"), Buf("diag1")]
            self.const_b = Buf("const")
            self.st_b = [Buf(f"st{i}") for i in range(4)]
            self.ps_b = [Buf(f"ps{i}") for i in range(8)]
            self.plan = self.weight_plan()

            self.emit_all()
            self.finalize(es)
        return nc

    def cs(self, name, c=0, n=1):
        o = _CL[name] + c
        return self.consts[:, o:o + n]

    def h_ap(self, c, off, w):
        si, o = off // SW, off % SW
        assert o + w <= SW
        base = si * NCH * SW + c * SW + o
        return self.hT[:, base: base + w]

    def xn_ap(self, c, off, w):
        return self.xn[:, c * TT + off: c * TT + off + w]

    def sq_ap(self, c, off, w):
        return self.sq[:, c * TT + off: c * TT + off + w]

    def a_ap(self, c, off, w):
        return self.a32[:, c * TT + off: c * TT + off + w]

    def aT_ap(self, i, off, w):
        return self.r44[:, i * TT + off: i * TT + off + w]

    UTW = UH + TT

    def uT_ap(self, c, col, w):
        return self.r44[:, c * self.UTW + col: c * self.UTW + col + w]

    QO = 0
    AO = 8 * TT
    KO = 16 * TT
    KW = P + TT
    VO = 16 * TT + 2 * (P + TT)

    def qT_ap(self, c, off, w, rows=slice(0, P)):
        return self.r44[rows, self.QO + c * TT + off: self.QO + c * TT + off + w]

    def at_ap(self, c, off, w):
        return self.r44[:, self.AO + c * TT + off: self.AO + c * TT + off + w]

    def kT_ap(self, j, col, w, rows=slice(0, P)):
        return self.r44[rows, self.KO + j * self.KW + col: self.KO + j * self.KW + col + w]

    def V_ap(self, blk, c0=0, w=256):
        return self.r44[:, self.VO + blk * 256 + c0: self.VO + blk * 256 + c0 + w]

    def w_ap(self, slot, e0, w):
        return self.wring[:, slot * SLOT_E + e0: slot * SLOT_E + e0 + w]

    def tmp_unit(self):
        c = self.tmp_rr % 8
        self.tmp_rr += 1
        return self.a_ap(c, 0, SW), self.a32_b[c][0]

    def next_bank(self):
        b = self.bank_rr % 6
        self.bank_rr += 1
        return b

    def w_issue_upto(self, n):
        S = self.S
        while self.wissued <= min(n, len(self.plan) - 1):
            i = self.wissued
            kind, idx = self.plan[i]
            slot = i % NSLOT
            E = self.unit_E(kind, idx)
            src = self.wd[kind][idx * P:(idx + 1) * P, 0:E]
            dst = self.w_ap(slot, 0, E)
            S.dma("pool", (lambda e, dst=dst, src=src: e.dma_start(out=dst, in_=src)), f"d:w{slot}",
                  writes=[self.w_b[slot]])
            self.wissued += 1

    def w_get(self, kind, idx):
        i = self.wnext
        assert self.plan[i] == (kind, idx), (self.plan[i], kind, idx)
        self.w_issue_upto(i + NSLOT - 1)
        self.wnext += 1
        return i % NSLOT

    def mm(self, out, lhsT, rhs, start, stop, reads, writes, inc, tp=None):
        if tp is None:
            fn = (lambda e, out=out, lhsT=lhsT, rhs=rhs, start=start, stop=stop:
                  e.matmul(out, lhsT=lhsT, rhs=rhs, start=start, stop=stop))
        else:
            fn = (lambda e, out=out, lhsT=lhsT, rhs=rhs, start=start, stop=stop, tp=tp:
                  e.matmul(out, lhsT=lhsT, rhs=rhs, start=start, stop=stop, tile_position=tp))
        self.S.op("pe", fn, reads=reads, writes=writes, inc=inc)

    def act(self, out, in_, func, reads, writes, bias=None, scale=1.0):
        if bias is None:
            bias = self.cs("zero")
        self.S.op("act", (lambda e, out=out, in_=in_, func=func, bias=bias, scale=scale:
                          e.activation(out=out, in_=in_, func=func, bias=bias, scale=scale)),
                  reads=list(reads) + [self.const_b], writes=writes)

    def dve(self, fn, reads, writes):
        self.S.op("dve", fn, reads=reads, writes=writes)

    def gemm(self, slot, e0, nk, rhs_fn, subs, evac):
        banks = [self.next_bank() for _ in subs]
        for k in range(nk):
            for si, (off, w) in enumerate(subs):
                rap, rb = rhs_fn(k, si, off, w)
                lhs = self.w_ap(slot, e0 + k * P, P)
                wr = [self.ps_b[b] for b in banks] if (k == 0 and si == 0) else [self.ps_b[banks[si]]]
                self.mm(self.ps[banks[si]][:, 0:w], lhs, rap, k == 0, k == nk - 1,
                        reads=[self.w_b[slot]] + rb, writes=wr, inc=(k == nk - 1 and si == len(subs) - 1))
        for si, (off, w) in enumerate(subs):
            evac(si, off, w, self.ps[banks[si]][:, 0:w], self.ps_b[banks[si]])

    def stats_mm(self, si, off, w):
        bk = 6 + (si % 2)
        for c in range(NCH):
            self.mm(self.ps[bk][:, 0:w], self.onesm[:, :], self.sq_ap(c, off, w), c == 0, c == NCH - 1,
                    reads=[self.sq_b[c][si], self.const_b], writes=[self.ps_b[bk]], inc=(c == NCH - 1))
        return bk

    def rstd_act(self, src, src_b, si, off, w, add_eps=True):
        r = self.st[:, off:off + w]
        self.act(r, src, AF.Ln, src_b, [self.st_b[si]], bias=self.cs("eps") if add_eps else None)
        self.act(r, r, AF.Exp, [self.st_b[si]], [self.st_b[si]], scale=-0.5)
        return r

    def sq_h(self, tl, si):
        off, w = tl["subs"][si]
        for c in range(NCH):
            self.act(self.sq_ap(c, off, w), self.h_ap(c, off, w), AF.Square, [self.hT_b[c][si]], [self.sq_b[c][si]])

    def pre_stats(self, tl, si):
        off, w = tl["subs"][si]
        bk = self.stats_mm(si, off, w)
        self.rstd_act(self.ps[bk][:, 0:w], [self.ps_b[bk]], si, off, w)

    def pre_apply(self, tl, si, c, gname):
        off, w = tl["subs"][si]
        o, i0, g, r = self.xn_ap(c, off, w), self.h_ap(c, off, w), self.cs(gname, c), self.st[:, off:off + w]
        self.dve((lambda e, o=o, i0=i0, g=g, r=r:
                  e.scalar_tensor_tensor(out=o, in0=i0, scalar=g, in1=r, op0=ALU.mult, op1=ALU.mult)),
                 [self.hT_b[c][si], self.st_b[si], self.const_b], [self.xn_b[c][si]])

    def pre(self, tl, si, gname):
        self.pre_stats(tl, si)
        for c in range(NCH):
            self.pre_apply(tl, si, c, gname)

    def post_apply(self, tl, si, c, gname, square):
        off, w = tl["subs"][si]
        a, g, h, r = self.a_ap(c, off, w), self.cs(gname, c), self.h_ap(c, off, w), self.st[:, off:off + w]
        self.dve((lambda e, a=a, g=g, r=r:
                  e.scalar_tensor_tensor(out=a, in0=a, scalar=g, in1=r, op0=ALU.mult, op1=ALU.mult)),
                 [self.a32_b[c][si], self.st_b[si], self.const_b], [self.a32_b[c][si]])
        self.dve((lambda e, a=a, h=h: e.tensor_tensor(out=h, in0=h, in1=a, op=ALU.add)),
                 [self.a32_b[c][si], self.hT_b[c][si]], [self.hT_b[c][si]])
        if square:
            self.act(self.sq_ap(c, off, w), h, AF.Square, [self.hT_b[c][si]], [self.sq_b[c][si]])

    def post(self, tl, si, gname, square):
        self.pre_stats(tl, si)
        for c in range(NCH):
            self.post_apply(tl, si, c, gname, square)

    def run_first_unit(self, nsub, s1_ops, pre1, blocks):
        if nsub == 1:
            for op in s1_ops:
                op()
            if pre1:
                pre1()
            for b in blocks:
                b(0)
            return
        nb0 = min(6, len(blocks))
        for c, op in enumerate(s1_ops):
            op()
            if c < nb0:
                blocks[c](0)
        for c in range(len(s1_ops), nb0):
            blocks[c](0)
        if pre1:
            pre1()
        for b in blocks[nb0:]:
            b(0)
        for b in blocks:
            b(1)

    def boundary(self, tl, post_g, pre_g, blocks):
        nsub = len(tl["subs"])
        self.post(tl, 0, post_g, True)
        self.pre(tl, 0, pre_g)
        if nsub == 1:
            self.run_first_unit(1, [], None, blocks)
            return
        self.pre_stats(tl, 1)
        s1 = [(lambda c=c: self.post_apply(tl, 1, c, post_g, True)) for c in range(NCH)]
        self.run_first_unit(nsub, s1, (lambda: self.pre(tl, 1, pre_g)), blocks)

    def gemm_blk(self, slot, e0, nk, rhs_fn, tl, evac):
        def blk(si):
            off, w = tl["subs"][si]
            bk = self.next_bank()
            for k in range(nk):
                rap, rb = rhs_fn(k, si, off, w)
                self.mm(self.ps[bk][:, 0:w], self.w_ap(slot, e0 + k * P, P), rap, k == 0, k == nk - 1,
                        reads=[self.w_b[slot]] + rb, writes=[self.ps_b[bk]], inc=(k == nk - 1))
            evac(si, off, w, self.ps[bk][:, 0:w], self.ps_b[bk])
        return blk

    def evac_f(self, mc, bias_ap):
        def ev(si, off, w, pap, pb):
            self.act(self.a_ap(mc, off, w), pap, AF.Identity, [pb], [self.a32_b[mc][si]], bias=bias_ap)
            self.act(self.sq_ap(mc, off, w), pap, AF.Square, [pb], [self.sq_b[mc][si]], bias=bias_ap)
        return ev

    def xr(self, k, si, off, w):
        return self.xn_ap(k, off, w), [self.xn_b[k][si]]

    def in_pair_blocks(self, slot, u, tl):
        blocks = []
        for j in range(4):
            mc = 4 * u + j
            tmps = {}

            def ev_gate(si, off, w, pap, pb, mc=mc, tmps=tmps):
                tap, tb = self.tmp_unit()
                tmps[si] = (tap, tb)
                self.act(tap[:, 0:w], pap, AF.Sigmoid, [pb], [tb], bias=self.cs("b_in_g", mc))

            def ev_val(si, off, w, pap, pb, mc=mc, tmps=tmps):
                tap, tb = tmps[si]
                o, bv = self.uT_ap(mc, UH + off, w), self.cs("b_in_v", mc)
                self.dve((lambda e, o=o, pap=pap, bv=bv, t=tap[:, 0:w]:
                          e.scalar_tensor_tensor(out=o, in0=pap, scalar=bv, in1=t, op0=ALU.add, op1=ALU.mult)),
                         [pb, tb, self.const_b], [self.uT_b[mc]])
            blocks.append(((2 * j + 1) * 1024, ev_gate))
            blocks.append(((2 * j) * 1024, ev_val))
        return blocks

    def mixer0_first_blocks(self, tl):
        u3 = self.r44[:, 0:NCH * self.UTW].rearrange("p (c t) -> p c t", c=NCH)
        h3 = self.uhalo[:, :].rearrange("p (c t) -> p c t", c=NCH)
        self.S.op("pool", (lambda e, o=u3[:, :, 0:UH], i0=h3: e.tensor_copy(out=o, in_=i0)),
                  reads=[self.uhalo_b], writes=list(self.uT_b))
        slot = self.w_get("in", 0)
        return [self.gemm_blk(slot, e0, 8, self.xr, tl, ev) for (e0, ev) in self.in_pair_blocks(slot, 0, tl)]

    def mixer0_rest(self, tl):
        subs = tl["subs"]
        W = tl["W"]
        ns = len(subs)
        u3 = self.r44[:, 0:NCH * self.UTW].rearrange("p (c t) -> p c t", c=NCH)
        h3 = self.uhalo[:, :].rearrange("p (c t) -> p c t", c=NCH)
        slot = self.w_get("in", 1)
        for (e0, ev) in self.in_pair_blocks(slot, 1, tl):
            self.gemm(slot, e0, 8, self.xr, subs, ev)
        if tl["halo"]:
            f = self.cs("flag")
            o = u3[:, :, UH:UH + W]
            self.dve((lambda e, o=o, f=f: e.tensor_scalar(out=o, in0=o, scalar1=f, scalar2=None, op0=ALU.mult)),
                     list(self.uT_b) + [self.const_b], list(self.uT_b))
        self.S.op("pool", (lambda e, o=h3, i0=u3[:, :, W:W + UH]: e.tensor_copy(out=o, in_=i0)),
                  reads=list(self.uT_b), writes=[self.uhalo_b])
        dslot = 0
        for c in range(NCH):
            banks = [self.next_bank() for _ in subs]
            for tg in range((CONVW + DT - 1) // DT):
                taps = list(range(tg * DT, min((tg + 1) * DT, CONVW)))
                nt = len(taps)
                ds = dslot % 2
                dslot += 1
                o3 = self.diag[:, ds * DT * P:(ds * DT + nt) * P].rearrange("p (j m) -> p j m", j=nt)
                i0 = self.ident[:, :].unsqueeze(1).to_broadcast([P, nt, P])
                i1 = self.cs("dw_w", c * CONVW + taps[0], nt).unsqueeze(2).to_broadcast([P, nt, P])
                self.S.op("pool", (lambda e, o3=o3, i0=i0, i1=i1: e.tensor_tensor(out=o3, in0=i0, in1=i1, op=ALU.mult)),
                          reads=[self.const_b], writes=[self.diag_b[ds]])
                for jj, tap in enumerate(taps):
                    lhs = self.diag[:, (ds * DT + jj) * P:(ds * DT + jj + 1) * P]
                    for si, (off, w) in enumerate(subs):
                        rhs = self.uT_ap(c, off + 2 + tap, w)
                        wr = [self.ps_b[b] for b in banks] if (tap == 0 and si == 0) else [self.ps_b[banks[si]]]
                        self.mm(self.ps[banks[si]][:, 0:w], lhs, rhs, tap == 0, tap == CONVW - 1,
                                reads=[self.diag_b[ds], self.uT_b[c]], writes=wr,
                                inc=(jj == nt - 1 and si == len(subs) - 1))
            for si, (off, w) in enumerate(subs):
                pap, pb = self.ps[banks[si]][:, 0:w], self.ps_b[banks[si]]
                b = self.cs("dw_b", c)
                self.act(self.a_ap(c, off, w), pap, AF.Identity, [pb], [self.a32_b[c][si]], bias=b)
                self.act(self.sq_ap(c, off, w), pap, AF.Square, [pb], [self.sq_b[c][si]], bias=b)
                self.act(self.xn_ap(c, off, w), pap, AF.Identity, [pb], [self.xn_b[c][si]], bias=b)

        def ln_stats(si):
            off, w = subs[si]
            for c in range(NCH):
                self.mm(self.ps[6][:, 0:w], self.onesm[:, :], self.xn_ap(c, off, w), c == 0, c == NCH - 1,
                        reads=[self.xn_b[c][si], self.const_b], writes=[self.ps_b[6]], inc=(c == NCH - 1))
            for c in range(NCH):
                self.mm(self.ps[7][:, 0:w], self.onesm[:, :], self.sq_ap(c, off, w), c == 0, c == NCH - 1,
                        reads=[self.sq_b[c][si], self.const_b], writes=[self.ps_b[7]], inc=(c == NCH - 1))
            m = self.st[:, 2 * SW + off: 2 * SW + off + w]
            r = self.st[:, off:off + w]
            mb, rb = self.st_b[2 + si], self.st_b[si]
            self.dve((lambda e, o=m, i=self.ps[6][:, 0:w]: e.tensor_copy(out=o, in_=i)), [self.ps_b[6]], [mb])
            self.dve((lambda e, o=r, i=m: e.tensor_tensor(out=o, in0=i, in1=i, op=ALU.mult)), [mb], [rb])
            self.dve((lambda e, o=r, i=self.ps[7][:, 0:w], ep=self.cs("eps"):
                      e.scalar_tensor_tensor(out=o, in0=i, scalar=ep, in1=o, op0=ALU.add, op1=ALU.subtract)),
                     [self.ps_b[7], rb, self.const_b], [rb])
            self.rstd_act(r, [rb], si, off, w, add_eps=False)

        def ln_apply(si, c):
            off, w = subs[si]
            a = self.a_ap(c, off, w)
            M = self.st[:, 2 * SW + off: 2 * SW + off + w]
            R = self.st[:, off:off + w]
            self.dve((lambda e, a=a, M=M: e.tensor_tensor(out=a, in0=a, in1=M, op=ALU.subtract)),
                     [self.a32_b[c][si], self.st_b[2 + si]], [self.a32_b[c][si]])
            self.dve((lambda e, a=a, R=R: e.tensor_tensor(out=a, in0=a, in1=R, op=ALU.mult)),
                     [self.a32_b[c][si], self.st_b[si]], [self.a32_b[c][si]])
            self.act(self.xn_ap(c, off, w), a, AF.Silu, [self.a32_b[c][si]], [self.xn_b[c][si]],
                     bias=self.cs("ln_b", c), scale=self.cs("ln_g", c))
        ln_stats(0)
        for c in range(NCH):
            ln_apply(0, c)
        slot = self.w_get("out", 0)
        blocks = [self.gemm_blk(slot, mc * 1024, 8, self.xr, tl, self.evac_f(mc, self.cs("b_out", mc)))
                  for mc in range(8)]
        if ns == 1:
            self.run_first_unit(1, [], None, blocks)
        else:
            ln_stats(1)
            self.run_first_unit(ns, [(lambda c=c: ln_apply(1, c)) for c in range(NCH)], None, blocks)

    def gu_blocks(self, slot, u, tl):
        p0, p1 = GU_UNITS[u]
        blocks = []
        for i in range(p0, p1):
            li = i - p0
            tmps = {}

            def ev_gate(si, off, w, pap, pb, tmps=tmps):
                tap, tb = self.tmp_unit()
                tmps[si] = (tap, tb)
                self.act(tap[:, 0:w], pap, AF.Silu, [pb], [tb])

            def ev_up(si, off, w, pap, pb, i=i, tmps=tmps):
                tap, tb = tmps[si]
                o = self.aT_ap(i, off, w)
                self.dve((lambda e, o=o, pap=pap, t=tap[:, 0:w]:
                          e.tensor_tensor(out=o, in0=pap, in1=t, op=ALU.mult)),
                         [pb, tb], [self.aT_b[i][si]])
            blocks.append(((2 * li) * 1024, ev_gate))
            blocks.append(((2 * li + 1) * 1024, ev_up))
        return blocks

    def ffn_first_blocks(self, tl, l):
        slot = self.w_get(f"gu{l}", 0)
        return [self.gemm_blk(slot, e0, 8, self.xr, tl, ev) for (e0, ev) in self.gu_blocks(slot, 0, tl)]

    def store_out(self, tl, si):
        off, w = tl["subs"][si]
        t0 = tl["t0"] - HALO
        dst = self.outT[:, NCH * (t0 + off): NCH * (t0 + off + w)]
        self.S.dma("act", (lambda e, dst=dst, src=self.h_sub(si, w): e.dma_start(out=dst, in_=src)), f"d:o{si}",
                   reads=[self.hT_b[c][si] for c in range(NCH)])

    def ffn_rest(self, tl, l, final=False):
        subs = tl["subs"]
        for u in range(1, len(GU_UNITS)):
            slot = self.w_get(f"gu{l}", u)
            for (e0, ev) in self.gu_blocks(slot, u, tl):
                self.gemm(slot, e0, 8, self.xr, subs, ev)
        ar = lambda k, si, off, w: (self.aT_ap(k, off, w), [self.aT_b[k][si]])
        for u in range(4):
            slot = self.w_get(f"dn{l}", u)
            if final and u == 3:
                blocks = [self.gemm_blk(slot, j * NF * P, NF, ar, tl, self.evac_f(2 * u + j, self.cs("zero")))
                          for j in range(2)]
                for si in range(len(subs)):
                    for b in blocks:
                        b(si)
                    self.post(tl, si, "g_ffn_post1", False)
                    self.store_out(tl, si)
                continue
            for j in range(2):
                mc = 2 * u + j
                self.gemm(slot, j * NF * P, NF, ar, subs, self.evac_f(mc, self.cs("zero")))

    def mixer1_first_blocks(self, tl):
        S = self.S
        for j in range(2):
            o, i0 = self.kT_ap(j, 0, P), self.kprev[:, j * P:(j + 1) * P]
            S.op("pool", (lambda e, o=o, i0=i0: e.tensor_copy(out=o, in_=i0)),
                 reads=[self.kprev_b], writes=[self.kT_b[j][0]])
        o, i0 = self.V_ap(0), self.vprev[:, :]
        S.op("pool", (lambda e, o=o, i0=i0: e.tensor_copy(out=o, in_=i0)),
             reads=[self.vprev_b], writes=[self.V_b[0]])
        if tl["halo"]:
            return []
        slot = self.w_get("q", 0)
        blocks = []
        for cq in range(8):
            def ev_q(si, off, w, pap, pb, cq=cq):
                self.act(self.qT_ap(cq, off, w), pap, AF.Identity, [pb], [self.qT_b[cq][si]],
                         bias=self.cs("b_q", cq))
            blocks.append(self.gemm_blk(slot, cq * 1024, 8, self.xr, tl, ev_q))
        return blocks

    def mixer1_rest(self, tl):
        subs = tl["subs"]
        W = tl["W"]
        nb = W // P
        S = self.S
        slot = self.w_get("kv", 0)
        for j in range(2):
            def ev_k(si, off, w, pap, pb, j=j):
                blks = [self.kT_b[j][1 + (off + x) // P] for x in range(0, w, P)]
                self.act(self.kT_ap(j, P + off, w), pap, AF.Identity, [pb], blks, bias=self.cs("b_k", j))
            self.gemm(slot, j * 1024, 8, self.xr, subs, ev_k)
        for b in range(nb):
            bk = self.next_bank()
            si = (b * P) // SW if not tl["halo"] else 0
            for k in range(NCH):
                self.mm(self.ps[bk][:, 0:256], self.xn_ap(k, b * P, P), self.w_ap(slot, 2048 + k * 256, 256),
                        k == 0, k == NCH - 1, reads=[self.w_b[slot], self.xn_b[k][si]],
                        writes=[self.ps_b[bk]], inc=(k == NCH - 1))
            o, pap, bv = self.V_ap(1 + b), self.ps[bk][:, 0:256], self.cs("b_v", 0, 256)
            self.dve((lambda e, o=o, pap=pap, bv=bv: e.tensor_tensor(out=o, in0=pap, in1=bv, op=ALU.add)),
                     [self.ps_b[bk], self.const_b], [self.V_b[1 + b]])

        if not tl["halo"]:
            iters = [(b, pp) for b in range(nb) for pp in range(2)]
            state = {}

            def stage_a(it):
                b, pp = iters[it]
                si = (b * P) // SW
                pts = []
                for hh in range(2):
                    rows = slice(hh * 64, hh * 64 + 64)
                    for kb in range(2):
                        bank = hh * 2 + kb
                        q3 = self.r44[rows, self.QO + pp * 4 * TT: self.QO + (pp + 1) * 4 * TT] \
                            .rearrange("p (c t) -> p c t", c=4)[:, :, b * P:(b + 1) * P]
                        o3 = self.ps[bank][:, :].rearrange("p (g q) -> p g q", g=4)
                        self.mm(o3, self.kT_ap(pp, (b + kb) * P, P, rows=rows), q3, True, False,
                                reads=[self.kT_b[pp][b + kb]] + [self.qT_b[pp * 4 + g][si] for g in range(4)],
                                writes=[self.ps_b[bank]], inc=False, tp=(hh * 64, 0))
                for hh in range(2):
                    for kb in range(2):
                        bank = hh * 2 + kb
                        h0 = 4 * (2 * pp + hh)
                        o3 = self.ps[bank][:, :].rearrange("p (g q) -> p g q", g=4)
                        b3 = self.bhi[:, :].rearrange("p (h k q) -> p h k q", h=NQH, k=2)[:, h0:h0 + 4, kb, :]
                        self.mm(o3, self.ident[:, :], b3, False, True, reads=[self.const_b],
                                writes=[self.ps_b[bank]], inc=True)
                        pc, psi = (it * 4 + bank) % 8, 1
                        pt = self.a_ap(pc, psi * SW, SW).bitcast(BF16)[:, 0:SW]
                        ptb = self.a32_b[pc][psi]
                        self.act(pt, self.ps[bank][:, :], AF.Exp, [self.ps_b[bank]], [ptb], scale=0.125)
                        if tl["idx"] == 1 and b == 0 and kb == 0:
                            f = self.cs("flag")
                            self.dve((lambda e, pt=pt, f=f:
                                      e.tensor_scalar(out=pt, in0=pt, scalar1=f, scalar2=None, op0=ALU.mult)),
                                     [ptb, self.const_b], [ptb])
                        pts.append((hh, kb, pt, ptb))
                state[it] = pts

            def stage_b(it):
                b, pp = iters[it]
                si = (b * P) // SW
                pts = state.pop(it)
                bo = 4 + 2 * (it % 2)
                bd = bo + 1
                for (hh, kb, pt, ptb) in pts:
                    vcol = (2 * pp + hh) * 64
                    self.mm(self.ps[bo][hh * 64:hh * 64 + 64, :], self.V_ap(b + kb, vcol, 64), pt,
                            kb == 0, kb == 1, reads=[self.V_b[b + kb], ptb], writes=[self.ps_b[bo]],
                            inc=(hh == 1 and kb == 1), tp=(0, hh * 64))
                for (hh, kb, pt, ptb) in pts:
                    self.mm(self.ps[bd][hh * 64:hh * 64 + 64, :], self.ones1[:, 0:64], pt,
                            kb == 0, kb == 1, reads=[ptb, self.const_b], writes=[self.ps_b[bd]],
                            inc=(hh == 1 and kb == 1), tp=(0, hh * 64))
                rc, rcb = self.st[:, (2 + it % 2) * SW:(3 + it % 2) * SW], self.st_b[2 + it % 2]
                for g in range(4):
                    o, i0, sk = rc[:, g * P:(g + 1) * P], self.ps[bd][:, g * P:(g + 1) * P], \
                        self.exps[:, pp * 4 + g: pp * 4 + g + 1]
                    self.dve((lambda e, o=o, i0=i0, sk=sk:
                              e.tensor_scalar(out=o, in0=i0, scalar1=sk, scalar2=None, op0=ALU.add)),
                             [self.ps_b[bd], self.const_b], [rcb])
                self.act(rc, rc, AF.Ln, [rcb], [rcb])
                self.act(rc, rc, AF.Exp, [rcb], [rcb], scale=-1.0)
                o3 = self.r44[:, self.AO + pp * 4 * TT: self.AO + (pp + 1) * 4 * TT] \
                    .rearrange("p (c t) -> p c t", c=4)[:, :, b * P:(b + 1) * P]
                i3 = self.ps[bo][:, :].rearrange("p (g q) -> p g q", g=4)
                r3 = rc.rearrange("p (g q) -> p g q", g=4)
                self.dve((lambda e, o3=o3, i3=i3, r3=r3: e.tensor_tensor(out=o3, in0=i3, in1=r3, op=ALU.mult)),
                         [self.ps_b[bo], rcb], [self.at_b[pp * 4 + g][si] for g in range(4)])

            n = len(iters)
            stage_a(0)
            for it in range(1, n):
                stage_a(it)
                stage_b(it - 1)
            stage_b(n - 1)
        for j in range(2):
            o, i0 = self.kprev[:, j * P:(j + 1) * P], self.kT_ap(j, W, P)
            S.op("pool", (lambda e, o=o, i0=i0: e.tensor_copy(out=o, in_=i0)),
                 reads=[self.kT_b[j][nb]], writes=[self.kprev_b])
        o, i0 = self.vprev[:, :], self.V_ap(nb)
        S.op("pool", (lambda e, o=o, i0=i0: e.tensor_copy(out=o, in_=i0)),
             reads=[self.V_b[nb]], writes=[self.vprev_b])
        if tl["halo"]:
            return
        ar = lambda k, si, off, w: (self.at_ap(k, off, w), [self.at_b[k][si]])
        slot = self.w_get("o", 0)
        for mc in range(8):
            self.gemm(slot, mc * 1024, 8, ar, subs, self.evac_f(mc, self.cs("b_o", mc)))

    def h_sub(self, si, w):
        blk = self.hT[:, si * NCH * SW:(si + 1) * NCH * SW]
        if w == SW:
            return blk
        return blk.rearrange("p (c t) -> p c t", c=NCH)[:, :, 0:w]

    def load_x(self, tl):
        subs, t0 = tl["subs"], tl["t0"]
        for si, (off, w) in enumerate(subs):
            src = self.xT[:, NCH * (t0 + off): NCH * (t0 + off + w)]
            if w != SW:
                src = src.rearrange("p (c t) -> p c t", c=NCH)
            self.S.dma("sp", (lambda e, dst=self.h_sub(si, w), src=src: e.dma_start(out=dst, in_=src)), f"d:x{si}",
                       writes=[self.hT_b[c][si] for c in range(NCH)])

    def layout(self, which):
        if which == "A":
            return list(self.uT_b)
        if which == "B":
            return [b for row in self.aT_b for b in row]
        return [b for row in self.qT_b for b in row] + [b for row in self.at_b for b in row] + \
               [b for row in self.kT_b for b in row] + list(self.V_b)

    def handoff(self, old, new):
        bufs = []
        for k in old + new:
            bufs += self.layout(k)
        self.S.op("pool", (lambda e: e.tensor_copy(out=self.dummy[:, 0:8], in_=self.dummy[:, 8:16])),
                  reads=[], writes=bufs + [self.dummy_b])

    def emit_all(self):
        S = self.S
        nc = self.nc
        S.dma("sp", (lambda e: e.dma_start(out=self.consts[:, :], in_=self.consts_d)), "d:c0", writes=[self.const_b])
        bias_b = Buf("biasld")
        self.load_x(self.tiles()[0])
        self.identf = self.a32[:, 0:P]
        self.maskT = self.a32[:, P:3 * P]
        self.biasT = self.a32[:, 4 * P: 4 * P + NQH * 2 * P]
        alla = [b for row in self.a32_b for b in row]
        S.dma("sp", (lambda e: e.dma_start(out=self.biasT, in_=self.biasT_d)), "d:c1", writes=[bias_b] + alla)
        S.dma("sp", (lambda e: e.dma_start(out=self.maskT, in_=self.maskT_d)), "d:c2", writes=[bias_b] + alla)
        S.op("pool", (lambda e: e.memset(self.onesm[:, :], 1.0 / D)), writes=[self.const_b], reads=[])
        S.op("pool", (lambda e: e.memset(self.ones1[:, :], 1.0)), writes=[self.const_b], reads=[])
        S.op("pool", (lambda e: e.memset(self.uhalo[:, :], 0.0)), writes=[self.uhalo_b])
        S.op("pool", (lambda e: e.memset(self.kprev[:, :], 0.0)), writes=[self.kprev_b])
        S.op("pool", (lambda e: e.memset(self.vprev[:, :], 0.0)), writes=[self.vprev_b])
        S.dma("sp", (lambda e: e.dma_start(out=self.identf, in_=self.ident_d)), "d:c3",
              writes=[self.const_b] + alla)
        S.op("pool", (lambda e: e.tensor_copy(out=self.ident[:, :], in_=self.identf)),
             reads=[self.const_b] + alla, writes=[self.const_b])
        self.act(self.exps[:, :], self.cs("sinks", 0, 8), AF.Exp, [], [self.const_b])
        for h in range(NQH):
            o = self.biasT[:, h * 2 * P:(h + 1) * 2 * P]
            self.dve((lambda e, o=o: e.tensor_tensor(out=o, in0=o, in1=self.maskT, op=ALU.add)),
                     [bias_b] + alla, [bias_b] + alla)
        self.dve((lambda e: e.tensor_scalar(out=self.biasT, in0=self.biasT, scalar1=8.0, scalar2=None, op0=ALU.mult)),
                 [bias_b] + alla, [bias_b] + alla)
        self.dve((lambda e: e.tensor_copy(out=self.bhi[:, :], in_=self.biasT)), [bias_b] + alla, [self.const_b])

        S.op("pool", (lambda e: e.memset(self.dummy[:, :], 0.0)), writes=[self.dummy_b])
        for tl in self.tiles():
            subs, W, t0 = tl["subs"], tl["W"], tl["t0"]
            if tl["idx"] > 0:
                self.load_x(tl)
            nsub = len(subs)
            if tl["idx"] > 0:
                self.handoff(["B", "C"], ["A"])
            for si in range(nsub):
                self.sq_h(tl, si)
            self.pre(tl, 0, "g_mix_pre0")
            blocks = self.mixer0_first_blocks(tl)
            self.run_first_unit(nsub, [], (lambda: self.pre(tl, 1, "g_mix_pre0")) if nsub > 1 else None, blocks)
            self.mixer0_rest(tl)
            self.handoff(["A"], ["B"])
            self.boundary(tl, "g_mix_post0", "g_ffn_pre0", self.ffn_first_blocks(tl, 0))
            self.ffn_rest(tl, 0)
            self.handoff(["B"], ["C"])
            self.boundary(tl, "g_ffn_post0", "g_mix_pre1", self.mixer1_first_blocks(tl))
            self.mixer1_rest(tl)
            if tl["halo"]:
                continue
            self.handoff(["C"], ["B"])
            self.boundary(tl, "g_mix_post1", "g_ffn_pre1", self.ffn_first_blocks(tl, 1))
            self.ffn_rest(tl, 1, final=True)
        assert self.wnext == len(self.plan), (self.wnext, len(self.plan))

    def finalize(self, es):
        nc = self.nc
        S = self.S
        names = list(S.engs.keys()) + sorted(S.dcnt.keys())
        sems = {n: es.enter_context(nc.semaphore(n.replace(":", "_"))) for n in names}
        final_waits = [(n, v) for n, v in S.dcnt.items() if n.startswith("d:o")]

        def replay(e, ops, tail=()):
            for waits, fn, incsem, incval in ops:
                for sname, v in waits[1:]:
                    e.wait_ge(sems[sname], v)
                ins = fn(e)
                if waits:
                    ins._wait_ge(sems[waits[0][0]], waits[0][1])
                if incsem is not None:
                    ins.then_inc(sems[incsem], incval)
            for sname, v in tail:
                e.wait_ge(sems[sname], v)

        with nc.Block() as block:
            @block.tensor
            def _(e):
                replay(e, S.engs["pe"].ops)

            @block.scalar
            def _(e):
                replay(e, S.engs["act"].ops)

            @block.vector
            def _(e):
                replay(e, S.engs["dve"].ops)

            @block.gpsimd
            def _(e):
                replay(e, S.engs["pool"].ops)

            @block.sync
            def _(e):
                replay(e, S.engs["sp"].ops, tail=final_waits)


def _vec8(v):
    return np.ascontiguousarray(np.asarray(v, np.float32).reshape(NCH, P).T)


def _blk(Wm, cols_list):
    K = Wm.shape[0]
    kc = K // P
    out = np.empty((P, len(cols_list), kc, P), np.float32)
    for j, cols in enumerate(cols_list):
        out[:, j] = Wm[:, cols].reshape(kc, P, P).transpose(1, 0, 2)
    return out.reshape(P, -1)


def _t5_bucket(dist):
    dist = np.maximum(dist, 0)
    max_exact = 16
    large = max_exact + (np.log(np.maximum(dist, 1).astype(np.float32) / np.float32(max_exact))
                         / np.float32(np.log(128.0 / max_exact)) * np.float32(32 - max_exact)).astype(np.int32)
    large = np.minimum(large, 31)
    return np.where(dist < max_exact, dist, large)


def _qhead(cq, half):
    pp, g = cq // 4, cq % 4
    return 4 * (2 * pp + half) + g


def _prep_shared(inp):
    f = lambda k: np.asarray(inp[k], np.float32)
    consts = np.zeros((P, NCONST), np.float32)

    def put(name, arr):
        arr = np.asarray(arr, np.float32)
        consts[:, _CL[name]:_CL[name] + arr.shape[1]] = arr
    for l in range(2):
        put(f"g_mix_pre{l}", _vec8(f("mix_pre_g")[l]))
        put(f"g_mix_post{l}", _vec8(f("mix_post_g")[l]))
        put(f"g_ffn_pre{l}", _vec8(f("ffn_pre_g")[l]))
        put(f"g_ffn_post{l}", _vec8(f("ffn_post_g")[l]))
    b_in = f("conv_b_in")[0]
    put("b_in_v", _vec8(b_in[:D]))
    put("b_in_g", _vec8(b_in[D:]))
    put("dw_b", _vec8(f("conv_dw_b")[0]))
    put("ln_g", _vec8(f("conv_ln_g")[0]))
    put("ln_b", _vec8(f("conv_ln_b")[0]))
    put("b_out", _vec8(f("conv_b_out")[0]))
    dww = f("conv_dw_w")[0]
    put("dw_w", dww.T.reshape(NCH, P, CONVW).transpose(1, 0, 2).reshape(P, NCH * CONVW))
    bqkv = f("attn_b_qkv")[0]
    qcols = [np.concatenate([_qhead(cq, 0) * 64 + np.arange(64), _qhead(cq, 1) * 64 + np.arange(64)])
             for cq in range(8)]
    put("b_q", np.stack([bqkv[c] for c in qcols], axis=1))
    put("b_k", np.stack([bqkv[D + j * P: D + (j + 1) * P] for j in range(2)], axis=1))
    put("b_o", _vec8(f("attn_b_o")[0]))
    sinks = f("attn_sinks")[0]
    sk = np.zeros((P, 8), np.float32)
    for cq in range(8):
        sk[:64, cq] = sinks[_qhead(cq, 0)]
        sk[64:, cq] = sinks[_qhead(cq, 1)]
    put("sinks", sk)
    consts[:, _CL["eps"]] = EPS
    put("b_v", np.broadcast_to(bqkv[D + 256: D + 512][None, :], (P, 256)))

    s_i = np.arange(P)[:, None]
    q_i = np.arange(P)[None, :]
    dist = np.stack([q_i + P - s_i, q_i - s_i], axis=0)
    valid = (dist >= 0) & (dist < P)
    bucket = _t5_bucket(dist)
    rel = f("rel_bias")
    biasT = rel[bucket]
    biasT = np.ascontiguousarray(biasT.transpose(1, 3, 0, 2)).reshape(P, NQH * 2 * P)
    maskT = np.where(valid, np.float32(0.0), np.float32(NEG)).astype(np.float32)
    maskT = np.ascontiguousarray(maskT.transpose(1, 0, 2)).reshape(P, 2 * P)

    ar = np.arange(P)
    w_in = f("conv_w_in")[0]
    w_out = f("conv_w_out")[0]
    wqkv = f("attn_w_qkv")[0]
    wo = f("attn_w_o")[0]
    orow = np.concatenate(qcols)
    wo_p = wo[orow, :]
    in_units = []
    for u in range(2):
        cl = []
        for j in range(4):
            mc = 4 * u + j
            cl += [mc * P + ar, D + mc * P + ar]
        in_units.append(_blk(w_in, cl))
    wv_blk = np.ascontiguousarray(wqkv[:, D + 256: D + 512].reshape(NCH, P, 256).transpose(1, 0, 2)).reshape(P, 2048)
    sh = {
        "consts": consts, "biasT": biasT, "maskT": maskT, "ident": np.eye(P, dtype=np.float32),
        "w_in": np.concatenate(in_units, 0),
        "w_out": _blk(w_out, [mc * P + ar for mc in range(8)]),
        "w_q": _blk(wqkv, qcols),
        "w_kv": np.concatenate([_blk(wqkv, [D + ar, D + P + ar]), wv_blk], axis=1),
        "w_o": _blk(wo_p, [mc * P + ar for mc in range(8)]),
    }
    for l in range(2):
        wgu = f("ffn_w_gate_up")[l]
        wdn = f("ffn_w_down")[l]
        gus = []
        for (p0, p1) in GU_UNITS:
            cl = []
            for i in range(p0, p1):
                cl += [i * P + ar, DFF + i * P + ar]
            blk = np.zeros((P, 8192), np.float32)
            blk[:, :len(cl) * 1024] = _blk(wgu, cl)
            gus.append(blk)
        sh[f"w_gu{l}"] = np.concatenate(gus, 0)
        sh[f"w_dn{l}"] = np.concatenate([_blk(wdn, [2 * u * P + ar, (2 * u + 1) * P + ar]) for u in range(4)], 0)
    return sh


_PROG_CACHE = {}


def kernel(**inputs):
    x = np.asarray(inputs["x"], np.float32)
    sh = _prep_shared(inputs)
    in_maps = []
    for core in range(NCORES):
        b, half = core // 2, core % 2
        start = half * TOK
        xl = np.zeros((TLOC, D), np.float32)
        xl[HALO:] = x[b, start:start + TOK]
        if half == 1:
            xl[:HALO] = x[b, start - HALO:start]
        x3 = xl.T.reshape(NCH, P, TLOC).transpose(1, 0, 2)
        xT = np.concatenate([x3[:, :, tl["t0"] + off: tl["t0"] + off + w].reshape(P, -1)
                             for tl in Prog.tiles() for (off, w) in tl["subs"]], axis=1)
        xT = np.ascontiguousarray(xT)
        m = dict(sh)
        c = sh["consts"].copy()
        c[:, _CL["flag"]] = 1.0 if half == 1 else 0.0
        m["consts"] = c
        m["xT"] = xT
        in_maps.append(m)
    if "nc" not in _PROG_CACHE:
        _PROG_CACHE["nc"] = Prog().build()
    res = run_bass_kernel_spmd(_PROG_CACHE["nc"], in_maps, core_ids=list(range(NCORES)))
    out = np.empty((BATCH, SEQ, D), np.float32)
    for core in range(NCORES):
        b, half = core // 2, core % 2
        oT = np.asarray(res.results[core]["outT"]).reshape(P, TOK // SW, NCH, SW)
        out[b, half * TOK:(half + 1) * TOK] = oT.transpose(1, 3, 2, 0).reshape(TOK, D)
    return out
```

```python
import numpy as np
from contextlib import ExitStack
import concourse.bass as bass
import concourse.mybir as mybir
from concourse.bass_utils import run_bass_kernel_spmd

F32 = mybir.dt.float32
BF16 = mybir.dt.bfloat16
AF = mybir.ActivationFunctionType
ALU = mybir.AluOpType

P = 128
D = 1024
NCH = 8
DFF = 2816
NF = 22
SEQ = 8192
BATCH = 4
NCORES = 8
TOK = 4096
HALO = 256
TLOC = TOK + HALO
TT = 1024
SW = 512
CONVW = 31
UH = 32
DT = 16
NQH = 16
EPS = 1e-6
NEG = -30000.0
NSLOT = 2
SLOT_E = 8192

GU_UNITS = [(0, 4), (4, 8), (8, 12), (12, 16), (16, 19), (19, 22)]

_CL = {}
_off = 0
for _n, _w in [("g_mix_pre0", 8), ("g_mix_post0", 8), ("g_ffn_pre0", 8), ("g_ffn_post0", 8),
               ("g_mix_pre1", 8), ("g_mix_post1", 8), ("g_ffn_pre1", 8), ("g_ffn_post1", 8),
               ("b_in_v", 8), ("b_in_g", 8), ("dw_b", 8), ("ln_g", 8), ("ln_b", 8), ("b_out", 8),
               ("dw_w", 8 * CONVW), ("b_q", 8), ("b_k", 2), ("b_o", 8), ("flag", 1), ("sinks", 8),
               ("eps", 1), ("zero", 1), ("b_v", 256)]:
    _CL[_n] = _off
    _off += _w
NCONST = _off


class Buf:
    __slots__ = ("name", "w", "r")

    def __init__(self, name):
        self.name = name
        self.w = None
        self.r = {}


class _Eng:
    def __init__(self, name):
        self.name = name
        self.cnt = 0
        self.ops = []
        self.waited = {}


class Sched:
    def __init__(self):
        self.engs = {k: _Eng(k) for k in ("pe", "act", "dve", "pool", "sp")}
        self.dcnt = {}

    def _waits(self, e, reads, writes):
        deps = {}
        for b in reads:
            if b.w is not None and deps.get(b.w[0], 0) < b.w[1]:
                deps[b.w[0]] = b.w[1]
        for b in writes:
            if b.w is not None and deps.get(b.w[0], 0) < b.w[1]:
                deps[b.w[0]] = b.w[1]
            for s, v in b.r.items():
                if deps.get(s, 0) < v:
                    deps[s] = v
        waits = []
        for s, v in deps.items():
            if e.name == "pe" and s == "pe":
                continue
            if e.waited.get(s, 0) >= v:
                continue
            if s == e.name:
                assert v <= e.cnt, (e.name, v, e.cnt)
            e.waited[s] = v
            waits.append((s, v))
        return waits

    def op(self, eng, fn, reads=(), writes=(), inc=True):
        e = self.engs[eng]
        assert inc or eng == "pe"
        waits = self._waits(e, reads, writes)
        ev = (eng, e.cnt + 1)
        e.ops.append((waits, fn, eng if inc else None, 1))
        if inc:
            e.cnt += 1
        for b in reads:
            if b.r.get(eng, 0) < ev[1]:
                b.r[eng] = ev[1]
        for b in writes:
            b.w = ev
            b.r = {}

    def dma(self, eng, fn, dsem, reads=(), writes=()):
        e = self.engs[eng]
        waits = self._waits(e, reads, writes)
        self.dcnt[dsem] = self.dcnt.get(dsem, 0) + 16
        ev = (dsem, self.dcnt[dsem])
        e.ops.append((waits, fn, dsem, 16))
        for b in reads:
            if b.r.get(dsem, 0) < ev[1]:
                b.r[dsem] = ev[1]
        for b in writes:
            b.w = ev
            b.r = {}


class Prog:
    def __init__(self):
        self.nc = bass.Bass("TRN2", target_bir_lowering=False)
        self.S = Sched()
        self.bank_rr = 0
        self.tmp_rr = 0
        self.wnext = 0
        self.wissued = 0

    @staticmethod
    def tiles():
        t = [dict(idx=0, t0=0, W=HALO, subs=[(0, HALO)], halo=True)]
        for i in range(TOK // TT):
            t.append(dict(idx=i + 1, t0=HALO + i * TT, W=TT,
                          subs=[(s * SW, SW) for s in range(TT // SW)], halo=False))
        return t

    def weight_plan(self):
        plan = []
        for tl in self.tiles():
            plan += [("in", 0), ("in", 1), ("out", 0)]
            plan += [("gu0", u) for u in range(len(GU_UNITS))]
            plan += [("dn0", u) for u in range(4)]
            if tl["halo"]:
                plan.append(("kv", 0))
                continue
            plan += [("q", 0), ("kv", 0), ("o", 0)]
            plan += [("gu1", u) for u in range(len(GU_UNITS))]
            plan += [("dn1", u) for u in range(4)]
        return plan

    def unit_E(self, kind, idx):
        if kind.startswith("gu"):
            a, b = GU_UNITS[idx]
            return (b - a) * 2048
        return {"in": 8192, "out": 8192, "dn0": 5632, "dn1": 5632, "q": 8192, "kv": 4096, "o": 8192}[kind]

    def build(self):
        nc = self.nc
        S = self.S
        es = ExitStack()
        with es:
            def dram(name, shape, dt=F32, kind="ExternalInput"):
                return nc.dram_tensor(name, shape, dt, kind=kind).ap()

            self.xT = dram("xT", [P, NCH * TLOC])
            self.outT = dram("outT", [P, NCH * TOK], kind="ExternalOutput")
            self.consts_d = dram("consts", [P, NCONST])
            self.biasT_d = dram("biasT", [P, NQH * 2 * P])
            self.maskT_d = dram("maskT", [P, 2 * P])
            self.ident_d = dram("ident", [P, P])
            self.wd = {
                "in": dram("w_in", [2 * P, 8192]), "out": dram("w_out", [P, 8192]),
                "gu0": dram("w_gu0", [6 * P, 8192]), "gu1": dram("w_gu1", [6 * P, 8192]),
                "dn0": dram("w_dn0", [4 * P, 5632]), "dn1": dram("w_dn1", [4 * P, 5632]),
                "q": dram("w_q", [P, 8192]), "kv": dram("w_kv", [P, 4096]), "o": dram("w_o", [P, 8192]),
            }

            def sb(name, shape, dt):
                return es.enter_context(nc.sbuf_tensor(name, shape, dt))

            self.hT = sb("hT", [P, NCH * TT], F32)
            self.xn = sb("xn", [P, NCH * TT], BF16)
            self.sq = sb("sq", [P, NCH * TT], BF16)
            self.a32 = sb("a32", [P, NCH * TT], F32)
            self.r44 = sb("r44", [P, NF * TT], BF16)
            self.uhalo = sb("uhalo", [P, NCH * UH], BF16)
            self.kprev = sb("kprev", [P, 2 * P], BF16)
            self.vprev = sb("vprev", [P, 256], BF16)
            self.wring = sb("wring", [P, NSLOT * SLOT_E], BF16)
            self.diag = sb("diag", [P, 2 * DT * P], BF16)
            self.ident = sb("identb", [P, P], BF16)
            self.onesm = sb("onesm", [P, P], BF16)
            self.ones1 = sb("ones1", [P, P], BF16)
            self.consts = sb("constsb", [P, NCONST], F32)
            self.exps = sb("exps", [P, 8], F32)
            self.bhi = sb("bhi", [P, NQH * 2 * P], BF16)
            self.st = sb("st", [P, 4 * SW], F32)
            self.dummy = sb("dummyt", [P, 16], F32)
            self.dummy_b = Buf("dummy")
            self.ps = [es.enter_context(nc.psum_tensor(f"ps{i}", [P, SW], F32)) for i in range(8)]

            nsub = TT // SW
            self.hT_b = [[Buf(f"hT{c}_{s}") for s in range(nsub)] for c in range(NCH)]
            self.xn_b = [[Buf(f"xn{c}_{s}") for s in range(nsub)] for c in range(NCH)]
            self.sq_b = [[Buf(f"sq{c}_{s}") for s in range(nsub)] for c in range(NCH)]
            self.a32_b = [[Buf(f"a32{c}_{s}") for s in range(nsub)] for c in range(NCH)]
            self.r44_b = Buf("r44")
            self.aT_b = [[Buf(f"aT{i}_{s}") for s in range(nsub)] for i in range(NF)]
            self.uT_b = [Buf(f"uT{c}") for c in range(NCH)]
            self.qT_b = [[Buf(f"qT{c}_{s}") for s in range(nsub)] for c in range(NCH)]
            self.at_b = [[Buf(f"at{c}_{s}") for s in range(nsub)] for c in range(NCH)]
            self.kT_b = [[Buf(f"kT{j}_{b}") for b in range(TT // P + 1)] for j in range(2)]
            self.V_b = [Buf(f"V{b}") for b in range(TT // P + 1)]
            self.uhalo_b = Buf("uhalo")
            self.kprev_b = Buf("kprev")
            self.vprev_b = Buf("vprev")
            self.w_b = [Buf(f"w{i}") for i in range(NSLOT)]
            self.diag_b = [Buf("diag0"), Buf("diag1")]
            self.const_b = Buf("const")
            self.st_b = [Buf(f"st{i}") for i in range(4)]
            self.ps_b = [Buf(f"ps{i}") for i in range(8)]
            self.plan = self.weight_plan()

            self.emit_all()
            self.finalize(es)
        return nc

    def cs(self, name, c=0, n=1):
        o = _CL[name] + c
        return self.consts[:, o:o + n]

    def h_ap(self, c, off, w):
        si, o = off // SW, off % SW
        assert o + w <= SW
        base = si * NCH * SW + c * SW + o
        return self.hT[:, base: base + w]

    def xn_ap(self, c, off, w):
        return self.xn[:, c * TT + off: c * TT + off + w]

    def sq_ap(self, c, off, w):
        return self.sq[:, c * TT + off: c * TT + off + w]

    def a_ap(self, c, off, w):
        return self.a32[:, c * TT + off: c * TT + off + w]

    def aT_ap(self, i, off, w):
        return self.r44[:, i * TT + off: i * TT + off + w]

    UTW = UH + TT

    def uT_ap(self, c, col, w):
        return self.r44[:, c * self.UTW + col: c * self.UTW + col + w]

    QO = 0
    AO = 8 * TT
    KO = 16 * TT
    KW = P + TT
    VO = 16 * TT + 2 * (P + TT)

    def qT_ap(self, c, off, w, rows=slice(0, P)):
        return self.r44[rows, self.QO + c * TT + off: self.QO + c * TT + off + w]

    def at_ap(self, c, off, w):
        return self.r44[:, self.AO + c * TT + off: self.AO + c * TT + off + w]

    def kT_ap(self, j, col, w, rows=slice(0, P)):
        return self.r44[rows, self.KO + j * self.KW + col: self.KO + j * self.KW + col + w]

    def V_ap(self, blk, c0=0, w=256):
        return self.r44[:, self.VO + blk * 256 + c0: self.VO + blk * 256 + c0 + w]

    def w_ap(self, slot, e0, w):
        return self.wring[:, slot * SLOT_E + e0: slot * SLOT_E + e0 + w]

    def tmp_unit(self):
        c = self.tmp_rr % 8
        self.tmp_rr += 1
        return self.a_ap(c, 0, SW), self.a32_b[c][0]

    def next_bank(self):
        b = self.bank_rr % 6
        self.bank_rr += 1
        return b

    def w_issue_upto(self, n):
        S = self.S
        while self.wissued <= min(n, len(self.plan) - 1):
            i = self.wissued
            kind, idx = self.plan[i]
            slot = i % NSLOT
            E = self.unit_E(kind, idx)
            src = self.wd[kind][idx * P:(idx + 1) * P, 0:E]
            dst = self.w_ap(slot, 0, E)
            S.dma("pool", (lambda e, dst=dst, src=src: e.dma_start(out=dst, in_=src)), f"d:w{slot}",
                  writes=[self.w_b[slot]])
            self.wissued += 1

    def w_get(self, kind, idx):
        i = self.wnext
        assert self.plan[i] == (kind, idx), (self.plan[i], kind, idx)
        self.w_issue_upto(i + NSLOT - 1)
        self.wnext += 1
        return i % NSLOT

    def mm(self, out, lhsT, rhs, start, stop, reads, writes, inc, tp=None):
        if tp is None:
            fn = (lambda e, out=out, lhsT=lhsT, rhs=rhs, start=start, stop=stop:
                  e.matmul(out, lhsT=lhsT, rhs=rhs, start=start, stop=stop))
        else:
            fn = (lambda e, out=out, lhsT=lhsT, rhs=rhs, start=start, stop=stop, tp=tp:
                  e.matmul(out, lhsT=lhsT, rhs=rhs, start=start, stop=stop, tile_position=tp))
        self.S.op("pe", fn, reads=reads, writes=writes, inc=inc)

    def act(self, out, in_, func, reads, writes, bias=None, scale=1.0):
        if bias is None:
            bias = self.cs("zero")
        self.S.op("act", (lambda e, out=out, in_=in_, func=func, bias=bias, scale=scale:
                          e.activation(out=out, in_=in_, func=func, bias=bias, scale=scale)),
                  reads=list(reads) + [self.const_b], writes=writes)

    def dve(self, fn, reads, writes):
        self.S.op("dve", fn, reads=reads, writes=writes)

    def gemm(self, slot, e0, nk, rhs_fn, subs, evac):
        banks = [self.next_bank() for _ in subs]
        for k in range(nk):
            for si, (off, w) in enumerate(subs):
                rap, rb = rhs_fn(k, si, off, w)
                lhs = self.w_ap(slot, e0 + k * P, P)
                wr = [self.ps_b[b] for b in banks] if (k == 0 and si == 0) else [self.ps_b[banks[si]]]
                self.mm(self.ps[banks[si]][:, 0:w], lhs, rap, k == 0, k == nk - 1,
                        reads=[self.w_b[slot]] + rb, writes=wr, inc=(k == nk - 1 and si == len(subs) - 1))
        for si, (off, w) in enumerate(subs):
            evac(si, off, w, self.ps[banks[si]][:, 0:w], self.ps_b[banks[si]])

    def stats_mm(self, si, off, w):
        bk = 6 + (si % 2)
        for c in range(NCH):
            self.mm(self.ps[bk][:, 0:w], self.onesm[:, :], self.sq_ap(c, off, w), c == 0, c == NCH - 1,
                    reads=[self.sq_b[c][si], self.const_b], writes=[self.ps_b[bk]], inc=(c == NCH - 1))
        return bk

    def rstd_act(self, src, src_b, si, off, w, add_eps=True):
        r = self.st[:, off:off + w]
        self.act(r, src, AF.Ln, src_b, [self.st_b[si]], bias=self.cs("eps") if add_eps else None)
        self.act(r, r, AF.Exp, [self.st_b[si]], [self.st_b[si]], scale=-0.5)
        return r

    def sq_h(self, tl, si):
        off, w = tl["subs"][si]
        for c in range(NCH):
            self.act(self.sq_ap(c, off, w), self.h_ap(c, off, w), AF.Square, [self.hT_b[c][si]], [self.sq_b[c][si]])

    def pre_stats(self, tl, si):
        off, w = tl["subs"][si]
        bk = self.stats_mm(si, off, w)
        self.rstd_act(self.ps[bk][:, 0:w], [self.ps_b[bk]], si, off, w)

    def pre_apply(self, tl, si, c, gname):
        off, w = tl["subs"][si]
        o, i0, g, r = self.xn_ap(c, off, w), self.h_ap(c, off, w), self.cs(gname, c), self.st[:, off:off + w]
        self.dve((lambda e, o=o, i0=i0, g=g, r=r:
                  e.scalar_tensor_tensor(out=o, in0=i0, scalar=g, in1=r, op0=ALU.mult, op1=ALU.mult)),
                 [self.hT_b[c][si], self.st_b[si], self.const_b], [self.xn_b[c][si]])

    def pre(self, tl, si, gname):
        self.pre_stats(tl, si)
        for c in range(NCH):
            self.pre_apply(tl, si, c, gname)

    def post_apply(self, tl, si, c, gname, square):
        off, w = tl["subs"][si]
        a, g, h, r = self.a_ap(c, off, w), self.cs(gname, c), self.h_ap(c, off, w), self.st[:, off:off + w]
        self.dve((lambda e, a=a, g=g, r=r:
                  e.scalar_tensor_tensor(out=a, in0=a, scalar=g, in1=r, op0=ALU.mult, op1=ALU.mult)),
                 [self.a32_b[c][si], self.st_b[si], self.const_b], [self.a32_b[c][si]])
        self.dve((lambda e, a=a, h=h: e.tensor_tensor(out=h, in0=h, in1=a, op=ALU.add)),
                 [self.a32_b[c][si], self.hT_b[c][si]], [self.hT_b[c][si]])
        if square:
            self.act(self.sq_ap(c, off, w), h, AF.Square, [self.hT_b[c][si]], [self.sq_b[c][si]])

    def post(self, tl, si, gname, square):
        self.pre_stats(tl, si)
        for c in range(NCH):
            self.post_apply(tl, si, c, gname, square)

    def run_first_unit(self, nsub, s1_ops, pre1, blocks):
        if nsub == 1:
            for op in s1_ops:
                op()
            if pre1:
                pre1()
            for b in blocks:
                b(0)
            return
        nb0 = min(6, len(blocks))
        for c, op in enumerate(s1_ops):
            op()
            if c < nb0:
                blocks[c](0)
        for c in range(len(s1_ops), nb0):
            blocks[c](0)
        if pre1:
            pre1()
        for b in blocks[nb0:]:
            b(0)
        for b in blocks:
            b(1)

    def boundary(self, tl, post_g, pre_g, blocks):
        nsub = len(tl["subs"])
        self.post(tl, 0, post_g, True)
        self.pre(tl, 0, pre_g)
        if nsub == 1:
            self.run_first_unit(1, [], None, blocks)
            return
        self.pre_stats(tl, 1)
        s1 = [(lambda c=c: self.post_apply(tl, 1, c, post_g, True)) for c in range(NCH)]
        self.run_first_unit(nsub, s1, (lambda: self.pre(tl, 1, pre_g)), blocks)

    def gemm_blk(self, slot, e0, nk, rhs_fn, tl, evac):
        def blk(si):
            off, w = tl["subs"][si]
            bk = self.next_bank()
            for k in range(nk):
                rap, rb = rhs_fn(k, si, off, w)
                self.mm(self.ps[bk][:, 0:w], self.w_ap(slot, e0 + k * P, P), rap, k == 0, k == nk - 1,
                        reads=[self.w_b[slot]] + rb, writes=[self.ps_b[bk]], inc=(k == nk - 1))
            evac(si, off, w, self.ps[bk][:, 0:w], self.ps_b[bk])
        return blk

    def evac_f(self, mc, bias_ap):
        def ev(si, off, w, pap, pb):
            self.act(self.a_ap(mc, off, w), pap, AF.Identity, [pb], [self.a32_b[mc][si]], bias=bias_ap)
            self.act(self.sq_ap(mc, off, w), pap, AF.Square, [pb], [self.sq_b[mc][si]], bias=bias_ap)
        return ev

    def xr(self, k, si, off, w):
        return self.xn_ap(k, off, w), [self.xn_b[k][si]]

    def in_pair_blocks(self, slot, u, tl):
        blocks = []
        for j in range(4):
            mc = 4 * u + j
            tmps = {}

            def ev_gate(si, off, w, pap, pb, mc=mc, tmps=tmps):
                tap, tb = self.tmp_unit()
                tmps[si] = (tap, tb)
                self.act(tap[:, 0:w], pap, AF.Sigmoid, [pb], [tb], bias=self.cs("b_in_g", mc))

            def ev_val(si, off, w, pap, pb, mc=mc, tmps=tmps):
                tap, tb = tmps[si]
                o, bv = self.uT_ap(mc, UH + off, w), self.cs("b_in_v", mc)
                self.dve((lambda e, o=o, pap=pap, bv=bv, t=tap[:, 0:w]:
                          e.scalar_tensor_tensor(out=o, in0=pap, scalar=bv, in1=t, op0=ALU.add, op1=ALU.mult)),
                         [pb, tb, self.const_b], [self.uT_b[mc]])
            blocks.append(((2 * j + 1) * 1024, ev_gate))
            blocks.append(((2 * j) * 1024, ev_val))
        return blocks

    def mixer0_first_blocks(self, tl):
        u3 = self.r44[:, 0:NCH * self.UTW].rearrange("p (c t) -> p c t", c=NCH)
        h3 = self.uhalo[:, :].rearrange("p (c t) -> p c t", c=NCH)
        self.S.op("pool", (lambda e, o=u3[:, :, 0:UH], i0=h3: e.tensor_copy(out=o, in_=i0)),
                  reads=[self.uhalo_b], writes=list(self.uT_b))
        slot = self.w_get("in", 0)
        return [self.gemm_blk(slot, e0, 8, self.xr, tl, ev) for (e0, ev) in self.in_pair_blocks(slot, 0, tl)]

    def mixer0_rest(self, tl):
        subs = tl["subs"]
        W = tl["W"]
        ns = len(subs)
        u3 = self.r44[:, 0:NCH * self.UTW].rearrange("p (c t) -> p c t", c=NCH)
        h3 = self.uhalo[:, :].rearrange("p (c t) -> p c t", c=NCH)
        slot = self.w_get("in", 1)
        for (e0, ev) in self.in_pair_blocks(slot, 1, tl):
            self.gemm(slot, e0, 8, self.xr, subs, ev)
        if tl["halo"]:
            f = self.cs("flag")
            o = u3[:, :, UH:UH + W]
            self.dve((lambda e, o=o, f=f: e.tensor_scalar(out=o, in0=o, scalar1=f, scalar2=None, op0=ALU.mult)),
                     list(self.uT_b) + [self.const_b], list(self.uT_b))
        self.S.op("pool", (lambda e, o=h3, i0=u3[:, :, W:W + UH]: e.tensor_copy(out=o, in_=i0)),
                  reads=list(self.uT_b), writes=[self.uhalo_b])
        dslot = 0
        for c in range(NCH):
            banks = [self.next_bank() for _ in subs]
            for tg in range((CONVW + DT - 1) // DT):
                taps = list(range(tg * DT, min((tg + 1) * DT, CONVW)))
                nt = len(taps)
                ds = dslot % 2
                dslot += 1
                o3 = self.diag[:, ds * DT * P:(ds * DT + nt) * P].rearrange("p (j m) -> p j m", j=nt)
                i0 = self.ident[:, :].unsqueeze(1).to_broadcast([P, nt, P])
                i1 = self.cs("dw_w", c * CONVW + taps[0], nt).unsqueeze(2).to_broadcast([P, nt, P])
                self.S.op("pool", (lambda e, o3=o3, i0=i0, i1=i1: e.tensor_tensor(out=o3, in0=i0, in1=i1, op=ALU.mult)),
                          reads=[self.const_b], writes=[self.diag_b[ds]])
                for jj, tap in enumerate(taps):
                    lhs = self.diag[:, (ds * DT + jj) * P:(ds * DT + jj + 1) * P]
                    for si, (off, w) in enumerate(subs):
                        rhs = self.uT_ap(c, off + 2 + tap, w)
                        wr = [self.ps_b[b] for b in banks] if (tap == 0 and si == 0) else [self.ps_b[banks[si]]]
                        self.mm(self.ps[banks[si]][:, 0:w], lhs, rhs, tap == 0, tap == CONVW - 1,
                                reads=[self.diag_b[ds], self.uT_b[c]], writes=wr,
                                inc=(jj == nt - 1 and si == len(subs) - 1))
            for si, (off, w) in enumerate(subs):
                pap, pb = self.ps[banks[si]][:, 0:w], self.ps_b[banks[si]]
                b = self.cs("dw_b", c)
                self.act(self.a_ap(c, off, w), pap, AF.Identity, [pb], [self.a32_b[c][si]], bias=b)
                self.act(self.sq_ap(c, off, w), pap, AF.Square, [pb], [self.sq_b[c][si]], bias=b)
                self.act(self.xn_ap(c, off, w), pap, AF.Identity, [pb], [self.xn_b[c][si]], bias=b)

        def ln_stats(si):
            off, w = subs[si]
            for c in range(NCH):
                self.mm(self.ps[6][:, 0:w], self.onesm[:, :], self.xn_ap(c, off, w), c == 0, c == NCH - 1,
                        reads=[self.xn_b[c][si], self.const_b], writes=[self.ps_b[6]], inc=(c == NCH - 1))
            for c in range(NCH):
                self.mm(self.ps[7][:, 0:w], self.onesm[:, :], self.sq_ap(c, off, w), c == 0, c == NCH - 1,
                        reads=[self.sq_b[c][si], self.const_b], writes=[self.ps_b[7]], inc=(c == NCH - 1))
            m = self.st[:, 2 * SW + off: 2 * SW + off + w]
            r = self.st[:, off:off + w]
            mb, rb = self.st_b[2 + si], self.st_b[si]
            self.dve((lambda e, o=m, i=self.ps[6][:, 0:w]: e.tensor_copy(out=o, in_=i)), [self.ps_b[6]], [mb])
            self.dve((lambda e, o=r, i=m: e.tensor_tensor(out=o, in0=i, in1=i, op=ALU.mult)), [mb], [rb])
            self.dve((lambda e, o=r, i=self.ps[7][:, 0:w], ep=self.cs("eps"):
                      e.scalar_tensor_tensor(out=o, in0=i, scalar=ep, in1=o, op0=ALU.add, op1=ALU.subtract)),
                     [self.ps_b[7], rb, self.const_b], [rb])
            self.rstd_act(r, [rb], si, off, w, add_eps=False)

        def ln_apply(si, c):
            off, w = subs[si]
            a = self.a_ap(c, off, w)
            M = self.st[:, 2 * SW + off: 2 * SW + off + w]
            R = self.st[:, off:off + w]
            self.dve((lambda e, a=a, M=M: e.tensor_tensor(out=a, in0=a, in1=M, op=ALU.subtract)),
                     [self.a32_b[c][si], self.st_b[2 + si]], [self.a32_b[c][si]])
            self.dve((lambda e, a=a, R=R: e.tensor_tensor(out=a, in0=a, in1=R, op=ALU.mult)),
                     [self.a32_b[c][si], self.st_b[si]], [self.a32_b[c][si]])
            self.act(self.xn_ap(c, off, w), a, AF.Silu, [self.a32_b[c][si]], [self.xn_b[c][si]],
                     bias=self.cs("ln_b", c), scale=self.cs("ln_g", c))
        ln_stats(0)
        for c in range(NCH):
            ln_apply(0, c)
        slot = self.w_get("out", 0)
        blocks = [self.gemm_blk(slot, mc * 1024, 8, self.xr, tl, self.evac_f(mc, self.cs("b_out", mc)))
                  for mc in range(8)]
        if ns == 1:
            self.run_first_unit(1, [], None, blocks)
        else:
            ln_stats(1)
            self.run_first_unit(ns, [(lambda c=c: ln_apply(1, c)) for c in range(NCH)], None, blocks)

    def gu_blocks(self, slot, u, tl):
        p0, p1 = GU_UNITS[u]
        blocks = []
        for i in range(p0, p1):
            li = i - p0
            tmps = {}

            def ev_gate(si, off, w, pap, pb, tmps=tmps):
                tap, tb = self.tmp_unit()
                tmps[si] = (tap, tb)
                self.act(tap[:, 0:w], pap, AF.Silu, [pb], [tb])

            def ev_up(si, off, w, pap, pb, i=i, tmps=tmps):
                tap, tb = tmps[si]
                o = self.aT_ap(i, off, w)
                self.dve((lambda e, o=o, pap=pap, t=tap[:, 0:w]:
                          e.tensor_tensor(out=o, in0=pap, in1=t, op=ALU.mult)),
                         [pb, tb], [self.aT_b[i][si]])
            blocks.append(((2 * li) * 1024, ev_gate))
            blocks.append(((2 * li + 1) * 1024, ev_up))
        return blocks

    def ffn_first_blocks(self, tl, l):
        slot = self.w_get(f"gu{l}", 0)
        return [self.gemm_blk(slot, e0, 8, self.xr, tl, ev) for (e0, ev) in self.gu_blocks(slot, 0, tl)]

    def store_out(self, tl, si):
        off, w = tl["subs"][si]
        t0 = tl["t0"] - HALO
        dst = self.outT[:, NCH * (t0 + off): NCH * (t0 + off + w)]
        self.S.dma("act", (lambda e, dst=dst, src=self.h_sub(si, w): e.dma_start(out=dst, in_=src)), f"d:o{si}",
                   reads=[self.hT_b[c][si] for c in range(NCH)])

    def ffn_rest(self, tl, l, final=False):
        subs = tl["subs"]
        for u in range(1, len(GU_UNITS)):
            slot = self.w_get(f"gu{l}", u)
            for (e0, ev) in self.gu_blocks(slot, u, tl):
                self.gemm(slot, e0, 8, self.xr, subs, ev)
        ar = lambda k, si, off, w: (self.aT_ap(k, off, w), [self.aT_b[k][si]])
        for u in range(4):
            slot = self.w_get(f"dn{l}", u)
            if final and u == 3:
                blocks = [self.gemm_blk(slot, j * NF * P, NF, ar, tl, self.evac_f(2 * u + j, self.cs("zero")))
                          for j in range(2)]
                for si in range(len(subs)):
                    for b in blocks:
                        b(si)
                    self.post(tl, si, "g_ffn_post1", False)
                    self.store_out(tl, si)
                continue
            for j in range(2):
                mc = 2 * u + j
                self.gemm(slot, j * NF * P, NF, ar, subs, self.evac_f(mc, self.cs("zero")))

    def mixer1_first_blocks(self, tl):
        S = self.S
        for j in range(2):
            o, i0 = self.kT_ap(j, 0, P), self.kprev[:, j * P:(j + 1) * P]
            S.op("pool", (lambda e, o=o, i0=i0: e.tensor_copy(out=o, in_=i0)),
                 reads=[self.kprev_b], writes=[self.kT_b[j][0]])
        o, i0 = self.V_ap(0), self.vprev[:, :]
        S.op("pool", (lambda e, o=o, i0=i0: e.tensor_copy(out=o, in_=i0)),
             reads=[self.vprev_b], writes=[self.V_b[0]])
        if tl["halo"]:
            return []
        slot = self.w_get("q", 0)
        blocks = []
        for cq in range(8):
            def ev_q(si, off, w, pap, pb, cq=cq):
                self.act(self.qT_ap(cq, off, w), pap, AF.Identity, [pb], [self.qT_b[cq][si]],
                         bias=self.cs("b_q", cq))
            blocks.append(self.gemm_blk(slot, cq * 1024, 8, self.xr, tl, ev_q))
        return blocks

    def mixer1_rest(self, tl):
        subs = tl["subs"]
        W = tl["W"]
        nb = W // P
        S = self.S
        slot = self.w_get("kv", 0)
        for j in range(2):
            def ev_k(si, off, w, pap, pb, j=j):
                blks = [self.kT_b[j][1 + (off + x) // P] for x in range(0, w, P)]
                self.act(self.kT_ap(j, P + off, w), pap, AF.Identity, [pb], blks, bias=self.cs("b_k", j))
            self.gemm(slot, j * 1024, 8, self.xr, subs, ev_k)
        for b in range(nb):
            bk = self.next_bank()
            si = (b * P) // SW if not tl["halo"] else 0
            for k in range(NCH):
                self.mm(self.ps[bk][:, 0:256], self.xn_ap(k, b * P, P), self.w_ap(slot, 2048 + k * 256, 256),
                        k == 0, k == NCH - 1, reads=[self.w_b[slot], self.xn_b[k][si]],
                        writes=[self.ps_b[bk]], inc=(k == NCH - 1))
            o, pap, bv = self.V_ap(1 + b), self.ps[bk][:, 0:256], self.cs("b_v", 0, 256)
            self.dve((lambda e, o=o, pap=pap, bv=bv: e.tensor_tensor(out=o, in0=pap, in1=bv, op=ALU.add)),
                     [self.ps_b[bk], self.const_b], [self.V_b[1 + b]])

        if not tl["halo"]:
            iters = [(b, pp) for b in range(nb) for pp in range(2)]
            state = {}

            def stage_a(it):
                b, pp = iters[it]
                si = (b * P) // SW
                pts = []
                for hh in range(2):
                    rows = slice(hh * 64, hh * 64 + 64)
                    for kb in range(2):
                        bank = hh * 2 + kb
                        q3 = self.r44[rows, self.QO + pp * 4 * TT: self.QO + (pp + 1) * 4 * TT] \
                            .rearrange("p (c t) -> p c t", c=4)[:, :, b * P:(b + 1) * P]
                        o3 = self.ps[bank][:, :].rearrange("p (g q) -> p g q", g=4)
                        self.mm(o3, self.kT_ap(pp, (b + kb) * P, P, rows=rows), q3, True, False,
                                reads=[self.kT_b[pp][b + kb]] + [self.qT_b[pp * 4 + g][si] for g in range(4)],
                                writes=[self.ps_b[bank]], inc=False, tp=(hh * 64, 0))
                for hh in range(2):
                    for kb in range(2):
                        bank = hh * 2 + kb
                        h0 = 4 * (2 * pp + hh)
                        o3 = self.ps[bank][:, :].rearrange("p (g q) -> p g q", g=4)
                        b3 = self.bhi[:, :].rearrange("p (h k q) -> p h k q", h=NQH, k=2)[:, h0:h0 + 4, kb, :]
                        self.mm(o3, self.ident[:, :], b3, False, True, reads=[self.const_b],
                                writes=[self.ps_b[bank]], inc=True)
                        pc, psi = (it * 4 + bank) % 8, 1
                        pt = self.a_ap(pc, psi * SW, SW).bitcast(BF16)[:, 0:SW]
                        ptb = self.a32_b[pc][psi]
                        self.act(pt, self.ps[bank][:, :], AF.Exp, [self.ps_b[bank]], [ptb], scale=0.125)
                        if tl["idx"] == 1 and b == 0 and kb == 0:
                            f = self.cs("flag")
                            self.dve((lambda e, pt=pt, f=f:
                                      e.tensor_scalar(out=pt, in0=pt, scalar1=f, scalar2=None, op0=ALU.mult)),
                                     [ptb, self.const_b], [ptb])
                        pts.append((hh, kb, pt, ptb))
                state[it] = pts

            def stage_b(it):
                b, pp = iters[it]
                si = (b * P) // SW
                pts = state.pop(it)
                bo = 4 + 2 * (it % 2)
                bd = bo + 1
                for (hh, kb, pt, ptb) in pts:
                    vcol = (2 * pp + hh) * 64
                    self.mm(self.ps[bo][hh * 64:hh * 64 + 64, :], self.V_ap(b + kb, vcol, 64), pt,
                            kb == 0, kb == 1, reads=[self.V_b[b + kb], ptb], writes=[self.ps_b[bo]],
                            inc=(hh == 1 and kb == 1), tp=(0, hh * 64))
                for (hh, kb, pt, ptb) in pts:
                    self.mm(self.ps[bd][hh * 64:hh * 64 + 64, :], self.ones1[:, 0:64], pt,
                            kb == 0, kb == 1, reads=[ptb, self.const_b], writes=[self.ps_b[bd]],
                            inc=(hh == 1 and kb == 1), tp=(0, hh * 64))
                rc, rcb = self.st[:, (2 + it % 2) * SW:(3 + it % 2) * SW], self.st_b[2 + it % 2]
                for g in range(4):
                    o, i0, sk = rc[:, g * P:(g + 1) * P], self.ps[bd][:, g * P:(g + 1) * P], \
                        self.exps[:, pp * 4 + g: pp * 4 + g + 1]
                    self.dve((lambda e, o=o, i0=i0, sk=sk:
                              e.tensor_scalar(out=o, in0=i0, scalar1=sk, scalar2=None, op0=ALU.add)),
                             [self.ps_b[bd], self.const_b], [rcb])
                self.act(rc, rc, AF.Ln, [rcb], [rcb])
                self.act(rc, rc, AF.Exp, [rcb], [rcb], scale=-1.0)
                o3 = self.r44[:, self.AO + pp * 4 * TT: self.AO + (pp + 1) * 4 * TT] \
                    .rearrange("p (c t) -> p c t", c=4)[:, :, b * P:(b + 1) * P]
                i3 = self.ps[bo][:, :].rearrange("p (g q) -> p g q", g=4)
                r3 = rc.rearrange("p (g q) -> p g q", g=4)
                self.dve((lambda e, o3=o3, i3=i3, r3=r3: e.tensor_tensor(out=o3, in0=i3, in1=r3, op=ALU.mult)),
                         [self.ps_b[bo], rcb], [self.at_b[pp * 4 + g][si] for g in range(4)])

            n = len(iters)
            stage_a(0)
            for it in range(1, n):
                stage_a(it)
                stage_b(it - 1)
            stage_b(n - 1)
        for j in range(2):
            o, i0 = self.kprev[:, j * P:(j + 1) * P], self.kT_ap(j, W, P)
            S.op("pool", (lambda e, o=o, i0=i0: e.tensor_copy(out=o, in_=i0)),
                 reads=[self.kT_b[j][nb]], writes=[self.kprev_b])
        o, i0 = self.vprev[:, :], self.V_ap(nb)
        S.op("pool", (lambda e, o=o, i0=i0: e.tensor_copy(out=o, in_=i0)),
             reads=[self.V_b[nb]], writes=[self.vprev_b])
        if tl["halo"]:
            return
        ar = lambda k, si, off, w: (self.at_ap(k, off, w), [self.at_b[k][si]])
        slot = self.w_get("o", 0)
        for mc in range(8):
            self.gemm(slot, mc * 1024, 8, ar, subs, self.evac_f(mc, self.cs("b_o", mc)))

    def h_sub(self, si, w):
        blk = self.hT[:, si * NCH * SW:(si + 1) * NCH * SW]
        if w == SW:
            return blk
        return blk.rearrange("p (c t) -> p c t", c=NCH)[:, :, 0:w]

    def load_x(self, tl):
        subs, t0 = tl["subs"], tl["t0"]
        for si, (off, w) in enumerate(subs):
            src = self.xT[:, NCH * (t0 + off): NCH * (t0 + off + w)]
            if w != SW:
                src = src.rearrange("p (c t) -> p c t", c=NCH)
            self.S.dma("sp", (lambda e, dst=self.h_sub(si, w), src=src: e.dma_start(out=dst, in_=src)), f"d:x{si}",
                       writes=[self.hT_b[c][si] for c in range(NCH)])

    def layout(self, which):
        if which == "A":
            return list(self.uT_b)
        if which == "B":
            return [b for row in self.aT_b for b in row]
        return [b for row in self.qT_b for b in row] + [b for row in self.at_b for b in row] + \
               [b for row in self.kT_b for b in row] + list(self.V_b)

    def handoff(self, old, new):
        bufs = []
        for k in old + new:
            bufs += self.layout(k)
        self.S.op("pool", (lambda e: e.tensor_copy(out=self.dummy[:, 0:8], in_=self.dummy[:, 8:16])),
                  reads=[], writes=bufs + [self.dummy_b])

    def emit_all(self):
        S = self.S
        nc = self.nc
        S.dma("sp", (lambda e: e.dma_start(out=self.consts[:, :], in_=self.consts_d)), "d:c0", writes=[self.const_b])
        bias_b = Buf("biasld")
        self.load_x(self.tiles()[0])
        self.identf = self.a32[:, 0:P]
        self.maskT = self.a32[:, P:3 * P]
        self.biasT = self.a32[:, 4 * P: 4 * P + NQH * 2 * P]
        alla = [b for row in self.a32_b for b in row]
        S.dma("sp", (lambda e: e.dma_start(out=self.biasT, in_=self.biasT_d)), "d:c1", writes=[bias_b] + alla)
        S.dma("sp", (lambda e: e.dma_start(out=self.maskT, in_=self.maskT_d)), "d:c2", writes=[bias_b] + alla)
        S.op("pool", (lambda e: e.memset(self.onesm[:, :], 1.0 / D)), writes=[self.const_b], reads=[])
        S.op("pool", (lambda e: e.memset(self.ones1[:, :], 1.0)), writes=[self.const_b], reads=[])
        S.op("pool", (lambda e: e.memset(self.uhalo[:, :], 0.0)), writes=[self.uhalo_b])
        S.op("pool", (lambda e: e.memset(self.kprev[:, :], 0.0)), writes=[self.kprev_b])
        S.op("pool", (lambda e: e.memset(self.vprev[:, :], 0.0)), writes=[self.vprev_b])
        S.dma("sp", (lambda e: e.dma_start(out=self.identf, in_=self.ident_d)), "d:c3",
              writes=[self.const_b] + alla)
        S.op("pool", (lambda e: e.tensor_copy(out=self.ident[:, :], in_=self.identf)),
             reads=[self.const_b] + alla, writes=[self.const_b])
        self.act(self.exps[:, :], self.cs("sinks", 0, 8), AF.Exp, [], [self.const_b])
        for h in range(NQH):
            o = self.biasT[:, h * 2 * P:(h + 1) * 2 * P]
            self.dve((lambda e, o=o: e.tensor_tensor(out=o, in0=o, in1=self.maskT, op=ALU.add)),
                     [bias_b] + alla, [bias_b] + alla)
        self.dve((lambda e: e.tensor_scalar(out=self.biasT, in0=self.biasT, scalar1=8.0, scalar2=None, op0=ALU.mult)),
                 [bias_b] + alla, [bias_b] + alla)
        self.dve((lambda e: e.tensor_copy(out=self.bhi[:, :], in_=self.biasT)), [bias_b] + alla, [self.const_b])

        S.op("pool", (lambda e: e.memset(self.dummy[:, :], 0.0)), writes=[self.dummy_b])
        for tl in self.tiles():
            subs, W, t0 = tl["subs"], tl["W"], tl["t0"]
            if tl["idx"] > 0:
                self.load_x(tl)
            nsub = len(subs)
            if tl["idx"] > 0:
                self.handoff(["B", "C"], ["A"])
            for si in range(nsub):
                self.sq_h(tl, si)
            self.pre(tl, 0, "g_mix_pre0")
            blocks = self.mixer0_first_blocks(tl)
            self.run_first_unit(nsub, [], (lambda: self.pre(tl, 1, "g_mix_pre0")) if nsub > 1 else None, blocks)
            self.mixer0_rest(tl)
            self.handoff(["A"], ["B"])
            self.boundary(tl, "g_mix_post0", "g_ffn_pre0", self.ffn_first_blocks(tl, 0))
            self.ffn_rest(tl, 0)
            self.handoff(["B"], ["C"])
            self.boundary(tl, "g_ffn_post0", "g_mix_pre1", self.mixer1_first_blocks(tl))
            self.mixer1_rest(tl)
            if tl["halo"]:
                continue
            self.handoff(["C"], ["B"])
            self.boundary(tl, "g_mix_post1", "g_ffn_pre1", self.ffn_first_blocks(tl, 1))
            self.ffn_rest(tl, 1, final=True)
        assert self.wnext == len(self.plan), (self.wnext, len(self.plan))

    def finalize(self, es):
        nc = self.nc
        S = self.S
        names = list(S.engs.keys()) + sorted(S.dcnt.keys())
        sems = {n: es.enter_context(nc.semaphore(n.replace(":", "_"))) for n in names}
        final_waits = [(n, v) for n, v in S.dcnt.items() if n.startswith("d:o")]

        def replay(e, ops, tail=()):
            for waits, fn, incsem, incval in ops:
                for sname, v in waits[1:]:
                    e.wait_ge(sems[sname], v)
                ins = fn(e)
                if waits:
                    ins._wait_ge(sems[waits[0][0]], waits[0][1])
                if incsem is not None:
                    ins.then_inc(sems[incsem], incval)
            for sname, v in tail:
                e.wait_ge(sems[sname], v)

        with nc.Block() as block:
            @block.tensor
            def _(e):
                replay(e, S.engs["pe"].ops)

            @block.scalar
            def _(e):
                replay(e, S.engs["act"].ops)

            @block.vector
            def _(e):
                replay(e, S.engs["dve"].ops)

            @block.gpsimd
            def _(e):
                replay(e, S.engs["pool"].ops)

            @block.sync
            def _(e):
                replay(e, S.engs["sp"].ops, tail=final_waits)


def _vec8(v):
    return np.ascontiguousarray(np.asarray(v, np.float32).reshape(NCH, P).T)


def _blk(Wm, cols_list):
    K = Wm.shape[0]
    kc = K // P
    out = np.empty((P, len(cols_list), kc, P), np.float32)
    for j, cols in enumerate(cols_list):
        out[:, j] = Wm[:, cols].reshape(kc, P, P).transpose(1, 0, 2)
    return out.reshape(P, -1)


def _t5_bucket(dist):
    dist = np.maximum(dist, 0)
    max_exact = 16
    large = max_exact + (np.log(np.maximum(dist, 1).astype(np.float32) / np.float32(max_exact))
                         / np.float32(np.log(128.0 / max_exact)) * np.float32(32 - max_exact)).astype(np.int32)
    large = np.minimum(large, 31)
    return np.where(dist < max_exact, dist, large)


def _qhead(cq, half):
    pp, g = cq // 4, cq % 4
    return 4 * (2 * pp + half) + g


def _prep_shared(inp):
    f = lambda k: np.asarray(inp[k], np.float32)
    consts = np.zeros((P, NCONST), np.float32)

    def put(name, arr):
        arr = np.asarray(arr, np.float32)
        consts[:, _CL[name]:_CL[name] + arr.shape[1]] = arr
    for l in range(2):
        put(f"g_mix_pre{l}", _vec8(f("mix_pre_g")[l]))
        put(f"g_mix_post{l}", _vec8(f("mix_post_g")[l]))
        put(f"g_ffn_pre{l}", _vec8(f("ffn_pre_g")[l]))
        put(f"g_ffn_post{l}", _vec8(f("ffn_post_g")[l]))
    b_in = f("conv_b_in")[0]
    put("b_in_v", _vec8(b_in[:D]))
    put("b_in_g", _vec8(b_in[D:]))
    put("dw_b", _vec8(f("conv_dw_b")[0]))
    put("ln_g", _vec8(f("conv_ln_g")[0]))
    put("ln_b", _vec8(f("conv_ln_b")[0]))
    put("b_out", _vec8(f("conv_b_out")[0]))
    dww = f("conv_dw_w")[0]
    put("dw_w", dww.T.reshape(NCH, P, CONVW).transpose(1, 0, 2).reshape(P, NCH * CONVW))
    bqkv = f("attn_b_qkv")[0]
    qcols = [np.concatenate([_qhead(cq, 0) * 64 + np.arange(64), _qhead(cq, 1) * 64 + np.arange(64)])
             for cq in range(8)]
    put("b_q", np.stack([bqkv[c] for c in qcols], axis=1))
    put("b_k", np.stack([bqkv[D + j * P: D + (j + 1) * P] for j in range(2)], axis=1))
    put("b_o", _vec8(f("attn_b_o")[0]))
    sinks = f("attn_sinks")[0]
    sk = np.zeros((P, 8), np.float32)
    for cq in range(8):
        sk[:64, cq] = sinks[_qhead(cq, 0)]
        sk[64:, cq] = sinks[_qhead(cq, 1)]
    put("sinks", sk)
    consts[:, _CL["eps"]] = EPS
    put("b_v", np.broadcast_to(bqkv[D + 256: D + 512][None, :], (P, 256)))

    s_i = np.arange(P)[:, None]
    q_i = np.arange(P)[None, :]
    dist = np.stack([q_i + P - s_i, q_i - s_i], axis=0)
    valid = (dist >= 0) & (dist < P)
    bucket = _t5_bucket(dist)
    rel = f("rel_bias")
    biasT = rel[bucket]
    biasT = np.ascontiguousarray(biasT.transpose(1, 3, 0, 2)).reshape(P, NQH * 2 * P)
    maskT = np.where(valid, np.float32(0.0), np.float32(NEG)).astype(np.float32)
    maskT = np.ascontiguousarray(maskT.transpose(1, 0, 2)).reshape(P, 2 * P)

    ar = np.arange(P)
    w_in = f("conv_w_in")[0]
    w_out = f("conv_w_out")[0]
    wqkv = f("attn_w_qkv")[0]
    wo = f("attn_w_o")[0]
    orow = np.concatenate(qcols)
    wo_p = wo[orow, :]
    in_units = []
    for u in range(2):
        cl = []
        for j in range(4):
            mc = 4 * u + j
            cl += [mc * P + ar, D + mc * P + ar]
        in_units.append(_blk(w_in, cl))
    wv_blk = np.ascontiguousarray(wqkv[:, D + 256: D + 512].reshape(NCH, P, 256).transpose(1, 0, 2)).reshape(P, 2048)
    sh = {
        "consts": consts, "biasT": biasT, "maskT": maskT, "ident": np.eye(P, dtype=np.float32),
        "w_in": np.concatenate(in_units, 0),
        "w_out": _blk(w_out, [mc * P + ar for mc in range(8)]),
        "w_q": _blk(wqkv, qcols),
        "w_kv": np.concatenate([_blk(wqkv, [D + ar, D + P + ar]), wv_blk], axis=1),
        "w_o": _blk(wo_p, [mc * P + ar for mc in range(8)]),
    }
    for l in range(2):
        wgu = f("ffn_w_gate_up")[l]
        wdn = f("ffn_w_down")[l]
        gus = []
        for (p0, p1) in GU_UNITS:
            cl = []
            for i in range(p0, p1):
                cl += [i * P + ar, DFF + i * P + ar]
            blk = np.zeros((P, 8192), np.float32)
            blk[:, :len(cl) * 1024] = _blk(wgu, cl)
            gus.append(blk)
        sh[f"w_gu{l}"] = np.concatenate(gus, 0)
        sh[f"w_dn{l}"] = np.concatenate([_blk(wdn, [2 * u * P + ar, (2 * u + 1) * P + ar]) for u in range(4)], 0)
    return sh


_PROG_CACHE = {}


def kernel(**inputs):
    x = np.asarray(inputs["x"], np.float32)
    sh = _prep_shared(inputs)
    in_maps = []
    for core in range(NCORES):
        b, half = core // 2, core % 2
        start = half * TOK
        xl = np.zeros((TLOC, D), np.float32)
        xl[HALO:] = x[b, start:start + TOK]
        if half == 1:
            xl[:HALO] = x[b, start - HALO:start]
        x3 = xl.T.reshape(NCH, P, TLOC).transpose(1, 0, 2)
        xT = np.concatenate([x3[:, :, tl["t0"] + off: tl["t0"] + off + w].reshape(P, -1)
                             for tl in Prog.tiles() for (off, w) in tl["subs"]], axis=1)
        xT = np.ascontiguousarray(xT)
        m = dict(sh)
        c = sh["consts"].copy()
        c[:, _CL["flag"]] = 1.0 if half == 1 else 0.0
        m["consts"] = c
        m["xT"] = xT
        in_maps.append(m)
    if "nc" not in _PROG_CACHE:
        _PROG_CACHE["nc"] = Prog().build()
    res = run_bass_kernel_spmd(_PROG_CACHE["nc"], in_maps, core_ids=list(range(NCORES)))
    out = np.empty((BATCH, SEQ, D), np.float32)
    for core in range(NCORES):
        b, half = core // 2, core % 2
        oT = np.asarray(res.results[core]["outT"]).reshape(P, TOK // SW, NCH, SW)
        out[b, half * TOK:(half + 1) * TOK] = oT.transpose(1, 3, 2, 0).reshape(TOK, D)
    return out
```

```python
import numpy as np
from contextlib import ExitStack
import concourse.bass as bass
import concourse.mybir as mybir
from concourse.bass_utils import run_bass_kernel_spmd

F32 = mybir.dt.float32
BF16 = mybir.dt.bfloat16
AF = mybir.ActivationFunctionType
ALU = mybir.AluOpType

P = 128
D = 1024
NCH = 8
DFF = 2816
NF = 22
SEQ = 8192
BATCH = 4
NCORES = 8
TOK = 4096
HALO = 256
TLOC = TOK + HALO
TT = 1024
SW = 512
CONVW = 31
UH = 32
DT = 16
NQH = 16
EPS = 1e-6
NEG = -30000.0
NSLOT = 2
SLOT_E = 8192

GU_UNITS = [(0, 4), (4, 8), (8, 12), (12, 16), (16, 19), (19, 22)]

_CL = {}
_off = 0
for _n, _w in [("g_mix_pre0", 8), ("g_mix_post0", 8), ("g_ffn_pre0", 8), ("g_ffn_post0", 8),
               ("g_mix_pre1", 8), ("g_mix_post1", 8), ("g_ffn_pre1", 8), ("g_ffn_post1", 8),
               ("b_in_v", 8), ("b_in_g", 8), ("dw_b", 8), ("ln_g", 8), ("ln_b", 8), ("b_out", 8),
               ("dw_w", 8 * CONVW), ("b_q", 8), ("b_k", 2), ("b_o", 8), ("flag", 1), ("sinks", 8),
               ("eps", 1), ("zero", 1), ("b_v", 256)]:
    _CL[_n] = _off
    _off += _w
NCONST = _off


class Buf:
    __slots__ = ("name", "w", "r")

    def __init__(self, name):
        self.name = name
        self.w = None
        self.r = {}


class _Eng:
    def __init__(self, name):
        self.name = name
        self.cnt = 0
        self.ops = []
        self.waited = {}


class Sched:
    def __init__(self):
        self.engs = {k: _Eng(k) for k in ("pe", "act", "dve", "pool", "sp")}
        self.dcnt = {}

    def _waits(self, e, reads, writes):
        deps = {}
        for b in reads:
            if b.w is not None and deps.get(b.w[0], 0) < b.w[1]:
                deps[b.w[0]] = b.w[1]
        for b in writes:
            if b.w is not None and deps.get(b.w[0], 0) < b.w[1]:
                deps[b.w[0]] = b.w[1]
            for s, v in b.r.items():
                if deps.get(s, 0) < v:
                    deps[s] = v
        waits = []
        for s, v in deps.items():
            if e.name == "pe" and s == "pe":
                continue
            if e.waited.get(s, 0) >= v:
                continue
            if s == e.name:
                assert v <= e.cnt, (e.name, v, e.cnt)
            e.waited[s] = v
            waits.append((s, v))
        return waits

    def op(self, eng, fn, reads=(), writes=(), inc=True):
        e = self.engs[eng]
        assert inc or eng == "pe"
        waits = self._waits(e, reads, writes)
        ev = (eng, e.cnt + 1)
        e.ops.append((waits, fn, eng if inc else None, 1))
        if inc:
            e.cnt += 1
        for b in reads:
            if b.r.get(eng, 0) < ev[1]:
                b.r[eng] = ev[1]
        for b in writes:
            b.w = ev
            b.r = {}

    def dma(self, eng, fn, dsem, reads=(), writes=()):
        e = self.engs[eng]
        waits = self._waits(e, reads, writes)
        self.dcnt[dsem] = self.dcnt.get(dsem, 0) + 16
        ev = (dsem, self.dcnt[dsem])
        e.ops.append((waits, fn, dsem, 16))
        for b in reads:
            if b.r.get(dsem, 0) < ev[1]:
                b.r[dsem] = ev[1]
        for b in writes:
            b.w = ev
            b.r = {}


class Prog:
    def __init__(self):
        self.nc = bass.Bass("TRN2", target_bir_lowering=False)
        self.S = Sched()
        self.bank_rr = 0
        self.tmp_rr = 0
        self.wnext = 0
        self.wissued = 0

    @staticmethod
    def tiles():
        t = [dict(idx=0, t0=0, W=HALO, subs=[(0, HALO)], halo=True)]
        for i in range(TOK // TT):
            t.append(dict(idx=i + 1, t0=HALO + i * TT, W=TT,
                          subs=[(s * SW, SW) for s in range(TT // SW)], halo=False))
        return t

    def weight_plan(self):
        plan = []
        for tl in self.tiles():
            plan += [("in", 0), ("in", 1), ("out", 0)]
            plan += [("gu0", u) for u in range(len(GU_UNITS))]
            plan += [("dn0", u) for u in range(4)]
            if tl["halo"]:
                plan.append(("kv", 0))
                continue
            plan += [("q", 0), ("kv", 0), ("o", 0)]
            plan += [("gu1", u) for u in range(len(GU_UNITS))]
            plan += [("dn1", u) for u in range(4)]
        return plan

    def unit_E(self, kind, idx):
        if kind.startswith("gu"):
            a, b = GU_UNITS[idx]
            return (b - a) * 2048
        return {"in": 8192, "out": 8192, "dn0": 5632, "dn1": 5632, "q": 8192, "kv": 4096, "o": 8192}[kind]

    def build(self):
        nc = self.nc
        S = self.S
        es = ExitStack()
        with es:
            def dram(name, shape, dt=F32, kind="ExternalInput"):
                return nc.dram_tensor(name, shape, dt, kind=kind).ap()

            self.xT = dram("xT", [P, NCH * TLOC])
            self.outT = dram("outT", [P, NCH * TOK], kind="ExternalOutput")
            self.consts_d = dram("consts", [P, NCONST])
            self.biasT_d = dram("biasT", [P, NQH * 2 * P])
            self.maskT_d = dram("maskT", [P, 2 * P])
            self.ident_d = dram("ident", [P, P])
            self.wd = {
                "in": dram("w_in", [2 * P, 8192]), "out": dram("w_out", [P, 8192]),
                "gu0": dram("w_gu0", [6 * P, 8192]), "gu1": dram("w_gu1", [6 * P, 8192]),
                "dn0": dram("w_dn0", [4 * P, 5632]), "dn1": dram("w_dn1", [4 * P, 5632]),
                "q": dram("w_q", [P, 8192]), "kv": dram("w_kv", [P, 4096]), "o": dram("w_o", [P, 8192]),
            }

            def sb(name, shape, dt):
                return es.enter_context(nc.sbuf_tensor(name, shape, dt))

            self.hT = sb("hT", [P, NCH * TT], F32)
            self.xn = sb("xn", [P, NCH * TT], BF16)
            self.sq = sb("sq", [P, NCH * TT], BF16)
            self.a32 = sb("a32", [P, NCH * TT], F32)
            self.r44 = sb("r44", [P, NF * TT], BF16)
            self.uhalo = sb("uhalo", [P, NCH * UH], BF16)
            self.kprev = sb("kprev", [P, 2 * P], BF16)
            self.vprev = sb("vprev", [P, 256], BF16)
            self.wring = sb("wring", [P, NSLOT * SLOT_E], BF16)
            self.diag = sb("diag", [P, 2 * DT * P], BF16)
            self.ident = sb("identb", [P, P], BF16)
            self.onesm = sb("onesm", [P, P], BF16)
            self.ones1 = sb("ones1", [P, P], BF16)
            self.consts = sb("constsb", [P, NCONST], F32)
            self.exps = sb("exps", [P, 8], F32)
            self.bhi = sb("bhi", [P, NQH * 2 * P], BF16)
            self.st = sb("st", [P, 4 * SW], F32)
            self.dummy = sb("dummyt", [P, 16], F32)
            self.dummy_b = Buf("dummy")
            self.ps = [es.enter_context(nc.psum_tensor(f"ps{i}", [P, SW], F32)) for i in range(8)]

            nsub = TT // SW
            self.hT_b = [[Buf(f"hT{c}_{s}") for s in range(nsub)] for c in range(NCH)]
            self.xn_b = [[Buf(f"xn{c}_{s}") for s in range(nsub)] for c in range(NCH)]
            self.sq_b = [[Buf(f"sq{c}_{s}") for s in range(nsub)] for c in range(NCH)]
            self.a32_b = [[Buf(f"a32{c}_{s}") for s in range(nsub)] for c in range(NCH)]
            self.r44_b = Buf("r44")
            self.aT_b = [[Buf(f"aT{i}_{s}") for s in range(nsub)] for i in range(NF)]
            self.uT_b = [Buf(f"uT{c}") for c in range(NCH)]
            self.qT_b = [[Buf(f"qT{c}_{s}") for s in range(nsub)] for c in range(NCH)]
            self.at_b = [[Buf(f"at{c}_{s}") for s in range(nsub)] for c in range(NCH)]
            self.kT_b = [[Buf(f"kT{j}_{b}") for b in range(TT // P + 1)] for j in range(2)]
            self.V_b = [Buf(f"V{b}") for b in range(TT // P + 1)]
            self.uhalo_b = Buf("uhalo")
            self.kprev_b = Buf("kprev")
            self.vprev_b = Buf("vprev")
            self.w_b = [Buf(f"w{i}") for i in range(NSLOT)]
            self.diag_b = [Buf("diag0"), Buf("diag1")]
            self.const_b = Buf("const")
            self.st_b = [Buf(f"st{i}") for i in range(4)]
            self.ps_b = [Buf(f"ps{i}") for i in range(8)]
            self.plan = self.weight_plan()

            self.emit_all()
            self.finalize(es)
        return nc

    def cs(self, name, c=0, n=1):
        o = _CL[name] + c
        return self.consts[:, o:o + n]

    def h_ap(self, c, off, w):
        si, o = off // SW, off % SW
        assert o + w <= SW
        base = si * NCH * SW + c * SW + o
        return self.hT[:, base: base + w]

    def xn_ap(self, c, off, w):
        return self.xn[:, c * TT + off: c * TT + off + w]

    def sq_ap(self, c, off, w):
        return self.sq[:, c * TT + off: c * TT + off + w]

    def a_ap(self, c, off, w):
        return self.a32[:, c * TT + off: c * TT + off + w]

    def aT_ap(self, i, off, w):
        return self.r44[:, i * TT + off: i * TT + off + w]

    UTW = UH + TT

    def uT_ap(self, c, col, w):
        return self.r44[:, c * self.UTW + col: c * self.UTW + col + w]

    QO = 0
    AO = 8 * TT
    KO = 16 * TT
    KW = P + TT
    VO = 16 * TT + 2 * (P + TT)

    def qT_ap(self, c, off, w, rows=slice(0, P)):
        return self.r44[rows, self.QO + c * TT + off: self.QO + c * TT + off + w]

    def at_ap(self, c, off, w):
        return self.r44[:, self.AO + c * TT + off: self.AO + c * TT + off + w]

    def kT_ap(self, j, col, w, rows=slice(0, P)):
        return self.r44[rows, self.KO + j * self.KW + col: self.KO + j * self.KW + col + w]

    def V_ap(self, blk, c0=0, w=256):
        return self.r44[:, self.VO + blk * 256 + c0: self.VO + blk * 256 + c0 + w]

    def w_ap(self, slot, e0, w):
        return self.wring[:, slot * SLOT_E + e0: slot * SLOT_E + e0 + w]

    def tmp_unit(self):
        c = self.tmp_rr % 8
        self.tmp_rr += 1
        return self.a_ap(c, 0, SW), self.a32_b[c][0]

    def next_bank(self):
        b = self.bank_rr % 6
        self.bank_rr += 1
        return b

    def w_issue_upto(self, n):
        S = self.S
        while self.wissued <= min(n, len(self.plan) - 1):
            i = self.wissued
            kind, idx = self.plan[i]
            slot = i % NSLOT
            E = self.unit_E(kind, idx)
            src = self.wd[kind][idx * P:(idx + 1) * P, 0:E]
            dst = self.w_ap(slot, 0, E)
            S.dma("pool", (lambda e, dst=dst, src=src: e.dma_start(out=dst, in_=src)), f"d:w{slot}",
                  writes=[self.w_b[slot]])
            self.wissued += 1

    def w_get(self, kind, idx):
        i = self.wnext
        assert self.plan[i] == (kind, idx), (self.plan[i], kind, idx)
        self.w_issue_upto(i + NSLOT - 1)
        self.wnext += 1
        return i % NSLOT

    def mm(self, out, lhsT, rhs, start, stop, reads, writes, inc, tp=None):
        if tp is None:
            fn = (lambda e, out=out, lhsT=lhsT, rhs=rhs, start=start, stop=stop:
                  e.matmul(out, lhsT=lhsT, rhs=rhs, start=start, stop=stop))
        else:
            fn = (lambda e, out=out, lhsT=lhsT, rhs=rhs, start=start, stop=stop, tp=tp:
                  e.matmul(out, lhsT=lhsT, rhs=rhs, start=start, stop=stop, tile_position=tp))
        self.S.op("pe", fn, reads=reads, writes=writes, inc=inc)

    def act(self, out, in_, func, reads, writes, bias=None, scale=1.0):
        if bias is None:
            bias = self.cs("zero")
        self.S.op("act", (lambda e, out=out, in_=in_, func=func, bias=bias, scale=scale:
                          e.activation(out=out, in_=in_, func=func, bias=bias, scale=scale)),
                  reads=list(reads) + [self.const_b], writes=writes)

    def dve(self, fn, reads, writes):
        self.S.op("dve", fn, reads=reads, writes=writes)

    def gemm(self, slot, e0, nk, rhs_fn, subs, evac):
        banks = [self.next_bank() for _ in subs]
        for k in range(nk):
            for si, (off, w) in enumerate(subs):
                rap, rb = rhs_fn(k, si, off, w)
                lhs = self.w_ap(slot, e0 + k * P, P)
                wr = [self.ps_b[b] for b in banks] if (k == 0 and si == 0) else [self.ps_b[banks[si]]]
                self.mm(self.ps[banks[si]][:, 0:w], lhs, rap, k == 0, k == nk - 1,
                        reads=[self.w_b[slot]] + rb, writes=wr, inc=(k == nk - 1 and si == len(subs) - 1))
        for si, (off, w) in enumerate(subs):
            evac(si, off, w, self.ps[banks[si]][:, 0:w], self.ps_b[banks[si]])

    def stats_mm(self, si, off, w):
        bk = 6 + (si % 2)
        for c in range(NCH):
            self.mm(self.ps[bk][:, 0:w], self.onesm[:, :], self.sq_ap(c, off, w), c == 0, c == NCH - 1,
                    reads=[self.sq_b[c][si], self.const_b], writes=[self.ps_b[bk]], inc=(c == NCH - 1))
        return bk

    def rstd_act(self, src, src_b, si, off, w, add_eps=True):
        r = self.st[:, off:off + w]
        self.act(r, src, AF.Ln, src_b, [self.st_b[si]], bias=self.cs("eps") if add_eps else None)
        self.act(r, r, AF.Exp, [self.st_b[si]], [self.st_b[si]], scale=-0.5)
        return r

    def sq_h(self, tl, si):
        off, w = tl["subs"][si]
        for c in range(NCH):
            self.act(self.sq_ap(c, off, w), self.h_ap(c, off, w), AF.Square, [self.hT_b[c][si]], [self.sq_b[c][si]])

    def pre_stats(self, tl, si):
        off, w = tl["subs"][si]
        bk = self.stats_mm(si, off, w)
        self.rstd_act(self.ps[bk][:, 0:w], [self.ps_b[bk]], si, off, w)

    def pre_apply(self, tl, si, c, gname):
        off, w = tl["subs"][si]
        o, i0, g, r = self.xn_ap(c, off, w), self.h_ap(c, off, w), self.cs(gname, c), self.st[:, off:off + w]
        self.dve((lambda e, o=o, i0=i0, g=g, r=r:
                  e.scalar_tensor_tensor(out=o, in0=i0, scalar=g, in1=r, op0=ALU.mult, op1=ALU.mult)),
                 [self.hT_b[c][si], self.st_b[si], self.const_b], [self.xn_b[c][si]])

    def pre(self, tl, si, gname):
        self.pre_stats(tl, si)
        for c in range(NCH):
            self.pre_apply(tl, si, c, gname)

    def post_apply(self, tl, si, c, gname, square):
        off, w = tl["subs"][si]
        a, g, h, r = self.a_ap(c, off, w), self.cs(gname, c), self.h_ap(c, off, w), self.st[:, off:off + w]
        self.dve((lambda e, a=a, g=g, r=r:
                  e.scalar_tensor_tensor(out=a, in0=a, scalar=g, in1=r, op0=ALU.mult, op1=ALU.mult)),
                 [self.a32_b[c][si], self.st_b[si], self.const_b], [self.a32_b[c][si]])
        self.dve((lambda e, a=a, h=h: e.tensor_tensor(out=h, in0=h, in1=a, op=ALU.add)),
                 [self.a32_b[c][si], self.hT_b[c][si]], [self.hT_b[c][si]])
        if square:
            self.act(self.sq_ap(c, off, w), h, AF.Square, [self.hT_b[c][si]], [self.sq_b[c][si]])

    def post(self, tl, si, gname, square):
        self.pre_stats(tl, si)
        for c in range(NCH):
            self.post_apply(tl, si, c, gname, square)

    def run_first_unit(self, nsub, s1_ops, pre1, blocks, mid=None):
        if nsub == 1:
            for op in s1_ops:
                op()
            if pre1:
                pre1()
            for b in blocks:
                b(0)
            if mid:
                mid()
            return
        nb0 = min(6, len(blocks))
        for c, op in enumerate(s1_ops):
            op()
            if c < nb0:
                blocks[c](0)
        for c in range(len(s1_ops), nb0):
            blocks[c](0)
        if pre1:
            pre1()
        for b in blocks[nb0:]:
            b(0)
        if mid:
            mid()
        for b in blocks:
            b(1)

    def boundary(self, tl, post_g, pre_g, blocks, post0_done=False):
        nsub = len(tl["subs"])
        if not post0_done:
            self.post(tl, 0, post_g, True)
        self.pre(tl, 0, pre_g)
        if nsub == 1:
            self.run_first_unit(1, [], None, blocks)
            return
        self.pre_stats(tl, 1)
        s1 = [(lambda c=c: self.post_apply(tl, 1, c, post_g, True)) for c in range(NCH)]
        self.run_first_unit(nsub, s1, (lambda: self.pre(tl, 1, pre_g)), blocks)

    def gemm_blk(self, slot, e0, nk, rhs_fn, tl, evac):
        def blk(si):
            off, w = tl["subs"][si]
            bk = self.next_bank()
            for k in range(nk):
                rap, rb = rhs_fn(k, si, off, w)
                self.mm(self.ps[bk][:, 0:w], self.w_ap(slot, e0 + k * P, P), rap, k == 0, k == nk - 1,
                        reads=[self.w_b[slot]] + rb, writes=[self.ps_b[bk]], inc=(k == nk - 1))
            evac(si, off, w, self.ps[bk][:, 0:w], self.ps_b[bk])
        return blk

    def evac_f(self, mc, bias_ap):
        def ev(si, off, w, pap, pb):
            self.act(self.a_ap(mc, off, w), pap, AF.Identity, [pb], [self.a32_b[mc][si]], bias=bias_ap)
            self.act(self.sq_ap(mc, off, w), pap, AF.Square, [pb], [self.sq_b[mc][si]], bias=bias_ap)
        return ev

    def xr(self, k, si, off, w):
        return self.xn_ap(k, off, w), [self.xn_b[k][si]]

    def in_pair_blocks(self, slot, u, tl):
        blocks = []
        for j in range(4):
            mc = 4 * u + j
            tmps = {}

            def ev_gate(si, off, w, pap, pb, mc=mc, tmps=tmps):
                tap, tb = self.tmp_unit()
                tmps[si] = (tap, tb)
                self.act(tap[:, 0:w], pap, AF.Sigmoid, [pb], [tb], bias=self.cs("b_in_g", mc))

            def ev_val(si, off, w, pap, pb, mc=mc, tmps=tmps):
                tap, tb = tmps[si]
                o, bv = self.uT_ap(mc, UH + off, w), self.cs("b_in_v", mc)
                self.dve((lambda e, o=o, pap=pap, bv=bv, t=tap[:, 0:w]:
                          e.scalar_tensor_tensor(out=o, in0=pap, scalar=bv, in1=t, op0=ALU.add, op1=ALU.mult)),
                         [pb, tb, self.const_b], [self.uT_b[mc]])
            blocks.append(((2 * j + 1) * 1024, ev_gate))
            blocks.append(((2 * j) * 1024, ev_val))
        return blocks

    def mixer0_first_blocks(self, tl):
        u3 = self.r44[:, 0:NCH * self.UTW].rearrange("p (c t) -> p c t", c=NCH)
        h3 = self.uhalo[:, :].rearrange("p (c t) -> p c t", c=NCH)
        self.S.op("pool", (lambda e, o=u3[:, :, 0:UH], i0=h3: e.tensor_copy(out=o, in_=i0)),
                  reads=[self.uhalo_b], writes=list(self.uT_b))
        slot = self.w_get("in", 0)
        return [self.gemm_blk(slot, e0, 8, self.xr, tl, ev) for (e0, ev) in self.in_pair_blocks(slot, 0, tl)]

    def mixer0_rest(self, tl):
        subs = tl["subs"]
        W = tl["W"]
        ns = len(subs)
        u3 = self.r44[:, 0:NCH * self.UTW].rearrange("p (c t) -> p c t", c=NCH)
        h3 = self.uhalo[:, :].rearrange("p (c t) -> p c t", c=NCH)
        slot = self.w_get("in", 1)
        for (e0, ev) in self.in_pair_blocks(slot, 1, tl):
            self.gemm(slot, e0, 8, self.xr, subs, ev)
        if tl["halo"]:
            f = self.cs("flag")
            o = u3[:, :, UH:UH + W]
            self.dve((lambda e, o=o, f=f: e.tensor_scalar(out=o, in0=o, scalar1=f, scalar2=None, op0=ALU.mult)),
                     list(self.uT_b) + [self.const_b], list(self.uT_b))
        self.S.op("pool", (lambda e, o=h3, i0=u3[:, :, W:W + UH]: e.tensor_copy(out=o, in_=i0)),
                  reads=list(self.uT_b), writes=[self.uhalo_b])
        dslot = 0
        for c in range(NCH):
            banks = [self.next_bank() for _ in subs]
            for tg in range((CONVW + DT - 1) // DT):
                taps = list(range(tg * DT, min((tg + 1) * DT, CONVW)))
                nt = len(taps)
                ds = dslot % 2
                dslot += 1
                o3 = self.diag[:, ds * DT * P:(ds * DT + nt) * P].rearrange("p (j m) -> p j m", j=nt)
                i0 = self.ident[:, :].unsqueeze(1).to_broadcast([P, nt, P])
                i1 = self.cs("dw_w", c * CONVW + taps[0], nt).unsqueeze(2).to_broadcast([P, nt, P])
                self.S.op("pool", (lambda e, o3=o3, i0=i0, i1=i1: e.tensor_tensor(out=o3, in0=i0, in1=i1, op=ALU.mult)),
                          reads=[self.const_b], writes=[self.diag_b[ds]])
                for jj, tap in enumerate(taps):
                    lhs = self.diag[:, (ds * DT + jj) * P:(ds * DT + jj + 1) * P]
                    for si, (off, w) in enumerate(subs):
                        rhs = self.uT_ap(c, off + 2 + tap, w)
                        wr = [self.ps_b[b] for b in banks] if (tap == 0 and si == 0) else [self.ps_b[banks[si]]]
                        self.mm(self.ps[banks[si]][:, 0:w], lhs, rhs, tap == 0, tap == CONVW - 1,
                                reads=[self.diag_b[ds], self.uT_b[c]], writes=wr,
                                inc=(jj == nt - 1 and si == len(subs) - 1))
            for si, (off, w) in enumerate(subs):
                pap, pb = self.ps[banks[si]][:, 0:w], self.ps_b[banks[si]]
                b = self.cs("dw_b", c)
                self.act(self.a_ap(c, off, w), pap, AF.Identity, [pb], [self.a32_b[c][si]], bias=b)
                self.act(self.sq_ap(c, off, w), pap, AF.Square, [pb], [self.sq_b[c][si]], bias=b)
                self.act(self.xn_ap(c, off, w), pap, AF.Identity, [pb], [self.xn_b[c][si]], bias=b)

        def ln_stats(si):
            off, w = subs[si]
            for c in range(NCH):
                self.mm(self.ps[6][:, 0:w], self.onesm[:, :], self.xn_ap(c, off, w), c == 0, c == NCH - 1,
                        reads=[self.xn_b[c][si], self.const_b], writes=[self.ps_b[6]], inc=(c == NCH - 1))
            for c in range(NCH):
                self.mm(self.ps[7][:, 0:w], self.onesm[:, :], self.sq_ap(c, off, w), c == 0, c == NCH - 1,
                        reads=[self.sq_b[c][si], self.const_b], writes=[self.ps_b[7]], inc=(c == NCH - 1))
            m = self.st[:, 2 * SW + off: 2 * SW + off + w]
            r = self.st[:, off:off + w]
            mb, rb = self.st_b[2 + si], self.st_b[si]
            self.dve((lambda e, o=m, i=self.ps[6][:, 0:w]: e.tensor_copy(out=o, in_=i)), [self.ps_b[6]], [mb])
            self.dve((lambda e, o=r, i=m: e.tensor_tensor(out=o, in0=i, in1=i, op=ALU.mult)), [mb], [rb])
            self.dve((lambda e, o=r, i=self.ps[7][:, 0:w], ep=self.cs("eps"):
                      e.scalar_tensor_tensor(out=o, in0=i, scalar=ep, in1=o, op0=ALU.add, op1=ALU.subtract)),
                     [self.ps_b[7], rb, self.const_b], [rb])
            self.rstd_act(r, [rb], si, off, w, add_eps=False)

        def ln_apply(si, c):
            off, w = subs[si]
            a = self.a_ap(c, off, w)
            M = self.st[:, 2 * SW + off: 2 * SW + off + w]
            R = self.st[:, off:off + w]
            self.dve((lambda e, a=a, M=M: e.tensor_tensor(out=a, in0=a, in1=M, op=ALU.subtract)),
                     [self.a32_b[c][si], self.st_b[2 + si]], [self.a32_b[c][si]])
            self.dve((lambda e, a=a, R=R: e.tensor_tensor(out=a, in0=a, in1=R, op=ALU.mult)),
                     [self.a32_b[c][si], self.st_b[si]], [self.a32_b[c][si]])
            self.act(self.xn_ap(c, off, w), a, AF.Silu, [self.a32_b[c][si]], [self.xn_b[c][si]],
                     bias=self.cs("ln_b", c), scale=self.cs("ln_g", c))
        ln_stats(0)
        for c in range(NCH):
            ln_apply(0, c)
        slot = self.w_get("out", 0)
        blocks = [self.gemm_blk(slot, mc * 1024, 8, self.xr, tl, self.evac_f(mc, self.cs("b_out", mc)))
                  for mc in range(8)]
        mid = (lambda: self.post(tl, 0, "g_mix_post0", True))
        if ns == 1:
            self.run_first_unit(1, [], None, blocks, mid=mid)
        else:
            ln_stats(1)
            self.run_first_unit(ns, [(lambda c=c: ln_apply(1, c)) for c in range(NCH)], None, blocks, mid=mid)

    def gu_blocks(self, slot, u, tl):
        p0, p1 = GU_UNITS[u]
        blocks = []
        for i in range(p0, p1):
            li = i - p0
            tmps = {}

            def ev_gate(si, off, w, pap, pb, tmps=tmps):
                tap, tb = self.tmp_unit()
                tmps[si] = (tap, tb)
                self.act(tap[:, 0:w], pap, AF.Silu, [pb], [tb])

            def ev_up(si, off, w, pap, pb, i=i, tmps=tmps):
                tap, tb = tmps[si]
                o = self.aT_ap(i, off, w)
                self.dve((lambda e, o=o, pap=pap, t=tap[:, 0:w]:
                          e.tensor_tensor(out=o, in0=pap, in1=t, op=ALU.mult)),
                         [pb, tb], [self.aT_b[i][si]])
            blocks.append(((2 * li) * 1024, ev_gate))
            blocks.append(((2 * li + 1) * 1024, ev_up))
        return blocks

    def ffn_first_blocks(self, tl, l):
        slot = self.w_get(f"gu{l}", 0)
        return [self.gemm_blk(slot, e0, 8, self.xr, tl, ev) for (e0, ev) in self.gu_blocks(slot, 0, tl)]

    def store_out(self, tl, si):
        off, w = tl["subs"][si]
        t0 = tl["t0"] - HALO
        dst = self.outT[:, NCH * (t0 + off): NCH * (t0 + off + w)]
        self.S.dma("act", (lambda e, dst=dst, src=self.h_sub(si, w): e.dma_start(out=dst, in_=src)), f"d:o{si}",
                   reads=[self.hT_b[c][si] for c in range(NCH)])

    def ffn_rest(self, tl, l, final=False):
        subs = tl["subs"]
        for u in range(1, len(GU_UNITS)):
            slot = self.w_get(f"gu{l}", u)
            for (e0, ev) in self.gu_blocks(slot, u, tl):
                self.gemm(slot, e0, 8, self.xr, subs, ev)
        ar = lambda k, si, off, w: (self.aT_ap(k, off, w), [self.aT_b[k][si]])
        for u in range(4):
            slot = self.w_get(f"dn{l}", u)
            if u == 3:
                blocks = [self.gemm_blk(slot, j * NF * P, NF, ar, tl, self.evac_f(2 * u + j, self.cs("zero")))
                          for j in range(2)]
                if final:
                    for si in range(len(subs)):
                        for b in blocks:
                            b(si)
                        self.post(tl, si, "g_ffn_post1", False)
                        self.store_out(tl, si)
                else:
                    for b in blocks:
                        b(0)
                    self.post(tl, 0, f"g_ffn_post{l}", True)
                    if len(subs) > 1:
                        for b in blocks:
                            b(1)
                continue
            for j in range(2):
                mc = 2 * u + j
                self.gemm(slot, j * NF * P, NF, ar, subs, self.evac_f(mc, self.cs("zero")))

    def mixer1_first_blocks(self, tl):
        S = self.S
        for j in range(2):
            o, i0 = self.kT_ap(j, 0, P), self.kprev[:, j * P:(j + 1) * P]
            S.op("pool", (lambda e, o=o, i0=i0: e.tensor_copy(out=o, in_=i0)),
                 reads=[self.kprev_b], writes=[self.kT_b[j][0]])
        o, i0 = self.V_ap(0), self.vprev[:, :]
        S.op("pool", (lambda e, o=o, i0=i0: e.tensor_copy(out=o, in_=i0)),
             reads=[self.vprev_b], writes=[self.V_b[0]])
        if tl["halo"]:
            return []
        slot = self.w_get("q", 0)
        blocks = []
        for cq in range(8):
            def ev_q(si, off, w, pap, pb, cq=cq):
                self.act(self.qT_ap(cq, off, w), pap, AF.Identity, [pb], [self.qT_b[cq][si]],
                         bias=self.cs("b_q", cq))
            blocks.append(self.gemm_blk(slot, cq * 1024, 8, self.xr, tl, ev_q))
        return blocks

    def mixer1_rest(self, tl):
        subs = tl["subs"]
        W = tl["W"]
        nb = W // P
        S = self.S
        slot = self.w_get("kv", 0)
        for j in range(2):
            def ev_k(si, off, w, pap, pb, j=j):
                blks = [self.kT_b[j][1 + (off + x) // P] for x in range(0, w, P)]
                self.act(self.kT_ap(j, P + off, w), pap, AF.Identity, [pb], blks, bias=self.cs("b_k", j))
            self.gemm(slot, j * 1024, 8, self.xr, subs, ev_k)
        for b in range(nb):
            bk = self.next_bank()
            si = (b * P) // SW if not tl["halo"] else 0
            for k in range(NCH):
                self.mm(self.ps[bk][:, 0:256], self.xn_ap(k, b * P, P), self.w_ap(slot, 2048 + k * 256, 256),
                        k == 0, k == NCH - 1, reads=[self.w_b[slot], self.xn_b[k][si]],
                        writes=[self.ps_b[bk]], inc=(k == NCH - 1))
            o, pap, bv = self.V_ap(1 + b), self.ps[bk][:, 0:256], self.cs("b_v", 0, 256)
            self.dve((lambda e, o=o, pap=pap, bv=bv: e.tensor_tensor(out=o, in0=pap, in1=bv, op=ALU.add)),
                     [self.ps_b[bk], self.const_b], [self.V_b[1 + b]])

        if not tl["halo"]:
            iters = [(b, pp) for b in range(nb) for pp in range(2)]
            state = {}

            def stage_a(it):
                b, pp = iters[it]
                si = (b * P) // SW
                pts = []
                for hh in range(2):
                    rows = slice(hh * 64, hh * 64 + 64)
                    for kb in range(2):
                        bank = hh * 2 + kb
                        q3 = self.r44[rows, self.QO + pp * 4 * TT: self.QO + (pp + 1) * 4 * TT] \
                            .rearrange("p (c t) -> p c t", c=4)[:, :, b * P:(b + 1) * P]
                        o3 = self.ps[bank][:, :].rearrange("p (g q) -> p g q", g=4)
                        self.mm(o3, self.kT_ap(pp, (b + kb) * P, P, rows=rows), q3, True, False,
                                reads=[self.kT_b[pp][b + kb]] + [self.qT_b[pp * 4 + g][si] for g in range(4)],
                                writes=[self.ps_b[bank]], inc=False, tp=(hh * 64, 0))
                for hh in range(2):
                    for kb in range(2):
                        bank = hh * 2 + kb
                        h0 = 4 * (2 * pp + hh)
                        o3 = self.ps[bank][:, :].rearrange("p (g q) -> p g q", g=4)
                        b3 = self.bhi[:, :].rearrange("p (h k q) -> p h k q", h=NQH, k=2)[:, h0:h0 + 4, kb, :]
                        self.mm(o3, self.ident[:, :], b3, False, True, reads=[self.const_b],
                                writes=[self.ps_b[bank]], inc=True)
                        pc, psi = (it * 4 + bank) % 8, 1
                        pt = self.a_ap(pc, psi * SW, SW).bitcast(BF16)[:, 0:SW]
                        ptb = self.a32_b[pc][psi]
                        self.act(pt, self.ps[bank][:, :], AF.Exp, [self.ps_b[bank]], [ptb], scale=0.125)
                        if tl["idx"] == 1 and b == 0 and kb == 0:
                            f = self.cs("flag")
                            self.dve((lambda e, pt=pt, f=f:
                                      e.tensor_scalar(out=pt, in0=pt, scalar1=f, scalar2=None, op0=ALU.mult)),
                                     [ptb, self.const_b], [ptb])
                        pts.append((hh, kb, pt, ptb))
                state[it] = pts

            def stage_b(it):
                b, pp = iters[it]
                si = (b * P) // SW
                pts = state.pop(it)
                bo = 4 + 2 * (it % 2)
                bd = bo + 1
                for (hh, kb, pt, ptb) in pts:
                    vcol = (2 * pp + hh) * 64
                    self.mm(self.ps[bo][hh * 64:hh * 64 + 64, :], self.V_ap(b + kb, vcol, 64), pt,
                            kb == 0, kb == 1, reads=[self.V_b[b + kb], ptb], writes=[self.ps_b[bo]],
                            inc=(hh == 1 and kb == 1), tp=(0, hh * 64))
                for (hh, kb, pt, ptb) in pts:
                    self.mm(self.ps[bd][hh * 64:hh * 64 + 64, :], self.ones1[:, 0:64], pt,
                            kb == 0, kb == 1, reads=[ptb, self.const_b], writes=[self.ps_b[bd]],
                            inc=(hh == 1 and kb == 1), tp=(0, hh * 64))
                rc, rcb = self.st[:, (2 + it % 2) * SW:(3 + it % 2) * SW], self.st_b[2 + it % 2]
                for g in range(4):
                    o, i0, sk = rc[:, g * P:(g + 1) * P], self.ps[bd][:, g * P:(g + 1) * P], \
                        self.exps[:, pp * 4 + g: pp * 4 + g + 1]
                    self.dve((lambda e, o=o, i0=i0, sk=sk:
                              e.tensor_scalar(out=o, in0=i0, scalar1=sk, scalar2=None, op0=ALU.add)),
                             [self.ps_b[bd], self.const_b], [rcb])
                self.act(rc, rc, AF.Ln, [rcb], [rcb])
                self.act(rc, rc, AF.Exp, [rcb], [rcb], scale=-1.0)
                o3 = self.r44[:, self.AO + pp * 4 * TT: self.AO + (pp + 1) * 4 * TT] \
                    .rearrange("p (c t) -> p c t", c=4)[:, :, b * P:(b + 1) * P]
                i3 = self.ps[bo][:, :].rearrange("p (g q) -> p g q", g=4)
                r3 = rc.rearrange("p (g q) -> p g q", g=4)
                self.dve((lambda e, o3=o3, i3=i3, r3=r3: e.tensor_tensor(out=o3, in0=i3, in1=r3, op=ALU.mult)),
                         [self.ps_b[bo], rcb], [self.at_b[pp * 4 + g][si] for g in range(4)])

            n = len(iters)
            stage_a(0)
            for it in range(1, n):
                stage_a(it)
                stage_b(it - 1)
            stage_b(n - 1)
        for j in range(2):
            o, i0 = self.kprev[:, j * P:(j + 1) * P], self.kT_ap(j, W, P)
            S.op("pool", (lambda e, o=o, i0=i0: e.tensor_copy(out=o, in_=i0)),
                 reads=[self.kT_b[j][nb]], writes=[self.kprev_b])
        o, i0 = self.vprev[:, :], self.V_ap(nb)
        S.op("pool", (lambda e, o=o, i0=i0: e.tensor_copy(out=o, in_=i0)),
             reads=[self.V_b[nb]], writes=[self.vprev_b])
        if tl["halo"]:
            return
        ar = lambda k, si, off, w: (self.at_ap(k, off, w), [self.at_b[k][si]])
        slot = self.w_get("o", 0)
        blocks = [self.gemm_blk(slot, mc * 1024, 8, ar, tl, self.evac_f(mc, self.cs("b_o", mc))) for mc in range(8)]
        for b in blocks:
            b(0)
        self.post(tl, 0, "g_mix_post1", True)
        for b in blocks:
            b(1)

    def h_sub(self, si, w):
        blk = self.hT[:, si * NCH * SW:(si + 1) * NCH * SW]
        if w == SW:
            return blk
        return blk.rearrange("p (c t) -> p c t", c=NCH)[:, :, 0:w]

    def load_x(self, tl):
        subs, t0 = tl["subs"], tl["t0"]
        for si, (off, w) in enumerate(subs):
            src = self.xT[:, NCH * (t0 + off): NCH * (t0 + off + w)]
            if w != SW:
                src = src.rearrange("p (c t) -> p c t", c=NCH)
            self.S.dma("sp", (lambda e, dst=self.h_sub(si, w), src=src: e.dma_start(out=dst, in_=src)), f"d:x{si}",
                       writes=[self.hT_b[c][si] for c in range(NCH)])

    def layout(self, which):
        if which == "A":
            return list(self.uT_b)
        if which == "B":
            return [b for row in self.aT_b for b in row]
        return [b for row in self.qT_b for b in row] + [b for row in self.at_b for b in row] + \
               [b for row in self.kT_b for b in row] + list(self.V_b)

    def handoff(self, old, new):
        bufs = []
        for k in old + new:
            bufs += self.layout(k)
        self.S.op("pool", (lambda e: e.tensor_copy(out=self.dummy[:, 0:8], in_=self.dummy[:, 8:16])),
                  reads=[], writes=bufs + [self.dummy_b])

    def emit_all(self):
        S = self.S
        nc = self.nc
        S.dma("sp", (lambda e: e.dma_start(out=self.consts[:, :], in_=self.consts_d)), "d:c0", writes=[self.const_b])
        bias_b = Buf("biasld")
        self.load_x(self.tiles()[0])
        self.identf = self.a32[:, 0:P]
        self.maskT = self.a32[:, P:3 * P]
        self.biasT = self.a32[:, 4 * P: 4 * P + NQH * 2 * P]
        alla = [b for row in self.a32_b for b in row]
        S.dma("sp", (lambda e: e.dma_start(out=self.biasT, in_=self.biasT_d)), "d:c1", writes=[bias_b] + alla)
        S.dma("sp", (lambda e: e.dma_start(out=self.maskT, in_=self.maskT_d)), "d:c2", writes=[bias_b] + alla)
        S.op("pool", (lambda e: e.memset(self.onesm[:, :], 1.0 / D)), writes=[self.const_b], reads=[])
        S.op("pool", (lambda e: e.memset(self.ones1[:, :], 1.0)), writes=[self.const_b], reads=[])
        S.op("pool", (lambda e: e.memset(self.uhalo[:, :], 0.0)), writes=[self.uhalo_b])
        S.op("pool", (lambda e: e.memset(self.kprev[:, :], 0.0)), writes=[self.kprev_b])
        S.op("pool", (lambda e: e.memset(self.vprev[:, :], 0.0)), writes=[self.vprev_b])
        S.dma("sp", (lambda e: e.dma_start(out=self.identf, in_=self.ident_d)), "d:c3",
              writes=[self.const_b] + alla)
        S.op("pool", (lambda e: e.tensor_copy(out=self.ident[:, :], in_=self.identf)),
             reads=[self.const_b] + alla, writes=[self.const_b])
        self.act(self.exps[:, :], self.cs("sinks", 0, 8), AF.Exp, [], [self.const_b])
        for h in range(NQH):
            o = self.biasT[:, h * 2 * P:(h + 1) * 2 * P]
            self.dve((lambda e, o=o: e.tensor_tensor(out=o, in0=o, in1=self.maskT, op=ALU.add)),
                     [bias_b] + alla, [bias_b] + alla)
        self.dve((lambda e: e.tensor_scalar(out=self.biasT, in0=self.biasT, scalar1=8.0, scalar2=None, op0=ALU.mult)),
                 [bias_b] + alla, [bias_b] + alla)
        self.dve((lambda e: e.tensor_copy(out=self.bhi[:, :], in_=self.biasT)), [bias_b] + alla, [self.const_b])

        S.op("pool", (lambda e: e.memset(self.dummy[:, :], 0.0)), writes=[self.dummy_b])
        for tl in self.tiles():
            subs, W, t0 = tl["subs"], tl["W"], tl["t0"]
            if tl["idx"] > 0:
                self.load_x(tl)
            nsub = len(subs)
            if tl["idx"] > 0:
                self.handoff(["B", "C"], ["A"])
            for si in range(nsub):
                self.sq_h(tl, si)
            self.pre(tl, 0, "g_mix_pre0")
            blocks = self.mixer0_first_blocks(tl)
            self.run_first_unit(nsub, [], (lambda: self.pre(tl, 1, "g_mix_pre0")) if nsub > 1 else None, blocks)
            self.mixer0_rest(tl)
            self.handoff(["A"], ["B"])
            self.boundary(tl, "g_mix_post0", "g_ffn_pre0", self.ffn_first_blocks(tl, 0), post0_done=True)
            self.ffn_rest(tl, 0)
            self.handoff(["B"], ["C"])
            self.boundary(tl, "g_ffn_post0", "g_mix_pre1", self.mixer1_first_blocks(tl), post0_done=True)
            self.mixer1_rest(tl)
            if tl["halo"]:
                continue
            self.handoff(["C"], ["B"])
            self.boundary(tl, "g_mix_post1", "g_ffn_pre1", self.ffn_first_blocks(tl, 1), post0_done=True)
            self.ffn_rest(tl, 1, final=True)
        assert self.wnext == len(self.plan), (self.wnext, len(self.plan))

    def finalize(self, es):
        nc = self.nc
        S = self.S
        names = list(S.engs.keys()) + sorted(S.dcnt.keys())
        sems = {n: es.enter_context(nc.semaphore(n.replace(":", "_"))) for n in names}
        final_waits = [(n, v) for n, v in S.dcnt.items() if n.startswith("d:o")]

        def replay(e, ops, tail=()):
            for waits, fn, incsem, incval in ops:
                for sname, v in waits[1:]:
                    e.wait_ge(sems[sname], v)
                ins = fn(e)
                if waits:
                    ins._wait_ge(sems[waits[0][0]], waits[0][1])
                if incsem is not None:
                    ins.then_inc(sems[incsem], incval)
            for sname, v in tail:
                e.wait_ge(sems[sname], v)

        with nc.Block() as block:
            @block.tensor
            def _(e):
                replay(e, S.engs["pe"].ops)

            @block.scalar
            def _(e):
                replay(e, S.engs["act"].ops)

            @block.vector
            def _(e):
                replay(e, S.engs["dve"].ops)

            @block.gpsimd
            def _(e):
                replay(e, S.engs["pool"].ops)

            @block.sync
            def _(e):
                replay(e, S.engs["sp"].ops, tail=final_waits)


def _vec8(v):
    return np.ascontiguousarray(np.asarray(v, np.float32).reshape(NCH, P).T)


def _blk(Wm, cols_list):
    K = Wm.shape[0]
    kc = K // P
    out = np.empty((P, len(cols_list), kc, P), np.float32)
    for j, cols in enumerate(cols_list):
        out[:, j] = Wm[:, cols].reshape(kc, P, P).transpose(1, 0, 2)
    return out.reshape(P, -1)


def _t5_bucket(dist):
    dist = np.maximum(dist, 0)
    max_exact = 16
    large = max_exact + (np.log(np.maximum(dist, 1).astype(np.float32) / np.float32(max_exact))
                         / np.float32(np.log(128.0 / max_exact)) * np.float32(32 - max_exact)).astype(np.int32)
    large = np.minimum(large, 31)
    return np.where(dist < max_exact, dist, large)


def _qhead(cq, half):
    pp, g = cq // 4, cq % 4
    return 4 * (2 * pp + half) + g


def _prep_shared(inp):
    f = lambda k: np.asarray(inp[k], np.float32)
    consts = np.zeros((P, NCONST), np.float32)

    def put(name, arr):
        arr = np.asarray(arr, np.float32)
        consts[:, _CL[name]:_CL[name] + arr.shape[1]] = arr
    for l in range(2):
        put(f"g_mix_pre{l}", _vec8(f("mix_pre_g")[l]))
        put(f"g_mix_post{l}", _vec8(f("mix_post_g")[l]))
        put(f"g_ffn_pre{l}", _vec8(f("ffn_pre_g")[l]))
        put(f"g_ffn_post{l}", _vec8(f("ffn_post_g")[l]))
    b_in = f("conv_b_in")[0]
    put("b_in_v", _vec8(b_in[:D]))
    put("b_in_g", _vec8(b_in[D:]))
    put("dw_b", _vec8(f("conv_dw_b")[0]))
    put("ln_g", _vec8(f("conv_ln_g")[0]))
    put("ln_b", _vec8(f("conv_ln_b")[0]))
    put("b_out", _vec8(f("conv_b_out")[0]))
    dww = f("conv_dw_w")[0]
    put("dw_w", dww.T.reshape(NCH, P, CONVW).transpose(1, 0, 2).reshape(P, NCH * CONVW))
    bqkv = f("attn_b_qkv")[0]
    qcols = [np.concatenate([_qhead(cq, 0) * 64 + np.arange(64), _qhead(cq, 1) * 64 + np.arange(64)])
             for cq in range(8)]
    put("b_q", np.stack([bqkv[c] for c in qcols], axis=1))
    put("b_k", np.stack([bqkv[D + j * P: D + (j + 1) * P] for j in range(2)], axis=1))
    put("b_o", _vec8(f("attn_b_o")[0]))
    sinks = f("attn_sinks")[0]
    sk = np.zeros((P, 8), np.float32)
    for cq in range(8):
        sk[:64, cq] = sinks[_qhead(cq, 0)]
        sk[64:, cq] = sinks[_qhead(cq, 1)]
    put("sinks", sk)
    consts[:, _CL["eps"]] = EPS
    put("b_v", np.broadcast_to(bqkv[D + 256: D + 512][None, :], (P, 256)))

    s_i = np.arange(P)[:, None]
    q_i = np.arange(P)[None, :]
    dist = np.stack([q_i + P - s_i, q_i - s_i], axis=0)
    valid = (dist >= 0) & (dist < P)
    bucket = _t5_bucket(dist)
    rel = f("rel_bias")
    biasT = rel[bucket]
    biasT = np.ascontiguousarray(biasT.transpose(1, 3, 0, 2)).reshape(P, NQH * 2 * P)
    maskT = np.where(valid, np.float32(0.0), np.float32(NEG)).astype(np.float32)
    maskT = np.ascontiguousarray(maskT.transpose(1, 0, 2)).reshape(P, 2 * P)

    ar = np.arange(P)
    w_in = f("conv_w_in")[0]
    w_out = f("conv_w_out")[0]
    wqkv = f("attn_w_qkv")[0]
    wo = f("attn_w_o")[0]
    orow = np.concatenate(qcols)
    wo_p = wo[orow, :]
    in_units = []
    for u in range(2):
        cl = []
        for j in range(4):
            mc = 4 * u + j
            cl += [mc * P + ar, D + mc * P + ar]
        in_units.append(_blk(w_in, cl))
    wv_blk = np.ascontiguousarray(wqkv[:, D + 256: D + 512].reshape(NCH, P, 256).transpose(1, 0, 2)).reshape(P, 2048)
    sh = {
        "consts": consts, "biasT": biasT, "maskT": maskT, "ident": np.eye(P, dtype=np.float32),
        "w_in": np.concatenate(in_units, 0),
        "w_out": _blk(w_out, [mc * P + ar for mc in range(8)]),
        "w_q": _blk(wqkv, qcols),
        "w_kv": np.concatenate([_blk(wqkv, [D + ar, D + P + ar]), wv_blk], axis=1),
        "w_o": _blk(wo_p, [mc * P + ar for mc in range(8)]),
    }
    for l in range(2):
        wgu = f("ffn_w_gate_up")[l]
        wdn = f("ffn_w_down")[l]
        gus = []
        for (p0, p1) in GU_UNITS:
            cl = []
            for i in range(p0, p1):
                cl += [i * P + ar, DFF + i * P + ar]
            blk = np.zeros((P, 8192), np.float32)
            blk[:, :len(cl) * 1024] = _blk(wgu, cl)
            gus.append(blk)
        sh[f"w_gu{l}"] = np.concatenate(gus, 0)
        sh[f"w_dn{l}"] = np.concatenate([_blk(wdn, [2 * u * P + ar, (2 * u + 1) * P + ar]) for u in range(4)], 0)
    return sh


_PROG_CACHE = {}


def kernel(**inputs):
    x = np.asarray(inputs["x"], np.float32)
    sh = _prep_shared(inputs)
    in_maps = []
    for core in range(NCORES):
        b, half = core // 2, core % 2
        start = half * TOK
        xl = np.zeros((TLOC, D), np.float32)
        xl[HALO:] = x[b, start:start + TOK]
        if half == 1:
            xl[:HALO] = x[b, start - HALO:start]
        x3 = xl.T.reshape(NCH, P, TLOC).transpose(1, 0, 2)
        xT = np.concatenate([x3[:, :, tl["t0"] + off: tl["t0"] + off + w].reshape(P, -1)
                             for tl in Prog.tiles() for (off, w) in tl["subs"]], axis=1)
        xT = np.ascontiguousarray(xT)
        m = dict(sh)
        c = sh["consts"].copy()
        c[:, _CL["flag"]] = 1.0 if half == 1 else 0.0
        m["consts"] = c
        m["xT"] = xT
        in_maps.append(m)
    if "nc" not in _PROG_CACHE:
        _PROG_CACHE["nc"] = Prog().build()
    res = run_bass_kernel_spmd(_PROG_CACHE["nc"], in_maps, core_ids=list(range(NCORES)))
    out = np.empty((BATCH, SEQ, D), np.float32)
    for core in range(NCORES):
        b, half = core // 2, core % 2
        oT = np.asarray(res.results[core]["outT"]).reshape(P, TOK // SW, NCH, SW)
        out[b, half * TOK:(half + 1) * TOK] = oT.transpose(1, 3, 2, 0).reshape(TOK, D)
    return out
```

```python
import numpy as np
from contextlib import ExitStack
import concourse.bass as bass
import concourse.mybir as mybir
from concourse.bass_utils import run_bass_kernel_spmd

F32 = mybir.dt.float32
BF16 = mybir.dt.bfloat16
AF = mybir.ActivationFunctionType
ALU = mybir.AluOpType

P = 128
D = 1024
NCH = 8
DFF = 2816
NF = 22
SEQ = 8192
BATCH = 4
NCORES = 8
TOK = 4096
HALO = 256
TLOC = TOK + HALO
TT = 1024
SW = 512
CONVW = 31
UH = 32
DT = 16
NQH = 16
EPS = 1e-6
NEG = -30000.0
NSLOT = 2
SLOT_E = 8192

GU_UNITS = [(0, 4), (4, 8), (8, 12), (12, 16), (16, 19), (19, 22)]

_CL = {}
_off = 0
for _n, _w in [("g_mix_pre0", 8), ("g_mix_post0", 8), ("g_ffn_pre0", 8), ("g_ffn_post0", 8),
               ("g_mix_pre1", 8), ("g_mix_post1", 8), ("g_ffn_pre1", 8), ("g_ffn_post1", 8),
               ("b_in_v", 8), ("b_in_g", 8), ("dw_b", 8), ("ln_g", 8), ("ln_b", 8), ("b_out", 8),
               ("dw_w", 8 * CONVW), ("b_q", 8), ("b_k", 2), ("b_o", 8), ("flag", 1), ("sinks", 8),
               ("eps", 1), ("zero", 1), ("b_v", 256)]:
    _CL[_n] = _off
    _off += _w
NCONST = _off


class Buf:
    __slots__ = ("name", "w", "r")

    def __init__(self, name):
        self.name = name
        self.w = None
        self.r = {}


class _Eng:
    def __init__(self, name):
        self.name = name
        self.cnt = 0
        self.ops = []
        self.waited = {}


class Sched:
    def __init__(self):
        self.engs = {k: _Eng(k) for k in ("pe", "act", "dve", "pool", "sp")}
        self.dcnt = {}

    def _waits(self, e, reads, writes):
        deps = {}
        for b in reads:
            if b.w is not None and deps.get(b.w[0], 0) < b.w[1]:
                deps[b.w[0]] = b.w[1]
        for b in writes:
            if b.w is not None and deps.get(b.w[0], 0) < b.w[1]:
                deps[b.w[0]] = b.w[1]
            for s, v in b.r.items():
                if deps.get(s, 0) < v:
                    deps[s] = v
        waits = []
        for s, v in deps.items():
            if e.name == "pe" and s == "pe":
                continue
            if e.waited.get(s, 0) >= v:
                continue
            if s == e.name:
                assert v <= e.cnt, (e.name, v, e.cnt)
            e.waited[s] = v
            waits.append((s, v))
        return waits

    def op(self, eng, fn, reads=(), writes=(), inc=True):
        e = self.engs[eng]
        assert inc or eng == "pe"
        waits = self._waits(e, reads, writes)
        ev = (eng, e.cnt + 1)
        e.ops.append((waits, fn, eng if inc else None, 1))
        if inc:
            e.cnt += 1
        for b in reads:
            if b.r.get(eng, 0) < ev[1]:
                b.r[eng] = ev[1]
        for b in writes:
            b.w = ev
            b.r = {}

    def dma(self, eng, fn, dsem, reads=(), writes=()):
        e = self.engs[eng]
        waits = self._waits(e, reads, writes)
        self.dcnt[dsem] = self.dcnt.get(dsem, 0) + 16
        ev = (dsem, self.dcnt[dsem])
        e.ops.append((waits, fn, dsem, 16))
        for b in reads:
            if b.r.get(dsem, 0) < ev[1]:
                b.r[dsem] = ev[1]
        for b in writes:
            b.w = ev
            b.r = {}


class Prog:
    def __init__(self):
        self.nc = bass.Bass("TRN2", target_bir_lowering=False)
        self.S = Sched()
        self.bank_rr = 0
        self.tmp_rr = 0
        self.wnext = 0
        self.wissued = 0

    @staticmethod
    def tiles():
        t = [dict(idx=0, t0=0, W=HALO, subs=[(0, HALO)], halo=True)]
        for i in range(TOK // TT):
            t.append(dict(idx=i + 1, t0=HALO + i * TT, W=TT,
                          subs=[(s * SW, SW) for s in range(TT // SW)], halo=False))
        return t

    def weight_plan(self):
        plan = []
        for tl in self.tiles():
            plan += [("in", 0), ("in", 1), ("out", 0)]
            plan += [("gu0", u) for u in range(len(GU_UNITS))]
            plan += [("dn0", u) for u in range(4)]
            if tl["halo"]:
                plan.append(("kv", 0))
                continue
            plan += [("q", 0), ("kv", 0), ("o", 0)]
            plan += [("gu1", u) for u in range(len(GU_UNITS))]
            plan += [("dn1", u) for u in range(4)]
        return plan

    def unit_E(self, kind, idx):
        if kind.startswith("gu"):
            a, b = GU_UNITS[idx]
            return (b - a) * 2048
        return {"in": 8192, "out": 8192, "dn0": 5632, "dn1": 5632, "q": 8192, "kv": 4096, "o": 8192}[kind]

    def build(self):
        nc = self.nc
        S = self.S
        es = ExitStack()
        with es:
            def dram(name, shape, dt=F32, kind="ExternalInput"):
                return nc.dram_tensor(name, shape, dt, kind=kind).ap()

            self.xT = dram("xT", [P, NCH * TLOC])
            self.outT = dram("outT", [P, NCH * TOK], kind="ExternalOutput")
            self.consts_d = dram("consts", [P, NCONST])
            self.biasT_d = dram("biasT", [P, NQH * 2 * P])
            self.maskT_d = dram("maskT", [P, 2 * P])
            self.ident_d = dram("ident", [P, P])
            self.wd = {
                "in": dram("w_in", [2 * P, 8192]), "out": dram("w_out", [P, 8192]),
                "gu0": dram("w_gu0", [6 * P, 8192]), "gu1": dram("w_gu1", [6 * P, 8192]),
                "dn0": dram("w_dn0", [4 * P, 5632]), "dn1": dram("w_dn1", [4 * P, 5632]),
                "q": dram("w_q", [P, 8192]), "kv": dram("w_kv", [P, 4096]), "o": dram("w_o", [P, 8192]),
            }

            def sb(name, shape, dt):
                return es.enter_context(nc.sbuf_tensor(name, shape, dt))

            self.hT = sb("hT", [P, NCH * TT], F32)
            self.xn = sb("xn", [P, NCH * TT], BF16)
            self.sq = sb("sq", [P, NCH * TT], BF16)
            self.a32 = sb("a32", [P, NCH * TT], F32)
            self.r44 = sb("r44", [P, NF * TT], BF16)
            self.uhalo = sb("uhalo", [P, NCH * UH], BF16)
            self.kprev = sb("kprev", [P, 2 * P], BF16)
            self.vprev = sb("vprev", [P, 256], BF16)
            self.wring = sb("wring", [P, NSLOT * SLOT_E], BF16)
            self.diag = sb("diag", [P, 2 * DT * P], BF16)
            self.ident = sb("identb", [P, P], BF16)
            self.onesm = sb("onesm", [P, P], BF16)
            self.ones1 = sb("ones1", [P, P], BF16)
            self.consts = sb("constsb", [P, NCONST], F32)
            self.exps = sb("exps", [P, 8], F32)
            self.bhi = sb("bhi", [P, NQH * 2 * P], BF16)
            self.st = sb("st", [P, 4 * SW], F32)
            self.dummy = sb("dummyt", [P, 16], F32)
            self.dummy_b = Buf("dummy")
            self.ps = [es.enter_context(nc.psum_tensor(f"ps{i}", [P, SW], F32)) for i in range(8)]

            nsub = TT // SW
            self.hT_b = [[Buf(f"hT{c}_{s}") for s in range(nsub)] for c in range(NCH)]
            self.xn_b = [[Buf(f"xn{c}_{s}") for s in range(nsub)] for c in range(NCH)]
            self.sq_b = [[Buf(f"sq{c}_{s}") for s in range(nsub)] for c in range(NCH)]
            self.a32_b = [[Buf(f"a32{c}_{s}") for s in range(nsub)] for c in range(NCH)]
            self.r44_b = Buf("r44")
            self.aT_b = [[Buf(f"aT{i}_{s}") for s in range(nsub)] for i in range(NF)]
            self.uT_b = [Buf(f"uT{c}") for c in range(NCH)]
            self.qT_b = [[Buf(f"qT{c}_{s}") for s in range(nsub)] for c in range(NCH)]
            self.at_b = [[Buf(f"at{c}_{s}") for s in range(nsub)] for c in range(NCH)]
            self.kT_b = [[Buf(f"kT{j}_{b}") for b in range(TT // P + 1)] for j in range(2)]
            self.V_b = [Buf(f"V{b}") for b in range(TT // P + 1)]
            self.uhalo_b = Buf("uhalo")
            self.kprev_b = Buf("kprev")
            self.vprev_b = Buf("vprev")
            self.w_b = [Buf(f"w{i}") for i in range(NSLOT)]
            self.diag_b = [Buf("diag0"), Buf("diag1")]
            self.const_b = Buf("const")
            self.st_b = [Buf(f"st{i}") for i in range(4)]
            self.ps_b = [Buf(f"ps{i}") for i in range(8)]
            self.plan = self.weight_plan()

            self.emit_all()
            self.finalize(es)
        return nc

    def cs(self, name, c=0, n=1):
        o = _CL[name] + c
        return self.consts[:, o:o + n]

    def h_ap(self, c, off, w):
        si, o = off // SW, off % SW
        assert o + w <= SW
        base = si * NCH * SW + c * SW + o
        return self.hT[:, base: base + w]

    def xn_ap(self, c, off, w):
        return self.xn[:, c * TT + off: c * TT + off + w]

    def sq_ap(self, c, off, w):
        return self.sq[:, c * TT + off: c * TT + off + w]

    def a_ap(self, c, off, w):
        return self.a32[:, c * TT + off: c * TT + off + w]

    def aT_ap(self, i, off, w):
        return self.r44[:, i * TT + off: i * TT + off + w]

    UTW = UH + TT

    def uT_ap(self, c, col, w):
        return self.r44[:, c * self.UTW + col: c * self.UTW + col + w]

    QO = 0
    AO = 8 * TT
    KO = 16 * TT
    KW = P + TT
    VO = 16 * TT + 2 * (P + TT)

    def qT_ap(self, c, off, w, rows=slice(0, P)):
        return self.r44[rows, self.QO + c * TT + off: self.QO + c * TT + off + w]

    def at_ap(self, c, off, w):
        return self.r44[:, self.AO + c * TT + off: self.AO + c * TT + off + w]

    def kT_ap(self, j, col, w, rows=slice(0, P)):
        return self.r44[rows, self.KO + j * self.KW + col: self.KO + j * self.KW + col + w]

    def V_ap(self, blk, c0=0, w=256):
        return self.r44[:, self.VO + blk * 256 + c0: self.VO + blk * 256 + c0 + w]

    def w_ap(self, slot, e0, w):
        return self.wring[:, slot * SLOT_E + e0: slot * SLOT_E + e0 + w]

    def tmp_unit(self):
        c = self.tmp_rr % 8
        self.tmp_rr += 1
        return self.a_ap(c, 0, SW), self.a32_b[c][0]

    def next_bank(self):
        b = self.bank_rr % 6
        self.bank_rr += 1
        return b

    def w_issue_upto(self, n):
        S = self.S
        while self.wissued <= min(n, len(self.plan) - 1):
            i = self.wissued
            kind, idx = self.plan[i]
            slot = i % NSLOT
            E = self.unit_E(kind, idx)
            src = self.wd[kind][idx * P:(idx + 1) * P, 0:E]
            dst = self.w_ap(slot, 0, E)
            S.dma("pool", (lambda e, dst=dst, src=src: e.dma_start(out=dst, in_=src)), f"d:w{slot}",
                  writes=[self.w_b[slot]])
            self.wissued += 1

    def w_get(self, kind, idx):
        i = self.wnext
        assert self.plan[i] == (kind, idx), (self.plan[i], kind, idx)
        self.w_issue_upto(i + NSLOT - 1)
        self.wnext += 1
        return i % NSLOT

    def mm(self, out, lhsT, rhs, start, stop, reads, writes, inc, tp=None):
        if tp is None:
            fn = (lambda e, out=out, lhsT=lhsT, rhs=rhs, start=start, stop=stop:
                  e.matmul(out, lhsT=lhsT, rhs=rhs, start=start, stop=stop))
        else:
            fn = (lambda e, out=out, lhsT=lhsT, rhs=rhs, start=start, stop=stop, tp=tp:
                  e.matmul(out, lhsT=lhsT, rhs=rhs, start=start, stop=stop, tile_position=tp))
        self.S.op("pe", fn, reads=reads, writes=writes, inc=inc)

    def act(self, out, in_, func, reads, writes, bias=None, scale=1.0):
        if bias is None:
            bias = self.cs("zero")
        self.S.op("act", (lambda e, out=out, in_=in_, func=func, bias=bias, scale=scale:
                          e.activation(out=out, in_=in_, func=func, bias=bias, scale=scale)),
                  reads=list(reads) + [self.const_b], writes=writes)

    def dve(self, fn, reads, writes):
        self.S.op("dve", fn, reads=reads, writes=writes)

    def gemm(self, slot, e0, nk, rhs_fn, subs, evac):
        banks = [self.next_bank() for _ in subs]
        for k in range(nk):
            for si, (off, w) in enumerate(subs):
                rap, rb = rhs_fn(k, si, off, w)
                lhs = self.w_ap(slot, e0 + k * P, P)
                wr = [self.ps_b[b] for b in banks] if (k == 0 and si == 0) else [self.ps_b[banks[si]]]
                self.mm(self.ps[banks[si]][:, 0:w], lhs, rap, k == 0, k == nk - 1,
                        reads=[self.w_b[slot]] + rb, writes=wr, inc=(k == nk - 1 and si == len(subs) - 1))
        for si, (off, w) in enumerate(subs):
            evac(si, off, w, self.ps[banks[si]][:, 0:w], self.ps_b[banks[si]])

    def stats_mm(self, si, off, w):
        bk = 6 + (si % 2)
        for c in range(NCH):
            self.mm(self.ps[bk][:, 0:w], self.onesm[:, :], self.sq_ap(c, off, w), c == 0, c == NCH - 1,
                    reads=[self.sq_b[c][si], self.const_b], writes=[self.ps_b[bk]], inc=(c == NCH - 1))
        return bk

    def rstd_act(self, src, src_b, si, off, w, add_eps=True):
        r = self.st[:, off:off + w]
        self.act(r, src, AF.Ln, src_b, [self.st_b[si]], bias=self.cs("eps") if add_eps else None)
        self.act(r, r, AF.Exp, [self.st_b[si]], [self.st_b[si]], scale=-0.5)
        return r

    def sq_h(self, tl, si):
        off, w = tl["subs"][si]
        for c in range(NCH):
            self.act(self.sq_ap(c, off, w), self.h_ap(c, off, w), AF.Square, [self.hT_b[c][si]], [self.sq_b[c][si]])

    def pre_stats(self, tl, si):
        off, w = tl["subs"][si]
        bk = self.stats_mm(si, off, w)
        self.rstd_act(self.ps[bk][:, 0:w], [self.ps_b[bk]], si, off, w)

    def pre_apply(self, tl, si, c, gname):
        off, w = tl["subs"][si]
        o, i0, g, r = self.xn_ap(c, off, w), self.h_ap(c, off, w), self.cs(gname, c), self.st[:, off:off + w]
        self.dve((lambda e, o=o, i0=i0, g=g, r=r:
                  e.scalar_tensor_tensor(out=o, in0=i0, scalar=g, in1=r, op0=ALU.mult, op1=ALU.mult)),
                 [self.hT_b[c][si], self.st_b[si], self.const_b], [self.xn_b[c][si]])

    def pre(self, tl, si, gname):
        self.pre_stats(tl, si)
        for c in range(NCH):
            self.pre_apply(tl, si, c, gname)

    def post_apply(self, tl, si, c, gname, square):
        off, w = tl["subs"][si]
        a, g, h, r = self.a_ap(c, off, w), self.cs(gname, c), self.h_ap(c, off, w), self.st[:, off:off + w]
        self.dve((lambda e, a=a, g=g, r=r:
                  e.scalar_tensor_tensor(out=a, in0=a, scalar=g, in1=r, op0=ALU.mult, op1=ALU.mult)),
                 [self.a32_b[c][si], self.st_b[si], self.const_b], [self.a32_b[c][si]])
        self.dve((lambda e, a=a, h=h: e.tensor_tensor(out=h, in0=h, in1=a, op=ALU.add)),
                 [self.a32_b[c][si], self.hT_b[c][si]], [self.hT_b[c][si]])
        if square:
            self.act(self.sq_ap(c, off, w), h, AF.Square, [self.hT_b[c][si]], [self.sq_b[c][si]])

    def post(self, tl, si, gname, square):
        self.pre_stats(tl, si)
        for c in range(NCH):
            self.post_apply(tl, si, c, gname, square)

    def run_first_unit(self, nsub, s1_ops, pre1, blocks, mid=None, late=None):
        if nsub == 1:
            for op in s1_ops:
                op()
            if pre1:
                pre1()
            for b in blocks:
                b(0)
            if mid:
                mid()
            if late:
                late()
            return
        nb0 = min(6, len(blocks))
        for c, op in enumerate(s1_ops):
            op()
            if c < nb0:
                blocks[c](0)
        for c in range(len(s1_ops), nb0):
            blocks[c](0)
        if pre1:
            pre1()
        for b in blocks[nb0:]:
            b(0)
        if mid:
            mid()
        n1 = len(blocks) - 2 if (late and len(blocks) >= 6) else len(blocks)
        for b in blocks[:n1]:
            b(1)
        if late:
            late()
        for b in blocks[n1:]:
            b(1)

    def boundary(self, tl, post_g, pre_g, blocks, post0_done=False, pre0_done=False):
        nsub = len(tl["subs"])
        if not post0_done:
            self.post(tl, 0, post_g, True)
        if not pre0_done:
            self.pre(tl, 0, pre_g)
        if nsub == 1:
            self.run_first_unit(1, [], None, blocks)
            return
        self.pre_stats(tl, 1)
        s1 = [(lambda c=c: self.post_apply(tl, 1, c, post_g, True)) for c in range(NCH)]
        self.run_first_unit(nsub, s1, (lambda: self.pre(tl, 1, pre_g)), blocks)

    def gemm_blk(self, slot, e0, nk, rhs_fn, tl, evac):
        def blk(si):
            off, w = tl["subs"][si]
            bk = self.next_bank()
            for k in range(nk):
                rap, rb = rhs_fn(k, si, off, w)
                self.mm(self.ps[bk][:, 0:w], self.w_ap(slot, e0 + k * P, P), rap, k == 0, k == nk - 1,
                        reads=[self.w_b[slot]] + rb, writes=[self.ps_b[bk]], inc=(k == nk - 1))
            evac(si, off, w, self.ps[bk][:, 0:w], self.ps_b[bk])
        return blk

    def evac_f(self, mc, bias_ap):
        def ev(si, off, w, pap, pb):
            self.act(self.a_ap(mc, off, w), pap, AF.Identity, [pb], [self.a32_b[mc][si]], bias=bias_ap)
            self.act(self.sq_ap(mc, off, w), pap, AF.Square, [pb], [self.sq_b[mc][si]], bias=bias_ap)
        return ev

    def xr(self, k, si, off, w):
        return self.xn_ap(k, off, w), [self.xn_b[k][si]]

    def in_pair_blocks(self, slot, u, tl):
        blocks = []
        for j in range(4):
            mc = 4 * u + j
            tmps = {}

            def ev_gate(si, off, w, pap, pb, mc=mc, tmps=tmps):
                tap, tb = self.tmp_unit()
                tmps[si] = (tap, tb)
                self.act(tap[:, 0:w], pap, AF.Sigmoid, [pb], [tb], bias=self.cs("b_in_g", mc))

            def ev_val(si, off, w, pap, pb, mc=mc, tmps=tmps):
                tap, tb = tmps[si]
                o, bv = self.uT_ap(mc, UH + off, w), self.cs("b_in_v", mc)
                self.dve((lambda e, o=o, pap=pap, bv=bv, t=tap[:, 0:w]:
                          e.scalar_tensor_tensor(out=o, in0=pap, scalar=bv, in1=t, op0=ALU.add, op1=ALU.mult)),
                         [pb, tb, self.const_b], [self.uT_b[mc]])
            blocks.append(((2 * j + 1) * 1024, ev_gate))
            blocks.append(((2 * j) * 1024, ev_val))
        return blocks

    def mixer0_first_blocks(self, tl):
        u3 = self.r44[:, 0:NCH * self.UTW].rearrange("p (c t) -> p c t", c=NCH)
        h3 = self.uhalo[:, :].rearrange("p (c t) -> p c t", c=NCH)
        self.S.op("pool", (lambda e, o=u3[:, :, 0:UH], i0=h3: e.tensor_copy(out=o, in_=i0)),
                  reads=[self.uhalo_b], writes=list(self.uT_b))
        slot = self.w_get("in", 0)
        return [self.gemm_blk(slot, e0, 8, self.xr, tl, ev) for (e0, ev) in self.in_pair_blocks(slot, 0, tl)]

    def mixer0_rest(self, tl):
        subs = tl["subs"]
        W = tl["W"]
        ns = len(subs)
        u3 = self.r44[:, 0:NCH * self.UTW].rearrange("p (c t) -> p c t", c=NCH)
        h3 = self.uhalo[:, :].rearrange("p (c t) -> p c t", c=NCH)
        slot = self.w_get("in", 1)
        for (e0, ev) in self.in_pair_blocks(slot, 1, tl):
            self.gemm(slot, e0, 8, self.xr, subs, ev)
        if tl["halo"]:
            f = self.cs("flag")
            o = u3[:, :, UH:UH + W]
            self.dve((lambda e, o=o, f=f: e.tensor_scalar(out=o, in0=o, scalar1=f, scalar2=None, op0=ALU.mult)),
                     list(self.uT_b) + [self.const_b], list(self.uT_b))
        self.S.op("pool", (lambda e, o=h3, i0=u3[:, :, W:W + UH]: e.tensor_copy(out=o, in_=i0)),
                  reads=list(self.uT_b), writes=[self.uhalo_b])
        dslot = 0
        for c in range(NCH):
            banks = [self.next_bank() for _ in subs]
            for tg in range((CONVW + DT - 1) // DT):
                taps = list(range(tg * DT, min((tg + 1) * DT, CONVW)))
                nt = len(taps)
                ds = dslot % 2
                dslot += 1
                o3 = self.diag[:, ds * DT * P:(ds * DT + nt) * P].rearrange("p (j m) -> p j m", j=nt)
                i0 = self.ident[:, :].unsqueeze(1).to_broadcast([P, nt, P])
                i1 = self.cs("dw_w", c * CONVW + taps[0], nt).unsqueeze(2).to_broadcast([P, nt, P])
                self.S.op("pool", (lambda e, o3=o3, i0=i0, i1=i1: e.tensor_tensor(out=o3, in0=i0, in1=i1, op=ALU.mult)),
                          reads=[self.const_b], writes=[self.diag_b[ds]])
                for jj, tap in enumerate(taps):
                    lhs = self.diag[:, (ds * DT + jj) * P:(ds * DT + jj + 1) * P]
                    for si, (off, w) in enumerate(subs):
                        rhs = self.uT_ap(c, off + 2 + tap, w)
                        wr = [self.ps_b[b] for b in banks] if (tap == 0 and si == 0) else [self.ps_b[banks[si]]]
                        self.mm(self.ps[banks[si]][:, 0:w], lhs, rhs, tap == 0, tap == CONVW - 1,
                                reads=[self.diag_b[ds], self.uT_b[c]], writes=wr,
                                inc=(jj == nt - 1 and si == len(subs) - 1))
            for si, (off, w) in enumerate(subs):
                pap, pb = self.ps[banks[si]][:, 0:w], self.ps_b[banks[si]]
                b = self.cs("dw_b", c)
                self.act(self.a_ap(c, off, w), pap, AF.Identity, [pb], [self.a32_b[c][si]], bias=b)
                self.act(self.sq_ap(c, off, w), pap, AF.Square, [pb], [self.sq_b[c][si]], bias=b)
                self.act(self.xn_ap(c, off, w), pap, AF.Identity, [pb], [self.xn_b[c][si]], bias=b)

        def ln_stats(si):
            off, w = subs[si]
            for c in range(NCH):
                self.mm(self.ps[6][:, 0:w], self.onesm[:, :], self.xn_ap(c, off, w), c == 0, c == NCH - 1,
                        reads=[self.xn_b[c][si], self.const_b], writes=[self.ps_b[6]], inc=(c == NCH - 1))
            for c in range(NCH):
                self.mm(self.ps[7][:, 0:w], self.onesm[:, :], self.sq_ap(c, off, w), c == 0, c == NCH - 1,
                        reads=[self.sq_b[c][si], self.const_b], writes=[self.ps_b[7]], inc=(c == NCH - 1))
            m = self.st[:, 2 * SW + off: 2 * SW + off + w]
            r = self.st[:, off:off + w]
            mb, rb = self.st_b[2 + si], self.st_b[si]
            self.dve((lambda e, o=m, i=self.ps[6][:, 0:w]: e.tensor_copy(out=o, in_=i)), [self.ps_b[6]], [mb])
            self.dve((lambda e, o=r, i=m: e.tensor_tensor(out=o, in0=i, in1=i, op=ALU.mult)), [mb], [rb])
            self.dve((lambda e, o=r, i=self.ps[7][:, 0:w], ep=self.cs("eps"):
                      e.scalar_tensor_tensor(out=o, in0=i, scalar=ep, in1=o, op0=ALU.add, op1=ALU.subtract)),
                     [self.ps_b[7], rb, self.const_b], [rb])
            self.rstd_act(r, [rb], si, off, w, add_eps=False)

        def ln_apply(si, c):
            off, w = subs[si]
            a = self.a_ap(c, off, w)
            M = self.st[:, 2 * SW + off: 2 * SW + off + w]
            R = self.st[:, off:off + w]
            self.dve((lambda e, a=a, M=M: e.tensor_tensor(out=a, in0=a, in1=M, op=ALU.subtract)),
                     [self.a32_b[c][si], self.st_b[2 + si]], [self.a32_b[c][si]])
            self.dve((lambda e, a=a, R=R: e.tensor_tensor(out=a, in0=a, in1=R, op=ALU.mult)),
                     [self.a32_b[c][si], self.st_b[si]], [self.a32_b[c][si]])
            self.act(self.xn_ap(c, off, w), a, AF.Silu, [self.a32_b[c][si]], [self.xn_b[c][si]],
                     bias=self.cs("ln_b", c), scale=self.cs("ln_g", c))
        ln_stats(0)
        for c in range(NCH):
            ln_apply(0, c)
        slot = self.w_get("out", 0)
        blocks = [self.gemm_blk(slot, mc * 1024, 8, self.xr, tl, self.evac_f(mc, self.cs("b_out", mc)))
                  for mc in range(8)]
        mid = (lambda: self.post(tl, 0, "g_mix_post0", True))
        late = (lambda: self.pre(tl, 0, "g_ffn_pre0"))
        if ns == 1:
            self.run_first_unit(1, [], None, blocks, mid=mid, late=late)
        else:
            ln_stats(1)
            self.run_first_unit(ns, [(lambda c=c: ln_apply(1, c)) for c in range(NCH)], None, blocks,
                                mid=mid, late=late)

    def gu_blocks(self, slot, u, tl):
        p0, p1 = GU_UNITS[u]
        blocks = []
        for i in range(p0, p1):
            li = i - p0
            tmps = {}

            def ev_gate(si, off, w, pap, pb, tmps=tmps):
                tap, tb = self.tmp_unit()
                tmps[si] = (tap, tb)
                self.act(tap[:, 0:w], pap, AF.Silu, [pb], [tb])

            def ev_up(si, off, w, pap, pb, i=i, tmps=tmps):
                tap, tb = tmps[si]
                o = self.aT_ap(i, off, w)
                self.dve((lambda e, o=o, pap=pap, t=tap[:, 0:w]:
                          e.tensor_tensor(out=o, in0=pap, in1=t, op=ALU.mult)),
                         [pb, tb], [self.aT_b[i][si]])
            blocks.append(((2 * li) * 1024, ev_gate))
            blocks.append(((2 * li + 1) * 1024, ev_up))
        return blocks

    def ffn_first_blocks(self, tl, l):
        slot = self.w_get(f"gu{l}", 0)
        return [self.gemm_blk(slot, e0, 8, self.xr, tl, ev) for (e0, ev) in self.gu_blocks(slot, 0, tl)]

    def store_out(self, tl, si):
        off, w = tl["subs"][si]
        t0 = tl["t0"] - HALO
        dst = self.outT[:, NCH * (t0 + off): NCH * (t0 + off + w)]
        self.S.dma("act", (lambda e, dst=dst, src=self.h_sub(si, w): e.dma_start(out=dst, in_=src)), f"d:o{si}",
                   reads=[self.hT_b[c][si] for c in range(NCH)])

    def ffn_rest(self, tl, l, final=False):
        subs = tl["subs"]
        for u in range(1, len(GU_UNITS)):
            slot = self.w_get(f"gu{l}", u)
            for (e0, ev) in self.gu_blocks(slot, u, tl):
                self.gemm(slot, e0, 8, self.xr, subs, ev)
        ar = lambda k, si, off, w: (self.aT_ap(k, off, w), [self.aT_b[k][si]])
        for u in range(4):
            slot = self.w_get(f"dn{l}", u)
            if u == 3:
                blocks = [self.gemm_blk(slot, j * NF * P, NF, ar, tl, self.evac_f(2 * u + j, self.cs("zero")))
                          for j in range(2)]
                if final:
                    for si in range(len(subs)):
                        for b in blocks:
                            b(si)
                        self.post(tl, si, "g_ffn_post1", False)
                        self.store_out(tl, si)
                else:
                    for b in blocks:
                        b(0)
                    self.post(tl, 0, f"g_ffn_post{l}", True)
                    if len(subs) > 1:
                        for b in blocks:
                            b(1)
                continue
            for j in range(2):
                mc = 2 * u + j
                self.gemm(slot, j * NF * P, NF, ar, subs, self.evac_f(mc, self.cs("zero")))

    def mixer1_first_blocks(self, tl):
        S = self.S
        for j in range(2):
            o, i0 = self.kT_ap(j, 0, P), self.kprev[:, j * P:(j + 1) * P]
            S.op("pool", (lambda e, o=o, i0=i0: e.tensor_copy(out=o, in_=i0)),
                 reads=[self.kprev_b], writes=[self.kT_b[j][0]])
        o, i0 = self.V_ap(0), self.vprev[:, :]
        S.op("pool", (lambda e, o=o, i0=i0: e.tensor_copy(out=o, in_=i0)),
             reads=[self.vprev_b], writes=[self.V_b[0]])
        if tl["halo"]:
            return []
        slot = self.w_get("q", 0)
        blocks = []
        for cq in range(8):
            def ev_q(si, off, w, pap, pb, cq=cq):
                self.act(self.qT_ap(cq, off, w), pap, AF.Identity, [pb], [self.qT_b[cq][si]],
                         bias=self.cs("b_q", cq))
            blocks.append(self.gemm_blk(slot, cq * 1024, 8, self.xr, tl, ev_q))
        return blocks

    def mixer1_rest(self, tl):
        subs = tl["subs"]
        W = tl["W"]
        nb = W // P
        S = self.S
        slot = self.w_get("kv", 0)
        for j in range(2):
            def ev_k(si, off, w, pap, pb, j=j):
                blks = [self.kT_b[j][1 + (off + x) // P] for x in range(0, w, P)]
                self.act(self.kT_ap(j, P + off, w), pap, AF.Identity, [pb], blks, bias=self.cs("b_k", j))
            self.gemm(slot, j * 1024, 8, self.xr, subs, ev_k)
        for b in range(nb):
            bk = self.next_bank()
            si = (b * P) // SW if not tl["halo"] else 0
            for k in range(NCH):
                self.mm(self.ps[bk][:, 0:256], self.xn_ap(k, b * P, P), self.w_ap(slot, 2048 + k * 256, 256),
                        k == 0, k == NCH - 1, reads=[self.w_b[slot], self.xn_b[k][si]],
                        writes=[self.ps_b[bk]], inc=(k == NCH - 1))
            o, pap, bv = self.V_ap(1 + b), self.ps[bk][:, 0:256], self.cs("b_v", 0, 256)
            self.dve((lambda e, o=o, pap=pap, bv=bv: e.tensor_tensor(out=o, in0=pap, in1=bv, op=ALU.add)),
                     [self.ps_b[bk], self.const_b], [self.V_b[1 + b]])

        if not tl["halo"]:
            iters = [(b, pp) for b in range(nb) for pp in range(2)]
            state = {}

            def stage_a(it):
                b, pp = iters[it]
                si = (b * P) // SW
                pts = []
                for hh in range(2):
                    rows = slice(hh * 64, hh * 64 + 64)
                    for kb in range(2):
                        bank = hh * 2 + kb
                        q3 = self.r44[rows, self.QO + pp * 4 * TT: self.QO + (pp + 1) * 4 * TT] \
                            .rearrange("p (c t) -> p c t", c=4)[:, :, b * P:(b + 1) * P]
                        o3 = self.ps[bank][:, :].rearrange("p (g q) -> p g q", g=4)
                        self.mm(o3, self.kT_ap(pp, (b + kb) * P, P, rows=rows), q3, True, False,
                                reads=[self.kT_b[pp][b + kb]] + [self.qT_b[pp * 4 + g][si] for g in range(4)],
                                writes=[self.ps_b[bank]], inc=False, tp=(hh * 64, 0))
                for hh in range(2):
                    for kb in range(2):
                        bank = hh * 2 + kb
                        h0 = 4 * (2 * pp + hh)
                        o3 = self.ps[bank][:, :].rearrange("p (g q) -> p g q", g=4)
                        b3 = self.bhi[:, :].rearrange("p (h k q) -> p h k q", h=NQH, k=2)[:, h0:h0 + 4, kb, :]
                        self.mm(o3, self.ident[:, :], b3, False, True, reads=[self.const_b],
                                writes=[self.ps_b[bank]], inc=True)
                        pc, psi = (it * 4 + bank) % 8, 1
                        pt = self.a_ap(pc, psi * SW, SW).bitcast(BF16)[:, 0:SW]
                        ptb = self.a32_b[pc][psi]
                        self.act(pt, self.ps[bank][:, :], AF.Exp, [self.ps_b[bank]], [ptb], scale=0.125)
                        if tl["idx"] == 1 and b == 0 and kb == 0:
                            f = self.cs("flag")
                            self.dve((lambda e, pt=pt, f=f:
                                      e.tensor_scalar(out=pt, in0=pt, scalar1=f, scalar2=None, op0=ALU.mult)),
                                     [ptb, self.const_b], [ptb])
                        pts.append((hh, kb, pt, ptb))
                state[it] = pts

            def stage_b(it):
                b, pp = iters[it]
                si = (b * P) // SW
                pts = state.pop(it)
                bo = 4 + 2 * (it % 2)
                bd = bo + 1
                for (hh, kb, pt, ptb) in pts:
                    vcol = (2 * pp + hh) * 64
                    self.mm(self.ps[bo][hh * 64:hh * 64 + 64, :], self.V_ap(b + kb, vcol, 64), pt,
                            kb == 0, kb == 1, reads=[self.V_b[b + kb], ptb], writes=[self.ps_b[bo]],
                            inc=(hh == 1 and kb == 1), tp=(0, hh * 64))
                for (hh, kb, pt, ptb) in pts:
                    self.mm(self.ps[bd][hh * 64:hh * 64 + 64, :], self.ones1[:, 0:64], pt,
                            kb == 0, kb == 1, reads=[ptb, self.const_b], writes=[self.ps_b[bd]],
                            inc=(hh == 1 and kb == 1), tp=(0, hh * 64))
                rc, rcb = self.st[:, (2 + it % 2) * SW:(3 + it % 2) * SW], self.st_b[2 + it % 2]
                for g in range(4):
                    o, i0, sk = rc[:, g * P:(g + 1) * P], self.ps[bd][:, g * P:(g + 1) * P], \
                        self.exps[:, pp * 4 + g: pp * 4 + g + 1]
                    self.dve((lambda e, o=o, i0=i0, sk=sk:
                              e.tensor_scalar(out=o, in0=i0, scalar1=sk, scalar2=None, op0=ALU.add)),
                             [self.ps_b[bd], self.const_b], [rcb])
                self.act(rc, rc, AF.Ln, [rcb], [rcb])
                self.act(rc, rc, AF.Exp, [rcb], [rcb], scale=-1.0)
                o3 = self.r44[:, self.AO + pp * 4 * TT: self.AO + (pp + 1) * 4 * TT] \
                    .rearrange("p (c t) -> p c t", c=4)[:, :, b * P:(b + 1) * P]
                i3 = self.ps[bo][:, :].rearrange("p (g q) -> p g q", g=4)
                r3 = rc.rearrange("p (g q) -> p g q", g=4)
                self.dve((lambda e, o3=o3, i3=i3, r3=r3: e.tensor_tensor(out=o3, in0=i3, in1=r3, op=ALU.mult)),
                         [self.ps_b[bo], rcb], [self.at_b[pp * 4 + g][si] for g in range(4)])

            n = len(iters)
            stage_a(0)
            for it in range(1, n):
                stage_a(it)
                stage_b(it - 1)
            stage_b(n - 1)
        for j in range(2):
            o, i0 = self.kprev[:, j * P:(j + 1) * P], self.kT_ap(j, W, P)
            S.op("pool", (lambda e, o=o, i0=i0: e.tensor_copy(out=o, in_=i0)),
                 reads=[self.kT_b[j][nb]], writes=[self.kprev_b])
        o, i0 = self.vprev[:, :], self.V_ap(nb)
        S.op("pool", (lambda e, o=o, i0=i0: e.tensor_copy(out=o, in_=i0)),
             reads=[self.V_b[nb]], writes=[self.vprev_b])
        if tl["halo"]:
            return
        ar = lambda k, si, off, w: (self.at_ap(k, off, w), [self.at_b[k][si]])
        slot = self.w_get("o", 0)
        blocks = [self.gemm_blk(slot, mc * 1024, 8, ar, tl, self.evac_f(mc, self.cs("b_o", mc))) for mc in range(8)]
        for b in blocks:
            b(0)
        self.post(tl, 0, "g_mix_post1", True)
        for b in blocks[:6]:
            b(1)
        self.pre(tl, 0, "g_ffn_pre1")
        for b in blocks[6:]:
            b(1)

    def h_sub(self, si, w):
        blk = self.hT[:, si * NCH * SW:(si + 1) * NCH * SW]
        if w == SW:
            return blk
        return blk.rearrange("p (c t) -> p c t", c=NCH)[:, :, 0:w]

    def load_x(self, tl):
        subs, t0 = tl["subs"], tl["t0"]
        for si, (off, w) in enumerate(subs):
            src = self.xT[:, NCH * (t0 + off): NCH * (t0 + off + w)]
            if w != SW:
                src = src.rearrange("p (c t) -> p c t", c=NCH)
            self.S.dma("sp", (lambda e, dst=self.h_sub(si, w), src=src: e.dma_start(out=dst, in_=src)), f"d:x{si}",
                       writes=[self.hT_b[c][si] for c in range(NCH)])

    def layout(self, which):
        if which == "A":
            return list(self.uT_b)
        if which == "B":
            return [b for row in self.aT_b for b in row]
        return [b for row in self.qT_b for b in row] + [b for row in self.at_b for b in row] + \
               [b for row in self.kT_b for b in row] + list(self.V_b)

    def handoff(self, old, new):
        bufs = []
        for k in old + new:
            bufs += self.layout(k)
        self.S.op("pool", (lambda e: e.tensor_copy(out=self.dummy[:, 0:8], in_=self.dummy[:, 8:16])),
                  reads=[], writes=bufs + [self.dummy_b])

    def emit_all(self):
        S = self.S
        nc = self.nc
        S.dma("sp", (lambda e: e.dma_start(out=self.consts[:, :], in_=self.consts_d)), "d:c0", writes=[self.const_b])
        bias_b = Buf("biasld")
        self.load_x(self.tiles()[0])
        self.identf = self.a32[:, 0:P]
        self.maskT = self.a32[:, P:3 * P]
        self.biasT = self.a32[:, 4 * P: 4 * P + NQH * 2 * P]
        alla = [b for row in self.a32_b for b in row]
        S.dma("sp", (lambda e: e.dma_start(out=self.biasT, in_=self.biasT_d)), "d:c1", writes=[bias_b] + alla)
        S.dma("sp", (lambda e: e.dma_start(out=self.maskT, in_=self.maskT_d)), "d:c2", writes=[bias_b] + alla)
        S.op("pool", (lambda e: e.memset(self.onesm[:, :], 1.0 / D)), writes=[self.const_b], reads=[])
        S.op("pool", (lambda e: e.memset(self.ones1[:, :], 1.0)), writes=[self.const_b], reads=[])
        S.op("pool", (lambda e: e.memset(self.uhalo[:, :], 0.0)), writes=[self.uhalo_b])
        S.op("pool", (lambda e: e.memset(self.kprev[:, :], 0.0)), writes=[self.kprev_b])
        S.op("pool", (lambda e: e.memset(self.vprev[:, :], 0.0)), writes=[self.vprev_b])
        S.dma("sp", (lambda e: e.dma_start(out=self.identf, in_=self.ident_d)), "d:c3",
              writes=[self.const_b] + alla)
        S.op("pool", (lambda e: e.tensor_copy(out=self.ident[:, :], in_=self.identf)),
             reads=[self.const_b] + alla, writes=[self.const_b])
        self.act(self.exps[:, :], self.cs("sinks", 0, 8), AF.Exp, [], [self.const_b])
        for h in range(NQH):
            o = self.biasT[:, h * 2 * P:(h + 1) * 2 * P]
            self.dve((lambda e, o=o: e.tensor_tensor(out=o, in0=o, in1=self.maskT, op=ALU.add)),
                     [bias_b] + alla, [bias_b] + alla)
        self.dve((lambda e: e.tensor_scalar(out=self.biasT, in0=self.biasT, scalar1=8.0, scalar2=None, op0=ALU.mult)),
                 [bias_b] + alla, [bias_b] + alla)
        self.dve((lambda e: e.tensor_copy(out=self.bhi[:, :], in_=self.biasT)), [bias_b] + alla, [self.const_b])

        S.op("pool", (lambda e: e.memset(self.dummy[:, :], 0.0)), writes=[self.dummy_b])
        for tl in self.tiles():
            subs, W, t0 = tl["subs"], tl["W"], tl["t0"]
            if tl["idx"] > 0:
                self.load_x(tl)
            nsub = len(subs)
            if tl["idx"] > 0:
                self.handoff(["B", "C"], ["A"])
            for si in range(nsub):
                self.sq_h(tl, si)
            self.pre(tl, 0, "g_mix_pre0")
            blocks = self.mixer0_first_blocks(tl)
            self.run_first_unit(nsub, [], (lambda: self.pre(tl, 1, "g_mix_pre0")) if nsub > 1 else None, blocks)
            self.mixer0_rest(tl)
            self.handoff(["A"], ["B"])
            self.boundary(tl, "g_mix_post0", "g_ffn_pre0", self.ffn_first_blocks(tl, 0), post0_done=True, pre0_done=True)
            self.ffn_rest(tl, 0)
            self.handoff(["B"], ["C"])
            self.boundary(tl, "g_ffn_post0", "g_mix_pre1", self.mixer1_first_blocks(tl), post0_done=True)
            self.mixer1_rest(tl)
            if tl["halo"]:
                continue
            self.handoff(["C"], ["B"])
            self.boundary(tl, "g_mix_post1", "g_ffn_pre1", self.ffn_first_blocks(tl, 1), post0_done=True, pre0_done=True)
            self.ffn_rest(tl, 1, final=True)
        assert self.wnext == len(self.plan), (self.wnext, len(self.plan))

    def finalize(self, es):
        nc = self.nc
        S = self.S
        names = list(S.engs.keys()) + sorted(S.dcnt.keys())
        sems = {n: es.enter_context(nc.semaphore(n.replace(":", "_"))) for n in names}
        final_waits = [(n, v) for n, v in S.dcnt.items() if n.startswith("d:o")]

        def replay(e, ops, tail=()):
            for waits, fn, incsem, incval in ops:
                for sname, v in waits[1:]:
                    e.wait_ge(sems[sname], v)
                ins = fn(e)
                if waits:
                    ins._wait_ge(sems[waits[0][0]], waits[0][1])
                if incsem is not None:
                    ins.then_inc(sems[incsem], incval)
            for sname, v in tail:
                e.wait_ge(sems[sname], v)

        with nc.Block() as block:
            @block.tensor
            def _(e):
                replay(e, S.engs["pe"].ops)

            @block.scalar
            def _(e):
                replay(e, S.engs["act"].ops)

            @block.vector
            def _(e):
                replay(e, S.engs["dve"].ops)

            @block.gpsimd
            def _(e):
                replay(e, S.engs["pool"].ops)

            @block.sync
            def _(e):
                replay(e, S.engs["sp"].ops, tail=final_waits)


def _vec8(v):
    return np.ascontiguousarray(np.asarray(v, np.float32).reshape(NCH, P).T)


def _blk(Wm, cols_list):
    K = Wm.shape[0]
    kc = K // P
    out = np.empty((P, len(cols_list), kc, P), np.float32)
    for j, cols in enumerate(cols_list):
        out[:, j] = Wm[:, cols].reshape(kc, P, P).transpose(1, 0, 2)
    return out.reshape(P, -1)


def _t5_bucket(dist):
    dist = np.maximum(dist, 0)
    max_exact = 16
    large = max_exact + (np.log(np.maximum(dist, 1).astype(np.float32) / np.float32(max_exact))
                         / np.float32(np.log(128.0 / max_exact)) * np.float32(32 - max_exact)).astype(np.int32)
    large = np.minimum(large, 31)
    return np.where(dist < max_exact, dist, large)


def _qhead(cq, half):
    pp, g = cq // 4, cq % 4
    return 4 * (2 * pp + half) + g


def _prep_shared(inp):
    f = lambda k: np.asarray(inp[k], np.float32)
    consts = np.zeros((P, NCONST), np.float32)

    def put(name, arr):
        arr = np.asarray(arr, np.float32)
        consts[:, _CL[name]:_CL[name] + arr.shape[1]] = arr
    for l in range(2):
        put(f"g_mix_pre{l}", _vec8(f("mix_pre_g")[l]))
        put(f"g_mix_post{l}", _vec8(f("mix_post_g")[l]))
        put(f"g_ffn_pre{l}", _vec8(f("ffn_pre_g")[l]))
        put(f"g_ffn_post{l}", _vec8(f("ffn_post_g")[l]))
    b_in = f("conv_b_in")[0]
    put("b_in_v", _vec8(b_in[:D]))
    put("b_in_g", _vec8(b_in[D:]))
    put("dw_b", _vec8(f("conv_dw_b")[0]))
    put("ln_g", _vec8(f("conv_ln_g")[0]))
    put("ln_b", _vec8(f("conv_ln_b")[0]))
    put("b_out", _vec8(f("conv_b_out")[0]))
    dww = f("conv_dw_w")[0]
    put("dw_w", dww.T.reshape(NCH, P, CONVW).transpose(1, 0, 2).reshape(P, NCH * CONVW))
    bqkv = f("attn_b_qkv")[0]
    qcols = [np.concatenate([_qhead(cq, 0) * 64 + np.arange(64), _qhead(cq, 1) * 64 + np.arange(64)])
             for cq in range(8)]
    put("b_q", np.stack([bqkv[c] for c in qcols], axis=1))
    put("b_k", np.stack([bqkv[D + j * P: D + (j + 1) * P] for j in range(2)], axis=1))
    put("b_o", _vec8(f("attn_b_o")[0]))
    sinks = f("attn_sinks")[0]
    sk = np.zeros((P, 8), np.float32)
    for cq in range(8):
        sk[:64, cq] = sinks[_qhead(cq, 0)]
        sk[64:, cq] = sinks[_qhead(cq, 1)]
    put("sinks", sk)
    consts[:, _CL["eps"]] = EPS
    put("b_v", np.broadcast_to(bqkv[D + 256: D + 512][None, :], (P, 256)))

    s_i = np.arange(P)[:, None]
    q_i = np.arange(P)[None, :]
    dist = np.stack([q_i + P - s_i, q_i - s_i], axis=0)
    valid = (dist >= 0) & (dist < P)
    bucket = _t5_bucket(dist)
    rel = f("rel_bias")
    biasT = rel[bucket]
    biasT = np.ascontiguousarray(biasT.transpose(1, 3, 0, 2)).reshape(P, NQH * 2 * P)
    maskT = np.where(valid, np.float32(0.0), np.float32(NEG)).astype(np.float32)
    maskT = np.ascontiguousarray(maskT.transpose(1, 0, 2)).reshape(P, 2 * P)

    ar = np.arange(P)
    w_in = f("conv_w_in")[0]
    w_out = f("conv_w_out")[0]
    wqkv = f("attn_w_qkv")[0]
    wo = f("attn_w_o")[0]
    orow = np.concatenate(qcols)
    wo_p = wo[orow, :]
    in_units = []
    for u in range(2):
        cl = []
        for j in range(4):
            mc = 4 * u + j
            cl += [mc * P + ar, D + mc * P + ar]
        in_units.append(_blk(w_in, cl))
    wv_blk = np.ascontiguousarray(wqkv[:, D + 256: D + 512].reshape(NCH, P, 256).transpose(1, 0, 2)).reshape(P, 2048)
    sh = {
        "consts": consts, "biasT": biasT, "maskT": maskT, "ident": np.eye(P, dtype=np.float32),
        "w_in": np.concatenate(in_units, 0),
        "w_out": _blk(w_out, [mc * P + ar for mc in range(8)]),
        "w_q": _blk(wqkv, qcols),
        "w_kv": np.concatenate([_blk(wqkv, [D + ar, D + P + ar]), wv_blk], axis=1),
        "w_o": _blk(wo_p, [mc * P + ar for mc in range(8)]),
    }
    for l in range(2):
        wgu = f("ffn_w_gate_up")[l]
        wdn = f("ffn_w_down")[l]
        gus = []
        for (p0, p1) in GU_UNITS:
            cl = []
            for i in range(p0, p1):
                cl += [i * P + ar, DFF + i * P + ar]
            blk = np.zeros((P, 8192), np.float32)
            blk[:, :len(cl) * 1024] = _blk(wgu, cl)
            gus.append(blk)
        sh[f"w_gu{l}"] = np.concatenate(gus, 0)
        sh[f"w_dn{l}"] = np.concatenate([_blk(wdn, [2 * u * P + ar, (2 * u + 1) * P + ar]) for u in range(4)], 0)
    return sh


_PROG_CACHE = {}


def kernel(**inputs):
    x = np.asarray(inputs["x"], np.float32)
    sh = _prep_shared(inputs)
    in_maps = []
    for core in range(NCORES):
        b, half = core // 2, core % 2
        start = half * TOK
        xl = np.zeros((TLOC, D), np.float32)
        xl[HALO:] = x[b, start:start + TOK]
        if half == 1:
            xl[:HALO] = x[b, start - HALO:start]
        x3 = xl.T.reshape(NCH, P, TLOC).transpose(1, 0, 2)
        xT = np.concatenate([x3[:, :, tl["t0"] + off: tl["t0"] + off + w].reshape(P, -1)
                             for tl in Prog.tiles() for (off, w) in tl["subs"]], axis=1)
        xT = np.ascontiguousarray(xT)
        m = dict(sh)
        c = sh["consts"].copy()
        c[:, _CL["flag"]] = 1.0 if half == 1 else 0.0
        m["consts"] = c
        m["xT"] = xT
        in_maps.append(m)
    if "nc" not in _PROG_CACHE:
        _PROG_CACHE["nc"] = Prog().build()
    res = run_bass_kernel_spmd(_PROG_CACHE["nc"], in_maps, core_ids=list(range(NCORES)))
    out = np.empty((BATCH, SEQ, D), np.float32)
    for core in range(NCORES):
        b, half = core // 2, core % 2
        oT = np.asarray(res.results[core]["outT"]).reshape(P, TOK // SW, NCH, SW)
        out[b, half * TOK:(half + 1) * TOK] = oT.transpose(1, 3, 2, 0).reshape(TOK, D)
    return out
```

```python
import numpy as np
from contextlib import ExitStack
import concourse.bass as bass
import concourse.mybir as mybir
from concourse.bass_utils import run_bass_kernel_spmd

F32 = mybir.dt.float32
BF16 = mybir.dt.bfloat16
AF = mybir.ActivationFunctionType
ALU = mybir.AluOpType

P = 128
D = 1024
NCH = 8
DFF = 2816
NF = 22
SEQ = 8192
BATCH = 4
NCORES = 8
TOK = 4096
HALO = 256
TLOC = TOK + HALO
TT = 1024
SW = 512
CONVW = 31
UH = 32
DT = 16
NQH = 16
EPS = 1e-6
NEG = -30000.0
NSLOT = 2
SLOT_E = 8192

GU_UNITS = [(0, 4), (4, 8), (8, 12), (12, 16), (16, 19), (19, 22)]

_CL = {}
_off = 0
for _n, _w in [("g_mix_pre0", 8), ("g_mix_post0", 8), ("g_ffn_pre0", 8), ("g_ffn_post0", 8),
               ("g_mix_pre1", 8), ("g_mix_post1", 8), ("g_ffn_pre1", 8), ("g_ffn_post1", 8),
               ("b_in_v", 8), ("b_in_g", 8), ("dw_b", 8), ("ln_g", 8), ("ln_b", 8), ("b_out", 8),
               ("dw_w", 8 * CONVW), ("b_q", 8), ("b_k", 2), ("b_o", 8), ("flag", 1), ("sinks", 8),
               ("eps", 1), ("zero", 1), ("b_v", 256)]:
    _CL[_n] = _off
    _off += _w
NCONST = _off


class Buf:
    __slots__ = ("name", "w", "r")

    def __init__(self, name):
        self.name = name
        self.w = None
        self.r = {}


class _Eng:
    def __init__(self, name):
        self.name = name
        self.cnt = 0
        self.ops = []
        self.waited = {}


class Sched:
    def __init__(self):
        self.engs = {k: _Eng(k) for k in ("pe", "act", "dve", "pool", "sp")}
        self.dcnt = {}

    def _waits(self, e, reads, writes):
        deps = {}
        for b in reads:
            if b.w is not None and deps.get(b.w[0], 0) < b.w[1]:
                deps[b.w[0]] = b.w[1]
        for b in writes:
            if b.w is not None and deps.get(b.w[0], 0) < b.w[1]:
                deps[b.w[0]] = b.w[1]
            for s, v in b.r.items():
                if deps.get(s, 0) < v:
                    deps[s] = v
        waits = []
        for s, v in deps.items():
            if e.name == "pe" and s == "pe":
                continue
            if e.waited.get(s, 0) >= v:
                continue
            if s == e.name:
                assert v <= e.cnt, (e.name, v, e.cnt)
            e.waited[s] = v
            waits.append((s, v))
        return waits

    def op(self, eng, fn, reads=(), writes=(), inc=True):
        e = self.engs[eng]
        assert inc or eng == "pe"
        waits = self._waits(e, reads, writes)
        ev = (eng, e.cnt + 1)
        e.ops.append((waits, fn, eng if inc else None, 1))
        if inc:
            e.cnt += 1
        for b in reads:
            if b.r.get(eng, 0) < ev[1]:
                b.r[eng] = ev[1]
        for b in writes:
            b.w = ev
            b.r = {}

    def dma(self, eng, fn, dsem, reads=(), writes=()):
        e = self.engs[eng]
        waits = self._waits(e, reads, writes)
        self.dcnt[dsem] = self.dcnt.get(dsem, 0) + 16
        ev = (dsem, self.dcnt[dsem])
        e.ops.append((waits, fn, dsem, 16))
        for b in reads:
            if b.r.get(dsem, 0) < ev[1]:
                b.r[dsem] = ev[1]
        for b in writes:
            b.w = ev
            b.r = {}


class Prog:
    def __init__(self):
        self.nc = bass.Bass("TRN2", target_bir_lowering=False)
        self.S = Sched()
        self.bank_rr = 0
        self.tmp_rr = 0
        self.wnext = 0
        self.wissued = 0

    @staticmethod
    def tiles():
        t = [dict(idx=0, t0=0, W=HALO, subs=[(0, HALO)], halo=True)]
        for i in range(TOK // TT):
            t.append(dict(idx=i + 1, t0=HALO + i * TT, W=TT,
                          subs=[(s * SW, SW) for s in range(TT // SW)], halo=False))
        return t

    def weight_plan(self):
        plan = []
        for tl in self.tiles():
            plan += [("in", 0), ("in", 1), ("out", 0)]
            plan += [("gu0", u) for u in range(len(GU_UNITS))]
            plan += [("dn0", u) for u in range(4)]
            if tl["halo"]:
                plan.append(("kv", 0))
                continue
            plan += [("q", 0), ("kv", 0), ("o", 0)]
            plan += [("gu1", u) for u in range(len(GU_UNITS))]
            plan += [("dn1", u) for u in range(4)]
        return plan

    def unit_E(self, kind, idx):
        if kind.startswith("gu"):
            a, b = GU_UNITS[idx]
            return (b - a) * 2048
        return {"in": 8192, "out": 8192, "dn0": 5632, "dn1": 5632, "q": 8192, "kv": 4096, "o": 8192}[kind]

    def build(self):
        nc = self.nc
        S = self.S
        es = ExitStack()
        with es:
            def dram(name, shape, dt=F32, kind="ExternalInput"):
                return nc.dram_tensor(name, shape, dt, kind=kind).ap()

            self.xT = dram("xT", [P, NCH * TLOC])
            self.outT = dram("outT", [P, NCH * TOK], kind="ExternalOutput")
            self.consts_d = dram("consts", [P, NCONST])
            self.biasT_d = dram("biasT", [P, NQH * 2 * P])
            self.maskT_d = dram("maskT", [P, 2 * P])
            self.ident_d = dram("ident", [P, P])
            self.wd = {
                "in": dram("w_in", [2 * P, 8192]), "out": dram("w_out", [P, 8192]),
                "gu0": dram("w_gu0", [6 * P, 8192]), "gu1": dram("w_gu1", [6 * P, 8192]),
                "dn0": dram("w_dn0", [4 * P, 5632]), "dn1": dram("w_dn1", [4 * P, 5632]),
                "q": dram("w_q", [P, 8192]), "kv": dram("w_kv", [P, 4096]), "o": dram("w_o", [P, 8192]),
            }

            def sb(name, shape, dt):
                return es.enter_context(nc.sbuf_tensor(name, shape, dt))

            self.hT = sb("hT", [P, NCH * TT], F32)
            self.xn = sb("xn", [P, NCH * TT], BF16)
            self.sq = sb("sq", [P, NCH * TT], BF16)
            self.a32 = sb("a32", [P, NCH * TT], F32)
            self.r44 = sb("r44", [P, NF * TT], BF16)
            self.uhalo = sb("uhalo", [P, NCH * UH], BF16)
            self.kprev = sb("kprev", [P, 2 * P], BF16)
            self.vprev = sb("vprev", [P, 256], BF16)
            self.wring = sb("wring", [P, NSLOT * SLOT_E], BF16)
            self.diag = sb("diag", [P, 2 * DT * P], BF16)
            self.ident = sb("identb", [P, P], BF16)
            self.onesm = sb("onesm", [P, P], BF16)
            self.ones1 = sb("ones1", [P, P], BF16)
            self.consts = sb("constsb", [P, NCONST], F32)
            self.exps = sb("exps", [P, 8], F32)
            self.bhi = sb("bhi", [P, NQH * 2 * P], BF16)
            self.st = sb("st", [P, 4 * SW], F32)
            self.dummy = sb("dummyt", [P, 16], F32)
            self.dummy_b = Buf("dummy")
            self.ps = [es.enter_context(nc.psum_tensor(f"ps{i}", [P, SW], F32)) for i in range(8)]

            nsub = TT // SW
            self.hT_b = [[Buf(f"hT{c}_{s}") for s in range(nsub)] for c in range(NCH)]
            self.xn_b = [[Buf(f"xn{c}_{s}") for s in range(nsub)] for c in range(NCH)]
            self.sq_b = [[Buf(f"sq{c}_{s}") for s in range(nsub)] for c in range(NCH)]
            self.a32_b = [[Buf(f"a32{c}_{s}") for s in range(nsub)] for c in range(NCH)]
            self.r44_b = Buf("r44")
            self.aT_b = [[Buf(f"aT{i}_{s}") for s in range(nsub)] for i in range(NF)]
            self.uT_b = [Buf(f"uT{c}") for c in range(NCH)]
            self.qT_b = [[Buf(f"qT{c}_{s}") for s in range(nsub)] for c in range(NCH)]
            self.at_b = [[Buf(f"at{c}_{s}") for s in range(nsub)] for c in range(NCH)]
            self.kT_b = [[Buf(f"kT{j}_{b}") for b in range(TT // P + 1)] for j in range(2)]
            self.V_b = [Buf(f"V{b}") for b in range(TT // P + 1)]
            self.uhalo_b = Buf("uhalo")
            self.kprev_b = Buf("kprev")
            self.vprev_b = Buf("vprev")
            self.w_b = [Buf(f"w{i}") for i in range(NSLOT)]
            self.diag_b = [Buf("diag0"), Buf("diag1")]
            self.const_b = Buf("const")
            self.st_b = [Buf(f"st{i}") for i in range(4)]
            self.ps_b = [Buf(f"ps{i}") for i in range(8)]
            self.plan = self.weight_plan()

            self.emit_all()
            self.finalize(es)
        return nc

    def cs(self, name, c=0, n=1):
        o = _CL[name] + c
        return self.consts[:, o:o + n]

    def h_ap(self, c, off, w):
        si, o = off // SW, off % SW
        assert o + w <= SW
        base = si * NCH * SW + c * SW + o
        return self.hT[:, base: base + w]

    def xn_ap(self, c, off, w):
        return self.xn[:, c * TT + off: c * TT + off + w]

    def sq_ap(self, c, off, w):
        return self.sq[:, c * TT + off: c * TT + off + w]

    def a_ap(self, c, off, w):
        return self.a32[:, c * TT + off: c * TT + off + w]

    def aT_ap(self, i, off, w):
        return self.r44[:, i * TT + off: i * TT + off + w]

    UTW = UH + TT

    def uT_ap(self, c, col, w):
        return self.r44[:, c * self.UTW + col: c * self.UTW + col + w]

    QO = 0
    AO = 8 * TT
    KO = 16 * TT
    KW = P + TT
    VO = 16 * TT + 2 * (P + TT)

    def qT_ap(self, c, off, w, rows=slice(0, P)):
        return self.r44[rows, self.QO + c * TT + off: self.QO + c * TT + off + w]

    def at_ap(self, c, off, w):
        return self.r44[:, self.AO + c * TT + off: self.AO + c * TT + off + w]

    def kT_ap(self, j, col, w, rows=slice(0, P)):
        return self.r44[rows, self.KO + j * self.KW + col: self.KO + j * self.KW + col + w]

    def V_ap(self, blk, c0=0, w=256):
        return self.r44[:, self.VO + blk * 256 + c0: self.VO + blk * 256 + c0 + w]

    def w_ap(self, slot, e0, w):
        return self.wring[:, slot * SLOT_E + e0: slot * SLOT_E + e0 + w]

    def tmp_unit(self):
        c = self.tmp_rr % 8
        self.tmp_rr += 1
        return self.a_ap(c, 0, SW), self.a32_b[c][0]

    def next_bank(self):
        b = self.bank_rr % 6
        self.bank_rr += 1
        return b

    def w_issue_upto(self, n):
        S = self.S
        while self.wissued <= min(n, len(self.plan) - 1):
            i = self.wissued
            kind, idx = self.plan[i]
            slot = i % NSLOT
            E = self.unit_E(kind, idx)
            src = self.wd[kind][idx * P:(idx + 1) * P, 0:E]
            dst = self.w_ap(slot, 0, E)
            S.dma("pool", (lambda e, dst=dst, src=src: e.dma_start(out=dst, in_=src)), f"d:w{slot}",
                  writes=[self.w_b[slot]])
            self.wissued += 1

    def w_get(self, kind, idx):
        i = self.wnext
        assert self.plan[i] == (kind, idx), (self.plan[i], kind, idx)
        self.w_issue_upto(i + NSLOT - 1)
        self.wnext += 1
        return i % NSLOT

    def mm(self, out, lhsT, rhs, start, stop, reads, writes, inc, tp=None):
        if tp is None:
            fn = (lambda e, out=out, lhsT=lhsT, rhs=rhs, start=start, stop=stop:
                  e.matmul(out, lhsT=lhsT, rhs=rhs, start=start, stop=stop))
        else:
            fn = (lambda e, out=out, lhsT=lhsT, rhs=rhs, start=start, stop=stop, tp=tp:
                  e.matmul(out, lhsT=lhsT, rhs=rhs, start=start, stop=stop, tile_position=tp))
        self.S.op("pe", fn, reads=reads, writes=writes, inc=inc)

    def act(self, out, in_, func, reads, writes, bias=None, scale=1.0):
        if bias is None:
            bias = self.cs("zero")
        self.S.op("act", (lambda e, out=out, in_=in_, func=func, bias=bias, scale=scale:
                          e.activation(out=out, in_=in_, func=func, bias=bias, scale=scale)),
                  reads=list(reads) + [self.const_b], writes=writes)

    def dve(self, fn, reads, writes):
        self.S.op("dve", fn, reads=reads, writes=writes)

    def gemm(self, slot, e0, nk, rhs_fn, subs, evac):
        banks = [self.next_bank() for _ in subs]
        for k in range(nk):
            for si, (off, w) in enumerate(subs):
                rap, rb = rhs_fn(k, si, off, w)
                lhs = self.w_ap(slot, e0 + k * P, P)
                wr = [self.ps_b[b] for b in banks] if (k == 0 and si == 0) else [self.ps_b[banks[si]]]
                self.mm(self.ps[banks[si]][:, 0:w], lhs, rap, k == 0, k == nk - 1,
                        reads=[self.w_b[slot]] + rb, writes=wr, inc=(k == nk - 1 and si == len(subs) - 1))
        for si, (off, w) in enumerate(subs):
            evac(si, off, w, self.ps[banks[si]][:, 0:w], self.ps_b[banks[si]])

    def stats_mm(self, si, off, w):
        bk = 6 + (si % 2)
        for c in range(NCH):
            self.mm(self.ps[bk][:, 0:w], self.onesm[:, :], self.sq_ap(c, off, w), c == 0, c == NCH - 1,
                    reads=[self.sq_b[c][si], self.const_b], writes=[self.ps_b[bk]], inc=(c == NCH - 1))
        return bk

    def rstd_act(self, src, src_b, si, off, w, add_eps=True):
        r = self.st[:, off:off + w]
        self.act(r, src, AF.Ln, src_b, [self.st_b[si]], bias=self.cs("eps") if add_eps else None)
        self.act(r, r, AF.Exp, [self.st_b[si]], [self.st_b[si]], scale=-0.5)
        return r

    def sq_h(self, tl, si):
        off, w = tl["subs"][si]
        for c in range(NCH):
            self.act(self.sq_ap(c, off, w), self.h_ap(c, off, w), AF.Square, [self.hT_b[c][si]], [self.sq_b[c][si]])

    def pre_stats(self, tl, si):
        off, w = tl["subs"][si]
        bk = self.stats_mm(si, off, w)
        self.rstd_act(self.ps[bk][:, 0:w], [self.ps_b[bk]], si, off, w)

    def pre_apply(self, tl, si, c, gname):
        off, w = tl["subs"][si]
        o, i0, g, r = self.xn_ap(c, off, w), self.h_ap(c, off, w), self.cs(gname, c), self.st[:, off:off + w]
        self.dve((lambda e, o=o, i0=i0, g=g, r=r:
                  e.scalar_tensor_tensor(out=o, in0=i0, scalar=g, in1=r, op0=ALU.mult, op1=ALU.mult)),
                 [self.hT_b[c][si], self.st_b[si], self.const_b], [self.xn_b[c][si]])

    def pre(self, tl, si, gname):
        self.pre_stats(tl, si)
        for c in range(NCH):
            self.pre_apply(tl, si, c, gname)

    def post_apply(self, tl, si, c, gname, square):
        off, w = tl["subs"][si]
        a, g, h, r = self.a_ap(c, off, w), self.cs(gname, c), self.h_ap(c, off, w), self.st[:, off:off + w]
        self.dve((lambda e, a=a, g=g, r=r:
                  e.scalar_tensor_tensor(out=a, in0=a, scalar=g, in1=r, op0=ALU.mult, op1=ALU.mult)),
                 [self.a32_b[c][si], self.st_b[si], self.const_b], [self.a32_b[c][si]])
        self.dve((lambda e, a=a, h=h: e.tensor_tensor(out=h, in0=h, in1=a, op=ALU.add)),
                 [self.a32_b[c][si], self.hT_b[c][si]], [self.hT_b[c][si]])
        if square:
            self.act(self.sq_ap(c, off, w), h, AF.Square, [self.hT_b[c][si]], [self.sq_b[c][si]])

    def post(self, tl, si, gname, square):
        self.pre_stats(tl, si)
        for c in range(NCH):
            self.post_apply(tl, si, c, gname, square)

    def run_first_unit(self, nsub, s1_ops, pre1, blocks, mid=None, late=None):
        if nsub == 1:
            for op in s1_ops:
                op()
            if pre1:
                pre1()
            for b in blocks:
                b(0)
            if mid:
                mid()
            if late:
                late()
            return
        nb0 = min(6, len(blocks))
        for c, op in enumerate(s1_ops):
            op()
            if c < nb0:
                blocks[c](0)
        for c in range(len(s1_ops), nb0):
            blocks[c](0)
        if pre1:
            pre1()
        for b in blocks[nb0:]:
            b(0)
        if mid:
            mid()
        n1 = len(blocks) - 2 if (late and len(blocks) >= 6) else len(blocks)
        for b in blocks[:n1]:
            b(1)
        if late:
            late()
        for b in blocks[n1:]:
            b(1)

    def boundary(self, tl, post_g, pre_g, blocks, post0_done=False, pre0_done=False):
        nsub = len(tl["subs"])
        if not post0_done:
            self.post(tl, 0, post_g, True)
        if not pre0_done:
            self.pre(tl, 0, pre_g)
        if nsub == 1:
            self.run_first_unit(1, [], None, blocks)
            return
        self.pre_stats(tl, 1)
        s1 = [(lambda c=c: self.post_apply(tl, 1, c, post_g, True)) for c in range(NCH)]
        self.run_first_unit(nsub, s1, (lambda: self.pre(tl, 1, pre_g)), blocks)

    def gemm_blk(self, slot, e0, nk, rhs_fn, tl, evac):
        def blk(si):
            off, w = tl["subs"][si]
            bk = self.next_bank()
            for k in range(nk):
                rap, rb = rhs_fn(k, si, off, w)
                self.mm(self.ps[bk][:, 0:w], self.w_ap(slot, e0 + k * P, P), rap, k == 0, k == nk - 1,
                        reads=[self.w_b[slot]] + rb, writes=[self.ps_b[bk]], inc=(k == nk - 1))
            evac(si, off, w, self.ps[bk][:, 0:w], self.ps_b[bk])
        return blk

    def evac_f(self, mc, bias_ap):
        def ev(si, off, w, pap, pb):
            self.act(self.a_ap(mc, off, w), pap, AF.Identity, [pb], [self.a32_b[mc][si]], bias=bias_ap)
            self.act(self.sq_ap(mc, off, w), pap, AF.Square, [pb], [self.sq_b[mc][si]], bias=bias_ap)
        return ev

    def xr(self, k, si, off, w):
        return self.xn_ap(k, off, w), [self.xn_b[k][si]]

    def in_pair_blocks(self, slot, u, tl):
        blocks = []
        for j in range(4):
            mc = 4 * u + j
            tmps = {}

            def ev_gate(si, off, w, pap, pb, mc=mc, tmps=tmps):
                tap, tb = self.tmp_unit()
                tmps[si] = (tap, tb)
                self.act(tap[:, 0:w], pap, AF.Sigmoid, [pb], [tb], bias=self.cs("b_in_g", mc))

            def ev_val(si, off, w, pap, pb, mc=mc, tmps=tmps):
                tap, tb = tmps[si]
                o, bv = self.uT_ap(mc, UH + off, w), self.cs("b_in_v", mc)
                self.dve((lambda e, o=o, pap=pap, bv=bv, t=tap[:, 0:w]:
                          e.scalar_tensor_tensor(out=o, in0=pap, scalar=bv, in1=t, op0=ALU.add, op1=ALU.mult)),
                         [pb, tb, self.const_b], [self.uT_b[mc]])
            blocks.append(((2 * j + 1) * 1024, ev_gate))
            blocks.append(((2 * j) * 1024, ev_val))
        return blocks

    def mixer0_first_blocks(self, tl):
        u3 = self.r44[:, 0:NCH * self.UTW].rearrange("p (c t) -> p c t", c=NCH)
        h3 = self.uhalo[:, :].rearrange("p (c t) -> p c t", c=NCH)
        self.S.op("pool", (lambda e, o=u3[:, :, 0:UH], i0=h3: e.tensor_copy(out=o, in_=i0)),
                  reads=[self.uhalo_b], writes=list(self.uT_b))
        slot = self.w_get("in", 0)
        return [self.gemm_blk(slot, e0, 8, self.xr, tl, ev) for (e0, ev) in self.in_pair_blocks(slot, 0, tl)]

    def mixer0_rest(self, tl):
        subs = tl["subs"]
        W = tl["W"]
        ns = len(subs)
        u3 = self.r44[:, 0:NCH * self.UTW].rearrange("p (c t) -> p c t", c=NCH)
        h3 = self.uhalo[:, :].rearrange("p (c t) -> p c t", c=NCH)
        NTG = (CONVW + DT - 1) // DT

        def build_diag(c, tg):
            taps = list(range(tg * DT, min((tg + 1) * DT, CONVW)))
            nt = len(taps)
            ds = (c * NTG + tg) % 2
            o3 = self.diag[:, ds * DT * P:(ds * DT + nt) * P].rearrange("p (j m) -> p j m", j=nt)
            i0 = self.ident[:, :].unsqueeze(1).to_broadcast([P, nt, P])
            i1 = self.cs("dw_w", c * CONVW + taps[0], nt).unsqueeze(2).to_broadcast([P, nt, P])
            self.S.op("pool", (lambda e, o3=o3, i0=i0, i1=i1: e.tensor_tensor(out=o3, in0=i0, in1=i1, op=ALU.mult)),
                      reads=[self.const_b], writes=[self.diag_b[ds]])
        prebuilt = set()
        for tg in range(min(2, NTG)):
            build_diag(0, tg)
            prebuilt.add((0, tg))
        slot = self.w_get("in", 1)
        for (e0, ev) in self.in_pair_blocks(slot, 1, tl):
            self.gemm(slot, e0, 8, self.xr, subs, ev)
        if tl["halo"]:
            f = self.cs("flag")
            o = u3[:, :, UH:UH + W]
            self.dve((lambda e, o=o, f=f: e.tensor_scalar(out=o, in0=o, scalar1=f, scalar2=None, op0=ALU.mult)),
                     list(self.uT_b) + [self.const_b], list(self.uT_b))
        self.S.op("pool", (lambda e, o=h3, i0=u3[:, :, W:W + UH]: e.tensor_copy(out=o, in_=i0)),
                  reads=list(self.uT_b), writes=[self.uhalo_b])
        for c in range(NCH):
            banks = [self.next_bank() for _ in subs]
            for tg in range(NTG):
                taps = list(range(tg * DT, min((tg + 1) * DT, CONVW)))
                nt = len(taps)
                ds = (c * NTG + tg) % 2
                if (c, tg) not in prebuilt:
                    build_diag(c, tg)
                for jj, tap in enumerate(taps):
                    lhs = self.diag[:, (ds * DT + jj) * P:(ds * DT + jj + 1) * P]
                    for si, (off, w) in enumerate(subs):
                        rhs = self.uT_ap(c, off + 2 + tap, w)
                        wr = [self.ps_b[b] for b in banks] if (tap == 0 and si == 0) else [self.ps_b[banks[si]]]
                        self.mm(self.ps[banks[si]][:, 0:w], lhs, rhs, tap == 0, tap == CONVW - 1,
                                reads=[self.diag_b[ds], self.uT_b[c]], writes=wr,
                                inc=(jj == nt - 1 and si == len(subs) - 1))
            for si, (off, w) in enumerate(subs):
                pap, pb = self.ps[banks[si]][:, 0:w], self.ps_b[banks[si]]
                b = self.cs("dw_b", c)
                self.act(self.a_ap(c, off, w), pap, AF.Identity, [pb], [self.a32_b[c][si]], bias=b)
                self.act(self.sq_ap(c, off, w), pap, AF.Square, [pb], [self.sq_b[c][si]], bias=b)
                self.act(self.xn_ap(c, off, w), pap, AF.Identity, [pb], [self.xn_b[c][si]], bias=b)

        def ln_stats(si):
            off, w = subs[si]
            for c in range(NCH):
                self.mm(self.ps[6][:, 0:w], self.onesm[:, :], self.xn_ap(c, off, w), c == 0, c == NCH - 1,
                        reads=[self.xn_b[c][si], self.const_b], writes=[self.ps_b[6]], inc=(c == NCH - 1))
            for c in range(NCH):
                self.mm(self.ps[7][:, 0:w], self.onesm[:, :], self.sq_ap(c, off, w), c == 0, c == NCH - 1,
                        reads=[self.sq_b[c][si], self.const_b], writes=[self.ps_b[7]], inc=(c == NCH - 1))
            m = self.st[:, 2 * SW + off: 2 * SW + off + w]
            r = self.st[:, off:off + w]
            mb, rb = self.st_b[2 + si], self.st_b[si]
            self.dve((lambda e, o=m, i=self.ps[6][:, 0:w]: e.tensor_copy(out=o, in_=i)), [self.ps_b[6]], [mb])
            self.dve((lambda e, o=r, i=m: e.tensor_tensor(out=o, in0=i, in1=i, op=ALU.mult)), [mb], [rb])
            self.dve((lambda e, o=r, i=self.ps[7][:, 0:w], ep=self.cs("eps"):
                      e.scalar_tensor_tensor(out=o, in0=i, scalar=ep, in1=o, op0=ALU.add, op1=ALU.subtract)),
                     [self.ps_b[7], rb, self.const_b], [rb])
            self.rstd_act(r, [rb], si, off, w, add_eps=False)

        def ln_apply(si, c):
            off, w = subs[si]
            a = self.a_ap(c, off, w)
            M = self.st[:, 2 * SW + off: 2 * SW + off + w]
            R = self.st[:, off:off + w]
            self.dve((lambda e, a=a, M=M: e.tensor_tensor(out=a, in0=a, in1=M, op=ALU.subtract)),
                     [self.a32_b[c][si], self.st_b[2 + si]], [self.a32_b[c][si]])
            self.dve((lambda e, a=a, R=R: e.tensor_tensor(out=a, in0=a, in1=R, op=ALU.mult)),
                     [self.a32_b[c][si], self.st_b[si]], [self.a32_b[c][si]])
            self.act(self.xn_ap(c, off, w), a, AF.Silu, [self.a32_b[c][si]], [self.xn_b[c][si]],
                     bias=self.cs("ln_b", c), scale=self.cs("ln_g", c))
        ln_stats(0)
        for c in range(NCH):
            ln_apply(0, c)
        slot = self.w_get("out", 0)
        blocks = [self.gemm_blk(slot, mc * 1024, 8, self.xr, tl, self.evac_f(mc, self.cs("b_out", mc)))
                  for mc in range(8)]
        mid = (lambda: self.post(tl, 0, "g_mix_post0", True))
        late = (lambda: self.pre(tl, 0, "g_ffn_pre0"))
        if ns == 1:
            self.run_first_unit(1, [], None, blocks, mid=mid, late=late)
        else:
            ln_stats(1)
            self.run_first_unit(ns, [(lambda c=c: ln_apply(1, c)) for c in range(NCH)], None, blocks,
                                mid=mid, late=late)

    def gu_blocks(self, slot, u, tl):
        p0, p1 = GU_UNITS[u]
        blocks = []
        for i in range(p0, p1):
            li = i - p0
            tmps = {}

            def ev_gate(si, off, w, pap, pb, tmps=tmps):
                tap, tb = self.tmp_unit()
                tmps[si] = (tap, tb)
                self.act(tap[:, 0:w], pap, AF.Silu, [pb], [tb])

            def ev_up(si, off, w, pap, pb, i=i, tmps=tmps):
                tap, tb = tmps[si]
                o = self.aT_ap(i, off, w)
                self.dve((lambda e, o=o, pap=pap, t=tap[:, 0:w]:
                          e.tensor_tensor(out=o, in0=pap, in1=t, op=ALU.mult)),
                         [pb, tb], [self.aT_b[i][si]])
            blocks.append(((2 * li) * 1024, ev_gate))
            blocks.append(((2 * li + 1) * 1024, ev_up))
        return blocks

    def ffn_first_blocks(self, tl, l):
        slot = self.w_get(f"gu{l}", 0)
        return [self.gemm_blk(slot, e0, 8, self.xr, tl, ev) for (e0, ev) in self.gu_blocks(slot, 0, tl)]

    def store_out(self, tl, si):
        off, w = tl["subs"][si]
        t0 = tl["t0"] - HALO
        dst = self.outT[:, NCH * (t0 + off): NCH * (t0 + off + w)]
        self.S.dma("act", (lambda e, dst=dst, src=self.h_sub(si, w): e.dma_start(out=dst, in_=src)), f"d:o{si}",
                   reads=[self.hT_b[c][si] for c in range(NCH)])

    def ffn_rest(self, tl, l, final=False):
        subs = tl["subs"]
        for u in range(1, len(GU_UNITS)):
            slot = self.w_get(f"gu{l}", u)
            for (e0, ev) in self.gu_blocks(slot, u, tl):
                self.gemm(slot, e0, 8, self.xr, subs, ev)
        ar = lambda k, si, off, w: (self.aT_ap(k, off, w), [self.aT_b[k][si]])
        for u in range(4):
            slot = self.w_get(f"dn{l}", u)
            if u == 3:
                blocks = [self.gemm_blk(slot, j * NF * P, NF, ar, tl, self.evac_f(2 * u + j, self.cs("zero")))
                          for j in range(2)]
                if final:
                    for si in range(len(subs)):
                        for b in blocks:
                            b(si)
                        self.post(tl, si, "g_ffn_post1", False)
                        self.store_out(tl, si)
                else:
                    for b in blocks:
                        b(0)
                    self.post(tl, 0, f"g_ffn_post{l}", True)
                    if len(subs) > 1:
                        for b in blocks:
                            b(1)
                continue
            for j in range(2):
                mc = 2 * u + j
                self.gemm(slot, j * NF * P, NF, ar, subs, self.evac_f(mc, self.cs("zero")))

    def mixer1_first_blocks(self, tl):
        S = self.S
        for j in range(2):
            o, i0 = self.kT_ap(j, 0, P), self.kprev[:, j * P:(j + 1) * P]
            S.op("pool", (lambda e, o=o, i0=i0: e.tensor_copy(out=o, in_=i0)),
                 reads=[self.kprev_b], writes=[self.kT_b[j][0]])
        o, i0 = self.V_ap(0), self.vprev[:, :]
        S.op("pool", (lambda e, o=o, i0=i0: e.tensor_copy(out=o, in_=i0)),
             reads=[self.vprev_b], writes=[self.V_b[0]])
        if tl["halo"]:
            return []
        slot = self.w_get("q", 0)
        blocks = []
        for cq in range(8):
            def ev_q(si, off, w, pap, pb, cq=cq):
                self.act(self.qT_ap(cq, off, w), pap, AF.Identity, [pb], [self.qT_b[cq][si]],
                         bias=self.cs("b_q", cq))
            blocks.append(self.gemm_blk(slot, cq * 1024, 8, self.xr, tl, ev_q))
        return blocks

    def mixer1_rest(self, tl):
        subs = tl["subs"]
        W = tl["W"]
        nb = W // P
        S = self.S
        slot = self.w_get("kv", 0)
        for j in range(2):
            def ev_k(si, off, w, pap, pb, j=j):
                blks = [self.kT_b[j][1 + (off + x) // P] for x in range(0, w, P)]
                self.act(self.kT_ap(j, P + off, w), pap, AF.Identity, [pb], blks, bias=self.cs("b_k", j))
            self.gemm(slot, j * 1024, 8, self.xr, subs, ev_k)
        for b in range(nb):
            bk = self.next_bank()
            si = (b * P) // SW if not tl["halo"] else 0
            for k in range(NCH):
                self.mm(self.ps[bk][:, 0:256], self.xn_ap(k, b * P, P), self.w_ap(slot, 2048 + k * 256, 256),
                        k == 0, k == NCH - 1, reads=[self.w_b[slot], self.xn_b[k][si]],
                        writes=[self.ps_b[bk]], inc=(k == NCH - 1))
            o, pap, bv = self.V_ap(1 + b), self.ps[bk][:, 0:256], self.cs("b_v", 0, 256)
            self.dve((lambda e, o=o, pap=pap, bv=bv: e.tensor_tensor(out=o, in0=pap, in1=bv, op=ALU.add)),
                     [self.ps_b[bk], self.const_b], [self.V_b[1 + b]])

        if not tl["halo"]:
            iters = [(b, pp) for b in range(nb) for pp in range(2)]
            state = {}

            def stage_a(it):
                b, pp = iters[it]
                si = (b * P) // SW
                pts = []
                for hh in range(2):
                    rows = slice(hh * 64, hh * 64 + 64)
                    for kb in range(2):
                        bank = hh * 2 + kb
                        q3 = self.r44[rows, self.QO + pp * 4 * TT: self.QO + (pp + 1) * 4 * TT] \
                            .rearrange("p (c t) -> p c t", c=4)[:, :, b * P:(b + 1) * P]
                        o3 = self.ps[bank][:, :].rearrange("p (g q) -> p g q", g=4)
                        self.mm(o3, self.kT_ap(pp, (b + kb) * P, P, rows=rows), q3, True, False,
                                reads=[self.kT_b[pp][b + kb]] + [self.qT_b[pp * 4 + g][si] for g in range(4)],
                                writes=[self.ps_b[bank]], inc=False, tp=(hh * 64, 0))
                for hh in range(2):
                    for kb in range(2):
                        bank = hh * 2 + kb
                        h0 = 4 * (2 * pp + hh)
                        o3 = self.ps[bank][:, :].rearrange("p (g q) -> p g q", g=4)
                        b3 = self.bhi[:, :].rearrange("p (h k q) -> p h k q", h=NQH, k=2)[:, h0:h0 + 4, kb, :]
                        self.mm(o3, self.ident[:, :], b3, False, True, reads=[self.const_b],
                                writes=[self.ps_b[bank]], inc=True)
                        pc, psi = (it * 4 + bank) % 8, 1
                        pt = self.a_ap(pc, psi * SW, SW).bitcast(BF16)[:, 0:SW]
                        ptb = self.a32_b[pc][psi]
                        self.act(pt, self.ps[bank][:, :], AF.Exp, [self.ps_b[bank]], [ptb], scale=0.125)
                        if tl["idx"] == 1 and b == 0 and kb == 0:
                            f = self.cs("flag")
                            self.dve((lambda e, pt=pt, f=f:
                                      e.tensor_scalar(out=pt, in0=pt, scalar1=f, scalar2=None, op0=ALU.mult)),
                                     [ptb, self.const_b], [ptb])
                        pts.append((hh, kb, pt, ptb))
                state[it] = pts

            def stage_b(it):
                b, pp = iters[it]
                si = (b * P) // SW
                pts = state.pop(it)
                bo = 4 + 2 * (it % 2)
                bd = bo + 1
                for (hh, kb, pt, ptb) in pts:
                    vcol = (2 * pp + hh) * 64
                    self.mm(self.ps[bo][hh * 64:hh * 64 + 64, :], self.V_ap(b + kb, vcol, 64), pt,
                            kb == 0, kb == 1, reads=[self.V_b[b + kb], ptb], writes=[self.ps_b[bo]],
                            inc=(hh == 1 and kb == 1), tp=(0, hh * 64))
                for (hh, kb, pt, ptb) in pts:
                    self.mm(self.ps[bd][hh * 64:hh * 64 + 64, :], self.ones1[:, 0:64], pt,
                            kb == 0, kb == 1, reads=[ptb, self.const_b], writes=[self.ps_b[bd]],
                            inc=(hh == 1 and kb == 1), tp=(0, hh * 64))
                rc, rcb = self.st[:, (2 + it % 2) * SW:(3 + it % 2) * SW], self.st_b[2 + it % 2]
                for g in range(4):
                    o, i0, sk = rc[:, g * P:(g + 1) * P], self.ps[bd][:, g * P:(g + 1) * P], \
                        self.exps[:, pp * 4 + g: pp * 4 + g + 1]
                    self.dve((lambda e, o=o, i0=i0, sk=sk:
                              e.tensor_scalar(out=o, in0=i0, scalar1=sk, scalar2=None, op0=ALU.add)),
                             [self.ps_b[bd], self.const_b], [rcb])
                self.act(rc, rc, AF.Ln, [rcb], [rcb])
                self.act(rc, rc, AF.Exp, [rcb], [rcb], scale=-1.0)
                o3 = self.r44[:, self.AO + pp * 4 * TT: self.AO + (pp + 1) * 4 * TT] \
                    .rearrange("p (c t) -> p c t", c=4)[:, :, b * P:(b + 1) * P]
                i3 = self.ps[bo][:, :].rearrange("p (g q) -> p g q", g=4)
                r3 = rc.rearrange("p (g q) -> p g q", g=4)
                self.dve((lambda e, o3=o3, i3=i3, r3=r3: e.tensor_tensor(out=o3, in0=i3, in1=r3, op=ALU.mult)),
                         [self.ps_b[bo], rcb], [self.at_b[pp * 4 + g][si] for g in range(4)])

            n = len(iters)
            stage_a(0)
            for it in range(1, n):
                stage_a(it)
                stage_b(it - 1)
            stage_b(n - 1)
        for j in range(2):
            o, i0 = self.kprev[:, j * P:(j + 1) * P], self.kT_ap(j, W, P)
            S.op("pool", (lambda e, o=o, i0=i0: e.tensor_copy(out=o, in_=i0)),
                 reads=[self.kT_b[j][nb]], writes=[self.kprev_b])
        o, i0 = self.vprev[:, :], self.V_ap(nb)
        S.op("pool", (lambda e, o=o, i0=i0: e.tensor_copy(out=o, in_=i0)),
             reads=[self.V_b[nb]], writes=[self.vprev_b])
        if tl["halo"]:
            return
        ar = lambda k, si, off, w: (self.at_ap(k, off, w), [self.at_b[k][si]])
        slot = self.w_get("o", 0)
        blocks = [self.gemm_blk(slot, mc * 1024, 8, ar, tl, self.evac_f(mc, self.cs("b_o", mc))) for mc in range(8)]
        for b in blocks:
            b(0)
        self.post(tl, 0, "g_mix_post1", True)
        for b in blocks[:6]:
            b(1)
        self.pre(tl, 0, "g_ffn_pre1")
        for b in blocks[6:]:
            b(1)

    def h_sub(self, si, w):
        blk = self.hT[:, si * NCH * SW:(si + 1) * NCH * SW]
        if w == SW:
            return blk
        return blk.rearrange("p (c t) -> p c t", c=NCH)[:, :, 0:w]

    def load_x(self, tl):
        subs, t0 = tl["subs"], tl["t0"]
        for si, (off, w) in enumerate(subs):
            src = self.xT[:, NCH * (t0 + off): NCH * (t0 + off + w)]
            if w != SW:
                src = src.rearrange("p (c t) -> p c t", c=NCH)
            self.S.dma("sp", (lambda e, dst=self.h_sub(si, w), src=src: e.dma_start(out=dst, in_=src)), f"d:x{si}",
                       writes=[self.hT_b[c][si] for c in range(NCH)])

    def layout(self, which):
        if which == "A":
            return list(self.uT_b)
        if which == "B":
            return [b for row in self.aT_b for b in row]
        return [b for row in self.qT_b for b in row] + [b for row in self.at_b for b in row] + \
               [b for row in self.kT_b for b in row] + list(self.V_b)

    def handoff(self, old, new):
        bufs = []
        for k in old + new:
            bufs += self.layout(k)
        self.S.op("pool", (lambda e: e.tensor_copy(out=self.dummy[:, 0:8], in_=self.dummy[:, 8:16])),
                  reads=[], writes=bufs + [self.dummy_b])

    def emit_all(self):
        S = self.S
        nc = self.nc
        S.dma("sp", (lambda e: e.dma_start(out=self.consts[:, :], in_=self.consts_d)), "d:c0", writes=[self.const_b])
        bias_b = Buf("biasld")
        self.load_x(self.tiles()[0])
        self.identf = self.a32[:, 0:P]
        self.maskT = self.a32[:, P:3 * P]
        self.biasT = self.a32[:, 4 * P: 4 * P + NQH * 2 * P]
        alla = [b for row in self.a32_b for b in row]
        S.dma("sp", (lambda e: e.dma_start(out=self.biasT, in_=self.biasT_d)), "d:c1", writes=[bias_b] + alla)
        S.dma("sp", (lambda e: e.dma_start(out=self.maskT, in_=self.maskT_d)), "d:c2", writes=[bias_b] + alla)
        S.op("pool", (lambda e: e.memset(self.onesm[:, :], 1.0 / D)), writes=[self.const_b], reads=[])
        S.op("pool", (lambda e: e.memset(self.ones1[:, :], 1.0)), writes=[self.const_b], reads=[])
        S.op("pool", (lambda e: e.memset(self.uhalo[:, :], 0.0)), writes=[self.uhalo_b])
        S.op("pool", (lambda e: e.memset(self.kprev[:, :], 0.0)), writes=[self.kprev_b])
        S.op("pool", (lambda e: e.memset(self.vprev[:, :], 0.0)), writes=[self.vprev_b])
        S.dma("sp", (lambda e: e.dma_start(out=self.identf, in_=self.ident_d)), "d:c3",
              writes=[self.const_b] + alla)
        S.op("pool", (lambda e: e.tensor_copy(out=self.ident[:, :], in_=self.identf)),
             reads=[self.const_b] + alla, writes=[self.const_b])
        self.act(self.exps[:, :], self.cs("sinks", 0, 8), AF.Exp, [], [self.const_b])
        for h in range(NQH):
            o = self.biasT[:, h * 2 * P:(h + 1) * 2 * P]
            self.dve((lambda e, o=o: e.tensor_tensor(out=o, in0=o, in1=self.maskT, op=ALU.add)),
                     [bias_b] + alla, [bias_b] + alla)
        self.dve((lambda e: e.tensor_scalar(out=self.biasT, in0=self.biasT, scalar1=8.0, scalar2=None, op0=ALU.mult)),
                 [bias_b] + alla, [bias_b] + alla)
        self.dve((lambda e: e.tensor_copy(out=self.bhi[:, :], in_=self.biasT)), [bias_b] + alla, [self.const_b])

        S.op("pool", (lambda e: e.memset(self.dummy[:, :], 0.0)), writes=[self.dummy_b])
        for tl in self.tiles():
            subs, W, t0 = tl["subs"], tl["W"], tl["t0"]
            if tl["idx"] > 0:
                self.load_x(tl)
            nsub = len(subs)
            if tl["idx"] > 0:
                self.handoff(["B", "C"], ["A"])
            for si in range(nsub):
                self.sq_h(tl, si)
            self.pre(tl, 0, "g_mix_pre0")
            blocks = self.mixer0_first_blocks(tl)
            self.run_first_unit(nsub, [], (lambda: self.pre(tl, 1, "g_mix_pre0")) if nsub > 1 else None, blocks)
            self.mixer0_rest(tl)
            self.handoff(["A"], ["B"])
            self.boundary(tl, "g_mix_post0", "g_ffn_pre0", self.ffn_first_blocks(tl, 0), post0_done=True, pre0_done=True)
            self.ffn_rest(tl, 0)
            self.handoff(["B"], ["C"])
            self.boundary(tl, "g_ffn_post0", "g_mix_pre1", self.mixer1_first_blocks(tl), post0_done=True)
            self.mixer1_rest(tl)
            if tl["halo"]:
                continue
            self.handoff(["C"], ["B"])
            self.boundary(tl, "g_mix_post1", "g_ffn_pre1", self.ffn_first_blocks(tl, 1), post0_done=True, pre0_done=True)
            self.ffn_rest(tl, 1, final=True)
        assert self.wnext == len(self.plan), (self.wnext, len(self.plan))

    def finalize(self, es):
        nc = self.nc
        S = self.S
        names = list(S.engs.keys()) + sorted(S.dcnt.keys())
        sems = {n: es.enter_context(nc.semaphore(n.replace(":", "_"))) for n in names}
        final_waits = [(n, v) for n, v in S.dcnt.items() if n.startswith("d:o")]

        def replay(e, ops, tail=()):
            for waits, fn, incsem, incval in ops:
                for sname, v in waits[1:]:
                    e.wait_ge(sems[sname], v)
                ins = fn(e)
                if waits:
                    ins._wait_ge(sems[waits[0][0]], waits[0][1])
                if incsem is not None:
                    ins.then_inc(sems[incsem], incval)
            for sname, v in tail:
                e.wait_ge(sems[sname], v)

        with nc.Block() as block:
            @block.tensor
            def _(e):
                replay(e, S.engs["pe"].ops)

            @block.scalar
            def _(e):
                replay(e, S.engs["act"].ops)

            @block.vector
            def _(e):
                replay(e, S.engs["dve"].ops)

            @block.gpsimd
            def _(e):
                replay(e, S.engs["pool"].ops)

            @block.sync
            def _(e):
                replay(e, S.engs["sp"].ops, tail=final_waits)


def _vec8(v):
    return np.ascontiguousarray(np.asarray(v, np.float32).reshape(NCH, P).T)


def _blk(Wm, cols_list):
    K = Wm.shape[0]
    kc = K // P
    out = np.empty((P, len(cols_list), kc, P), np.float32)
    for j, cols in enumerate(cols_list):
        out[:, j] = Wm[:, cols].reshape(kc, P, P).transpose(1, 0, 2)
    return out.reshape(P, -1)


def _t5_bucket(dist):
    dist = np.maximum(dist, 0)
    max_exact = 16
    large = max_exact + (np.log(np.maximum(dist, 1).astype(np.float32) / np.float32(max_exact))
                         / np.float32(np.log(128.0 / max_exact)) * np.float32(32 - max_exact)).astype(np.int32)
    large = np.minimum(large, 31)
    return np.where(dist < max_exact, dist, large)


def _qhead(cq, half):
    pp, g = cq // 4, cq % 4
    return 4 * (2 * pp + half) + g


def _prep_shared(inp):
    f = lambda k: np.asarray(inp[k], np.float32)
    consts = np.zeros((P, NCONST), np.float32)

    def put(name, arr):
        arr = np.asarray(arr, np.float32)
        consts[:, _CL[name]:_CL[name] + arr.shape[1]] = arr
    for l in range(2):
        put(f"g_mix_pre{l}", _vec8(f("mix_pre_g")[l]))
        put(f"g_mix_post{l}", _vec8(f("mix_post_g")[l]))
        put(f"g_ffn_pre{l}", _vec8(f("ffn_pre_g")[l]))
        put(f"g_ffn_post{l}", _vec8(f("ffn_post_g")[l]))
    b_in = f("conv_b_in")[0]
    put("b_in_v", _vec8(b_in[:D]))
    put("b_in_g", _vec8(b_in[D:]))
    put("dw_b", _vec8(f("conv_dw_b")[0]))
    put("ln_g", _vec8(f("conv_ln_g")[0]))
    put("ln_b", _vec8(f("conv_ln_b")[0]))
    put("b_out", _vec8(f("conv_b_out")[0]))
    dww = f("conv_dw_w")[0]
    put("dw_w", dww.T.reshape(NCH, P, CONVW).transpose(1, 0, 2).reshape(P, NCH * CONVW))
    bqkv = f("attn_b_qkv")[0]
    qcols = [np.concatenate([_qhead(cq, 0) * 64 + np.arange(64), _qhead(cq, 1) * 64 + np.arange(64)])
             for cq in range(8)]
    put("b_q", np.stack([bqkv[c] for c in qcols], axis=1))
    put("b_k", np.stack([bqkv[D + j * P: D + (j + 1) * P] for j in range(2)], axis=1))
    put("b_o", _vec8(f("attn_b_o")[0]))
    sinks = f("attn_sinks")[0]
    sk = np.zeros((P, 8), np.float32)
    for cq in range(8):
        sk[:64, cq] = sinks[_qhead(cq, 0)]
        sk[64:, cq] = sinks[_qhead(cq, 1)]
    put("sinks", sk)
    consts[:, _CL["eps"]] = EPS
    put("b_v", np.broadcast_to(bqkv[D + 256: D + 512][None, :], (P, 256)))

    s_i = np.arange(P)[:, None]
    q_i = np.arange(P)[None, :]
    dist = np.stack([q_i + P - s_i, q_i - s_i], axis=0)
    valid = (dist >= 0) & (dist < P)
    bucket = _t5_bucket(dist)
    rel = f("rel_bias")
    biasT = rel[bucket]
    biasT = np.ascontiguousarray(biasT.transpose(1, 3, 0, 2)).reshape(P, NQH * 2 * P)
    maskT = np.where(valid, np.float32(0.0), np.float32(NEG)).astype(np.float32)
    maskT = np.ascontiguousarray(maskT.transpose(1, 0, 2)).reshape(P, 2 * P)

    ar = np.arange(P)
    w_in = f("conv_w_in")[0]
    w_out = f("conv_w_out")[0]
    wqkv = f("attn_w_qkv")[0]
    wo = f("attn_w_o")[0]
    orow = np.concatenate(qcols)
    wo_p = wo[orow, :]
    in_units = []
    for u in range(2):
        cl = []
        for j in range(4):
            mc = 4 * u + j
            cl += [mc * P + ar, D + mc * P + ar]
        in_units.append(_blk(w_in, cl))
    wv_blk = np.ascontiguousarray(wqkv[:, D + 256: D + 512].reshape(NCH, P, 256).transpose(1, 0, 2)).reshape(P, 2048)
    sh = {
        "consts": consts, "biasT": biasT, "maskT": maskT, "ident": np.eye(P, dtype=np.float32),
        "w_in": np.concatenate(in_units, 0),
        "w_out": _blk(w_out, [mc * P + ar for mc in range(8)]),
        "w_q": _blk(wqkv, qcols),
        "w_kv": np.concatenate([_blk(wqkv, [D + ar, D + P + ar]), wv_blk], axis=1),
        "w_o": _blk(wo_p, [mc * P + ar for mc in range(8)]),
    }
    for l in range(2):
        wgu = f("ffn_w_gate_up")[l]
        wdn = f("ffn_w_down")[l]
        gus = []
        for (p0, p1) in GU_UNITS:
            cl = []
            for i in range(p0, p1):
                cl += [i * P + ar, DFF + i * P + ar]
            blk = np.zeros((P, 8192), np.float32)
            blk[:, :len(cl) * 1024] = _blk(wgu, cl)
            gus.append(blk)
        sh[f"w_gu{l}"] = np.concatenate(gus, 0)
        sh[f"w_dn{l}"] = np.concatenate([_blk(wdn, [2 * u * P + ar, (2 * u + 1) * P + ar]) for u in range(4)], 0)
    return sh


_PROG_CACHE = {}


def kernel(**inputs):
    x = np.asarray(inputs["x"], np.float32)
    sh = _prep_shared(inputs)
    in_maps = []
    for core in range(NCORES):
        b, half = core // 2, core % 2
        start = half * TOK
        xl = np.zeros((TLOC, D), np.float32)
        xl[HALO:] = x[b, start:start + TOK]
        if half == 1:
            xl[:HALO] = x[b, start - HALO:start]
        x3 = xl.T.reshape(NCH, P, TLOC).transpose(1, 0, 2)
        xT = np.concatenate([x3[:, :, tl["t0"] + off: tl["t0"] + off + w].reshape(P, -1)
                             for tl in Prog.tiles() for (off, w) in tl["subs"]], axis=1)
        xT = np.ascontiguousarray(xT)
        m = dict(sh)
        c = sh["consts"].copy()
        c[:, _CL["flag"]] = 1.0 if half == 1 else 0.0
        m["consts"] = c
        m["xT"] = xT
        in_maps.append(m)
    if "nc" not in _PROG_CACHE:
        _PROG_CACHE["nc"] = Prog().build()
    res = run_bass_kernel_spmd(_PROG_CACHE["nc"], in_maps, core_ids=list(range(NCORES)))
    out = np.empty((BATCH, SEQ, D), np.float32)
    for core in range(NCORES):
        b, half = core // 2, core % 2
        oT = np.asarray(res.results[core]["outT"]).reshape(P, TOK // SW, NCH, SW)
        out[b, half * TOK:(half + 1) * TOK] = oT.transpose(1, 3, 2, 0).reshape(TOK, D)
    return out
```
